# Optimizing a Trainium2 kernel written in Bass

```python
import jax, jax.numpy as jnp
from jax import lax
import numpy as np

D_MODEL = 1024
BATCH = 8
SEQ = 4096
DEPTH = 4

HEAD_DIM = 64
A_Q_HEADS = 8
A_KV_HEADS = 2
B_GROUPS = ((128, 1), (512, 4), (2048, 16))
B_HEADS = 4
N_B = len(B_GROUPS)
D_FF = 4 * D_MODEL
GRID_W = 64
ROPE_THETA = 10000.0
Q_BLOCK = 128
EPS = 1e-6

A_Q_W = A_Q_HEADS * HEAD_DIM
A_KV_W = A_KV_HEADS * HEAD_DIM
B_W = B_HEADS * HEAD_DIM
IN_SIZES = [A_Q_W, A_KV_W, A_KV_W] + [B_W] * (3 * N_B) + [D_MODEL, D_MODEL]
IN_W = sum(IN_SIZES)
IN_SPLITS = [int(v) for v in np.cumsum(IN_SIZES)[:-1]]

kernel_name = "hybrid_gqa_axial_dilated_gated_encoder"


def rms(x, g):
    x32 = x.astype(jnp.float32)
    y = x32 * lax.rsqrt(jnp.mean(x32 * x32, axis=-1, keepdims=True) + EPS)
    return (y * g.astype(jnp.float32)).astype(x.dtype)


def rope_cos_sin(pos, dim):
    inv = ROPE_THETA ** (-jnp.arange(0, dim, 2, dtype=jnp.float32) / dim)
    ang = pos.astype(jnp.float32)[:, None] * inv[None, :]
    return jnp.cos(ang), jnp.sin(ang)


def apply_rope(x, cos, sin):
    x32 = x.astype(jnp.float32)
    x1, x2 = jnp.split(x32, 2, axis=-1)
    c = cos[None, :, None, :]
    s = sin[None, :, None, :]
    return jnp.concatenate([x1 * c - x2 * s, x2 * c + x1 * s], axis=-1).astype(x.dtype)


def axial_rope(x, row_cs, col_cs):
    half = x.shape[-1] // 2
    return jnp.concatenate([apply_rope(x[..., :half], *row_cs),
                            apply_rope(x[..., half:], *col_cs)], axis=-1)


def gqa_attention(q, k, v):
    B, S, Hq, hd = q.shape
    Hkv = k.shape[2]
    rep = Hq // Hkv
    q = q.reshape(B, S, Hkv, rep, hd)
    scale = hd ** -0.5

    def block(i):
        qi = lax.dynamic_slice_in_dim(q, i * Q_BLOCK, Q_BLOCK, axis=1)
        s = jnp.einsum('bqgrd,bkgd->bgrqk', qi, k, preferred_element_type=jnp.float32) * scale
        p = jax.nn.softmax(s, axis=-1).astype(v.dtype)
        return jnp.einsum('bgrqk,bkgd->bqgrd', p, v)

    o = lax.map(block, jnp.arange(S // Q_BLOCK))
    return jnp.moveaxis(o, 0, 1).reshape(B, S, Hq * hd)


def dilated_attention(qs, ks, vs):
    B, S, H, hd = qs[0].shape
    scale = hd ** -0.5
    offs = []
    for (w, d) in B_GROUPS:
        kk = (w // 2) // d
        offs.append(jnp.asarray(np.arange(-kk, kk + 1) * d, dtype=jnp.int32))

    def block(i):
        t = i * Q_BLOCK + jnp.arange(Q_BLOCK, dtype=jnp.int32)
        outs, lses = [], []
        for q, k, v, off in zip(qs, ks, vs, offs):
            idx = t[:, None] + off[None, :]
            valid = (idx >= 0) & (idx < S)
            idx = jnp.clip(idx, 0, S - 1)
            qi = lax.dynamic_slice_in_dim(q, i * Q_BLOCK, Q_BLOCK, axis=1)
            kg = k[:, idx]
            vg = v[:, idx]
            s = jnp.einsum('bqhd,bqkhd->bhqk', qi, kg, preferred_element_type=jnp.float32) * scale
            s = jnp.where(valid[None, None], s, -jnp.inf)
            lse = jax.nn.logsumexp(s, axis=-1)
            p = jnp.exp(s - lse[..., None]).astype(vg.dtype)
            outs.append(jnp.einsum('bhqk,bqkhd->bqhd', p, vg))
            lses.append(lse)
        wts = jax.nn.softmax(jnp.stack(lses), axis=0)
        wts = jnp.transpose(wts, (0, 1, 3, 2))[..., None]
        return jnp.sum(wts.astype(outs[0].dtype) * jnp.stack(outs), axis=0)

    o = lax.map(block, jnp.arange(S // Q_BLOCK))
    return jnp.moveaxis(o, 0, 1).reshape(B, S, H * hd)


def setup_inputs(seed: int = 0) -> dict:
    key = jax.random.key(seed)
    ks = jax.random.split(key, 16)
    f = jnp.float32
    nrm = lambda k, shape, fan: jax.random.normal(k, shape, f) * (fan ** -0.5)
    return {
        "x": jax.random.normal(ks[0], (BATCH, SEQ, D_MODEL), f),
        "c": jax.random.normal(ks[1], (BATCH, D_MODEL), f),
        "w_ada": nrm(ks[2], (DEPTH, D_MODEL, 6 * D_MODEL), D_MODEL),
        "b_ada": 0.02 * jax.random.normal(ks[3], (DEPTH, 6 * D_MODEL), f),
        "g_mix": 1.0 + 0.05 * jax.random.normal(ks[4], (DEPTH, D_MODEL), f),
        "g_mlp": 1.0 + 0.05 * jax.random.normal(ks[5], (DEPTH, D_MODEL), f),
        "w_in": nrm(ks[6], (DEPTH, D_MODEL, IN_W), D_MODEL),
        "q_norm_a": 1.0 + 0.05 * jax.random.normal(ks[7], (DEPTH, HEAD_DIM), f),
        "k_norm_a": 1.0 + 0.05 * jax.random.normal(ks[8], (DEPTH, HEAD_DIM), f),
        "q_norm_b": 1.0 + 0.05 * jax.random.normal(ks[9], (DEPTH, N_B, HEAD_DIM), f),
        "k_norm_b": 1.0 + 0.05 * jax.random.normal(ks[10], (DEPTH, N_B, HEAD_DIM), f),
        "w_branch_a": nrm(ks[11], (DEPTH, A_Q_W, D_MODEL), A_Q_W),
        "w_branch_b": nrm(ks[12], (DEPTH, B_W, D_MODEL), B_W),
        "w_out": nrm(ks[13], (DEPTH, D_MODEL, D_MODEL), D_MODEL),
        "w_ff1": nrm(ks[14], (DEPTH, D_MODEL, D_FF), D_MODEL),
        "w_ff2": nrm(ks[15], (DEPTH, D_FF, D_MODEL), D_FF),
    }


def reference(x, c, w_ada, b_ada, g_mix, g_mlp, w_in, q_norm_a, k_norm_a, q_norm_b, k_norm_b,
              w_branch_a, w_branch_b, w_out, w_ff1, w_ff2):
    B, S, D = x.shape
    rows = S // GRID_W
    row_idx = jnp.broadcast_to(jnp.arange(rows)[:, None], (rows, GRID_W)).reshape(-1)
    col_idx = jnp.broadcast_to(jnp.arange(GRID_W)[None, :], (rows, GRID_W)).reshape(-1)
    row_cs = rope_cos_sin(row_idx, HEAD_DIM // 2)
    col_cs = rope_cos_sin(col_idx, HEAD_DIM // 2)
    seq_cs = rope_cos_sin(jnp.arange(S), HEAD_DIM)
    c_act = jax.nn.silu(c)

    for l in range(DEPTH):
        ada = c_act @ w_ada[l] + b_ada[l]
        sh1, sc1, gt1, sh2, sc2, gt2 = [a[:, None, :] for a in jnp.split(ada, 6, axis=-1)]

        h = rms(x, g_mix[l]) * (1.0 + sc1) + sh1
        parts = jnp.split(h @ w_in[l], IN_SPLITS, axis=-1)
        qa = parts[0].reshape(B, S, A_Q_HEADS, HEAD_DIM)
        ka = parts[1].reshape(B, S, A_KV_HEADS, HEAD_DIM)
        va = parts[2].reshape(B, S, A_KV_HEADS, HEAD_DIM)
        qa = axial_rope(rms(qa, q_norm_a[l]), row_cs, col_cs)
        ka = axial_rope(rms(ka, k_norm_a[l]), row_cs, col_cs)
        y_a = gqa_attention(qa, ka, va) @ w_branch_a[l]

        qs, kss, vs = [], [], []
        for g in range(N_B):
            qb = parts[3 + 3 * g].reshape(B, S, B_HEADS, HEAD_DIM)
            kb = parts[4 + 3 * g].reshape(B, S, B_HEADS, HEAD_DIM)
            vb = parts[5 + 3 * g].reshape(B, S, B_HEADS, HEAD_DIM)
            qs.append(apply_rope(rms(qb, q_norm_b[l, g]), *seq_cs))
            kss.append(apply_rope(rms(kb, k_norm_b[l, g]), *seq_cs))
            vs.append(vb)
        y_b = dilated_attention(qs, kss, vs) @ w_branch_b[l]

        gate_a = jax.nn.sigmoid(parts[3 + 3 * N_B])
        gate_b = jax.nn.sigmoid(parts[4 + 3 * N_B])
        mixed = gate_a * y_a + gate_b * y_b
        x = x + gt1 * (mixed @ w_out[l])

        h2 = rms(x, g_mlp[l]) * (1.0 + sc2) + sh2
        x = x + gt2 * (jnp.square(jax.nn.relu(h2 @ w_ff1[l])) @ w_ff2[l])

    return x
```

```python
import contextlib
import numpy as np
import concourse.bass as bass
import concourse.mybir as mybir
from concourse.bass_utils import run_bass_kernel_spmd

F32 = mybir.dt.float32
BF16 = mybir.dt.bfloat16
AF = mybir.ActivationFunctionType
ALU = mybir.AluOpType

D = 1024
S_LEN = 4096
DEPTH = 4
NCORES = 8
EPS = 1e-6
NQK = 17
B_DIL = (1, 4, 16)

ENGS = ("pe", "act", "dve", "pool", "sp")
NE = len(ENGS)
EIDX = {e: i for i, e in enumerate(ENGS)}


class Buf:
    __slots__ = ("name", "last_w", "readers")

    def __init__(self, name=""):
        self.name = name
        self.last_w = None
        self.readers = []


class Op:
    __slots__ = ("eng", "idx", "fn", "deps", "dma_sem", "dma_val", "signalled", "count", "waits", "clock")

    def __init__(self, eng, idx, fn):
        self.eng = eng
        self.idx = idx
        self.fn = fn
        self.deps = []
        self.dma_sem = None
        self.dma_val = 0
        self.signalled = False
        self.count = 0
        self.waits = None
        self.clock = None


class Sched:
    def __init__(self):
        self.ops = {e: [] for e in ENGS}
        self.all_ops = []
        self.dma_sem_count = {}
        self.dma_last = {}

    def op(self, eng, fn, reads=(), writes=(), dma=None):
        o = Op(eng, len(self.ops[eng]), fn)
        deps = []
        for b in reads:
            if b.last_w is not None:
                deps.append(b.last_w)
        for b in writes:
            if b.last_w is not None:
                deps.append(b.last_w)
            deps.extend(b.readers)
        o.deps = deps
        for b in reads:
            b.readers.append(o)
        for b in writes:
            b.last_w = o
            b.readers = []
        if dma is not None:
            v = self.dma_sem_count.get(dma, 0) + 16
            self.dma_sem_count[dma] = v
            o.dma_sem = dma
            o.dma_val = v
            self.dma_last[dma] = o
        self.ops[eng].append(o)
        self.all_ops.append(o)
        return o

    def barrier(self, exclude=()):
        last = [self.ops[e][-1] for e in ENGS if self.ops[e]] + \
               [o for k, o in self.dma_last.items() if k not in exclude]
        for e in ENGS:
            o = self.op(e, lambda h: h.nop())
            o.deps = list(last)

    def resolve(self):
        clock = {e: [-1] * NE for e in ENGS}
        dma_seen = {e: {} for e in ENGS}
        for o in self.all_ops:
            e = o.eng
            ck = clock[e]
            ei = EIDX[e]
            if e == "pe":
                ck[ei] = o.idx - 1
            ewait = {}
            dwait = {}
            seen = dma_seen[e]
            for d in o.deps:
                if d.dma_sem is not None:
                    if d.dma_val > seen.get(d.dma_sem, 0) and d.dma_val > dwait.get(d.dma_sem, 0):
                        dwait[d.dma_sem] = d.dma_val
                else:
                    di = EIDX[d.eng]
                    if ck[di] >= d.idx:
                        continue
                    w = ewait.get(di)
                    if w is None or d.idx > w.idx:
                        ewait[di] = d
            wl = []
            if dwait:
                for d in o.deps:
                    if d.dma_sem is not None and dwait.get(d.dma_sem, 0) >= d.dma_val:
                        dc = d.clock
                        qi = EIDX[d.eng]
                        for j in range(NE):
                            if j != qi and dc[j] > ck[j]:
                                ck[j] = dc[j]
                for k, v in dwait.items():
                    wl.append((0, k, v))
                    seen[k] = v
            for di, d in ewait.items():
                if ck[di] >= d.idx:
                    continue
                d.signalled = True
                wl.append((1, d.eng, d))
                dc = d.clock
                for j in range(NE):
                    if dc[j] > ck[j]:
                        ck[j] = dc[j]
            o.waits = wl
            c = list(ck)
            if o.dma_sem is None and o.idx > c[ei]:
                c[ei] = o.idx
            o.clock = c
            o.deps = None
        for e in ENGS:
            n = 0
            for o in self.ops[e]:
                if o.signalled:
                    n += 1
                o.count = n

    def emit_engine(self, e, handle, eng_sems, dma_sems, final_wait_all=False):
        for o in self.ops[e]:
            for w in o.waits:
                if w[0] == 0:
                    handle.wait_ge(dma_sems[w[1]], w[2])
                else:
                    handle.wait_ge(eng_sems[w[1]], w[2].count)
            ins = o.fn(handle)
            if o.dma_sem is not None:
                ins.then_inc(dma_sems[o.dma_sem], 16)
            elif o.signalled:
                ins.then_inc(eng_sems[e], 1)
        if final_wait_all:
            for k, v in self.dma_sem_count.items():
                handle.wait_ge(dma_sems[k], v)


class T:
    __slots__ = ("ap", "buf")

    def __init__(self, ap, buf):
        self.ap = ap
        self.buf = buf

    def v(self, pattern, **kw):
        return self.ap.rearrange(pattern, **kw)


class Arena:
    def __init__(self, base_ap, n_words):
        self.base = base_ap
        self.n = n_words
        self.off = 0
        self.marks = []

    def alloc(self, n_elems, dtype=F32, name=""):
        nbytes = n_elems * (4 if dtype == F32 else 2)
        nw = (nbytes + 3) // 4
        nw = (nw + 15) // 16 * 16
        assert self.off + nw <= self.n, f"arena overflow {name}: {self.off}+{nw}>{self.n}"
        ap = self.base[:, self.off:self.off + nw]
        self.off += nw
        if dtype != F32:
            ap = ap.bitcast(dtype)
        ap = ap[:, 0:n_elems]
        return T(ap, Buf(name))

    def mark(self):
        self.marks.append(self.off)

    def release(self):
        self.off = self.marks.pop()


class Prog:
    def __init__(self, NL, debug=False):
        self.NL = NL
        self.debug = debug
        self.nc = bass.Bass("TRN2", target_bir_lowering=False)
        self.S = Sched()
        self.nkeys = 0
        self.named_keys = {}
        self._auto = 0

    def key(self, name=None):
        if name is None:
            raise ValueError("key needs a name")
        if name not in self.named_keys:
            self.named_keys[name] = self.nkeys
            self.nkeys += 1
        return self.named_keys[name]

    def dma(self, eng, out, in_, reads, writes, key):
        return self.S.op(eng, lambda h: h.dma_start(out=out, in_=in_), reads, writes, dma=key)

    def mm(self, out, lhsT, rhs, start, stop, reads, writes):
        return self.S.op("pe", lambda h: h.matmul(out, lhsT=lhsT, rhs=rhs, start=start, stop=stop), reads, writes)

    def act(self, out, in_, func, reads, writes, scale=1.0, bias=0.0):
        return self.S.op("act", lambda h: h.activation(out=out, in_=in_, func=func, bias=bias, scale=scale),
                         reads, writes)

    def tt(self, eng, out, in0, in1, op, reads, writes):
        return self.S.op(eng, lambda h: h.tensor_tensor(out=out, in0=in0, in1=in1, op=op), reads, writes)

    def stt(self, eng, out, in0, scalar, in1, op0, op1, reads, writes):
        return self.S.op(eng, lambda h: h.scalar_tensor_tensor(out=out, in0=in0, scalar=scalar, in1=in1,
                                                                op0=op0, op1=op1), reads, writes)

    def ts(self, eng, out, in0, s1, s2, op0, op1, reads, writes):
        return self.S.op(eng, lambda h: h.tensor_scalar(out=out, in0=in0, scalar1=s1, scalar2=s2, op0=op0, op1=op1),
                         reads, writes)

    def copy(self, eng, out, in_, reads, writes):
        if eng == "act":
            return self.act(out, in_, AF.Copy, reads, writes)
        return self.S.op(eng, lambda h: h.tensor_copy(out=out, in_=in_), reads, writes)

    def memset(self, eng, ap, val, writes):
        return self.S.op(eng, lambda h: h.memset(ap, val), (), writes)

    def recip(self, out, in_, reads, writes):
        return self.S.op("dve", lambda h: h.reciprocal(out=out, in_=in_), reads, writes)

    def declare(self):
        nc, NL = self.nc, self.NL

        def inp(name, shape, dt=F32):
            return nc.dram_tensor(name, list(shape), dt, kind="ExternalInput").ap()

        def scr(name, shape, dt):
            kind = "ExternalOutput" if (self.debug and name in ("qkT", "vtm", "xT", "dbg")) else "Internal"
            return nc.dram_tensor(name, list(shape), dt, kind=kind).ap()

        self.x_in = inp("x", [S_LEN, D])
        self.out = nc.dram_tensor("out", [S_LEN, D], F32, kind="ExternalOutput").ap()
        self.c_in = inp("c_fm", [128, 8])
        self.ident_in = inp("ident", [128, 128])
        self.cst_bf_in = inp("cst_bf", [128, 4, 128])
        self.tabs_in = inp("tabs", [4, 128, S_LEN])
        self.b_ada = inp("b_ada", [NL, 128, 48])
        self.g_mix = inp("g_mix", [NL, 128, 8])
        self.g_mlp = inp("g_mlp", [NL, 128, 8])
        self.g_qk = inp("g_qk", [NL, 128, 2 * NQK])
        W = {}
        W["ada"] = (inp("w_ada", [NL, 48, 128, 8 * 128]), [48, 128, 1024])
        W["qk"] = (inp("w_qk", [NL, NQK, 128, 1024]), [NQK, 128, 1024])
        W["qks"] = (inp("w_qks", [NL, NQK, 128, 1024]), [NQK, 128, 1024])
        W["v"] = (inp("w_v", [NL, 128, 8 * 896]), [128, 8 * 896])
        W["g"] = (inp("w_g", [NL, 16, 128, 1024]), [16, 128, 1024])
        W["ba"] = (inp("w_ba", [NL, 128, 4 * 1024]), [128, 4096])
        W["bb"] = (inp("w_bb", [NL, 128, 2 * 1024]), [128, 2048])
        W["o"] = (inp("w_o", [NL, 128, 8 * 1024]), [128, 8192])
        W["f1"] = (inp("w_f1", [NL, 32, 128, 1024]), [32, 128, 1024])
        W["f2"] = (inp("w_f2", [NL, 8, 128, 4096]), [8, 128, 4096])
        self.W = W
        self.Wb = {}
        self.Wb_buf = {}
        self.Wb_key = {}
        for k, (ap, shp) in W.items():
            self.Wb[k] = [scr("wb%d_%s" % (i, k), shp, BF16) for i in range(2)]
            self.Wb_buf[k] = [Buf("wb%d_%s" % (i, k)) for i in range(2)]
            self.Wb_key[k] = [self.key("wb%d_%s" % (i, k)) for i in range(2)]
        self.cast_keys = set(k for ks in self.Wb_key.values() for k in ks)
        self.xT = scr("xT", [D, S_LEN], F32)
        self.xT_buf = [Buf("xT%d" % j) for j in range(8)]
        self.qkT = scr("qkT", [NQK * 128, S_LEN], BF16)
        self.qkT_buf = [Buf("qkT%d" % i) for i in range(NQK)]
        self.vtm = scr("vtm", [S_LEN, 896], BF16)
        self.vtm_buf = Buf("vtm")

    def cast_plan(self, l, names, step=2048):
        plan = []
        for k in names:
            src, shp = self.W[k]
            n = int(np.prod(shp))
            sap = src[l]
            dap = self.Wb[k][l % 2]
            if len(shp) == 3:
                sap = sap.rearrange("a p (r b) -> (a p r) b", b=1024)
                dap = dap.rearrange("a p (r b) -> (a p r) b", b=1024)
            else:
                sap = sap.rearrange("p (r b) -> (p r) b", b=1024)
                dap = dap.rearrange("p (r b) -> (p r) b", b=1024)
            rows = n // 1024
            for r0 in range(0, rows, step):
                r1 = min(rows, r0 + step)
                plan.append((dap[r0:r1, :], sap[r0:r1, :], self.Wb_buf[k][l % 2], self.Wb_key[k][l % 2]))
        return plan

    def cast_emit(self, item, pace=()):
        dap, sap, buf, key = item
        self.dma("pool", dap, sap, pace, [buf], key)

    def cast_weights(self, l, names):
        for item in self.cast_plan(l, names, step=8192):
            self.cast_emit(item)

    def build(self):
        nc = self.nc
        self.declare()
        NW = 52992
        with (nc.sbuf_tensor("arena", [128, NW], F32) as arena_t,
              nc.psum_tensor("ps", [128, 4096], F32) as ps_t):
            self.A = Arena(arena_t, NW)
            self.ps = ps_t
            self.bank = [T(ps_t[:, b * 512:(b + 1) * 512], Buf("bank%d" % b)) for b in range(8)]
            self.body()
            self.S.resolve()
            with contextlib.ExitStack() as es:
                eng_sems = {e: es.enter_context(nc.semaphore("s_" + e)) for e in ENGS}
                dma_sems = {k: es.enter_context(nc.semaphore("d%d" % k)) for k in self.S.dma_sem_count}
                block = es.enter_context(nc.Block())
                S = self.S

                @block.tensor
                def _(h):
                    S.emit_engine("pe", h, eng_sems, dma_sems)

                @block.scalar
                def _(h):
                    S.emit_engine("act", h, eng_sems, dma_sems)

                @block.vector
                def _(h):
                    S.emit_engine("dve", h, eng_sems, dma_sems)

                @block.gpsimd
                def _(h):
                    S.emit_engine("pool", h, eng_sems, dma_sems)

                @block.sync
                def _(h):
                    S.emit_engine("sp", h, eng_sems, dma_sems, final_wait_all=True)
        return nc

    def load_consts(self):
        A = self.A
        self.ident = A.alloc(128, F32, "ident")
        self.dma("sp", self.ident.ap, self.ident_in, (), [self.ident.buf], self.key("k17"))
        cst = A.alloc(4 * 128, BF16, "cst")
        self.dma("pool", cst.ap.rearrange("p (a b) -> p a b", a=4), self.cst_bf_in, (), [cst.buf], self.key("k18"))
        self.cst = cst
        self.ones = T(cst.ap[:, 0:128], cst.buf)
        self.e64 = T(cst.ap[:, 128:256], cst.buf)
        self.masks = T(cst.ap[:, 256:512], cst.buf)
        self.cfm = A.alloc(8, F32, "cfm")
        self.dma("sp", self.cfm.ap, self.c_in, (), [self.cfm.buf], self.key("k19"))
        self.cact = A.alloc(8, BF16, "cact")
        self.act(self.cact.ap, self.cfm.ap, AF.Silu, [self.cfm.buf], [self.cact.buf])
        self.mod = A.alloc(48, F32, "mod")
        self.A1 = A.alloc(8, F32, "A1")
        self.A2 = A.alloc(8, F32, "A2")
        self.gq = A.alloc(2 * NQK, F32, "gq")
        self.small_key = self.key("k20")

    def body(self):
        A = self.A
        self.load_consts()
        self.phase_in()
        WN = ["ada", "qk", "qks", "v", "g", "ba", "bb", "o", "f1", "f2"]
        self.cast_weights(0, WN)
        for l in range(self.NL):
            self.phase_ada(l)
            self.phase_proj(l)
            self.pending_cast = self.cast_plan(l + 1, WN) if l + 1 < self.NL else []
            self.phase_gqa(l)
            while self.pending_cast:
                self.cast_emit(self.pending_cast.pop(0))
            self.phase_dil(l)
            self.phase_mix(l)
            self.phase_ffn(l)
        self.phase_out()

    def phase_in(self):
        A, S = self.A, self.S
        A.mark()
        xin = [A.alloc(1024, F32, "xin%d" % i) for i in range(2)]
        xo = [A.alloc(8 * 512, F32, "xo%d" % i) for i in range(2)]
        kin = [self.key("r1_%d" % _i) for _i in range(2)]
        ko = [self.key("r2_%d" % _i) for _i in range(2)]
        for j in range(8):
            o = xo[j % 2]
            for tb in range(4):
                blk = j * 4 + tb
                xi = xin[blk % 2]
                self.dma("sp", xi.ap, self.x_in[blk * 128:(blk + 1) * 128, :], (), [xi.buf], kin[blk % 2])
                for half in range(2):
                    bk = self.bank[(blk * 2 + half) % 8]
                    for cc in range(4):
                        c = half * 4 + cc
                        self.mm(bk.ap[:, cc * 128:(cc + 1) * 128], xi.ap[:, c * 128:(c + 1) * 128], self.ident.ap,
                                True, True, [xi.buf, self.ident.buf], [bk.buf])
                    dst = o.ap.rearrange("p (c t) -> p c t", c=8)[:, half * 4:(half + 1) * 4, tb * 128:(tb + 1) * 128]
                    src = bk.ap.rearrange("p (c t) -> p c t", c=4)
                    self.copy("dve" if half == 0 else "act", dst, src, [bk.buf], [o.buf])
            self.dma("sp", self.xT.rearrange("(c p) t -> p c t", p=128)[:, :, j * 512:(j + 1) * 512],
                     o.ap.rearrange("p (c t) -> p c t", c=8), [o.buf], [self.xT_buf[j]], ko[j % 2])
        S.barrier(self.cast_keys)
        A.release()

    def phase_out(self):
        A, S = self.A, self.S
        A.mark()
        xi2 = [A.alloc(8 * 512, F32, "xo_in%d" % i) for i in range(2)]
        xo2 = [A.alloc(1024, F32, "xo_out%d" % i) for i in range(2)]
        kin = [self.key("r3_%d" % _i) for _i in range(2)]
        ko = [self.key("r4_%d" % _i) for _i in range(2)]
        for j in range(8):
            xi = xi2[j % 2]
            self.dma("sp", xi.ap.rearrange("p (c t) -> p c t", c=8),
                     self.xT.rearrange("(c p) t -> p c t", p=128)[:, :, j * 512:(j + 1) * 512],
                     [self.xT_buf[j]], [xi.buf], kin[j % 2])
            xv = xi.ap.rearrange("p (c t) -> p c t", c=8)
            for tb in range(4):
                blk = j * 4 + tb
                o = xo2[blk % 2]
                for half in range(2):
                    bk = self.bank[(blk * 2 + half) % 8]
                    for cc in range(4):
                        c = half * 4 + cc
                        self.mm(bk.ap[:, cc * 128:(cc + 1) * 128], xv[:, c, tb * 128:(tb + 1) * 128], self.ident.ap,
                                True, True, [xi.buf, self.ident.buf], [bk.buf])
                    self.copy("dve" if half == 0 else "act", o.ap[:, half * 512:(half + 1) * 512], bk.ap,
                              [bk.buf], [o.buf])
                self.dma("sp", self.out[blk * 128:(blk + 1) * 128, :], o.ap, [o.buf], (), ko[blk % 2])
        A.release()

    def phase_ada(self, l):
        A, S = self.A, self.S
        A.mark()
        wt = [A.alloc(8 * 1024, BF16, "wada%d" % i) for i in range(2)]
        kw = [self.key("r5_%d" % _i) for _i in range(2)]
        bfm = A.alloc(48, F32, "bfm")
        gm = A.alloc(16, F32, "gm")
        bfm.buf = gm.buf = self.gq.buf
        self.dma("sp", bfm.ap, self.b_ada[l], (), [bfm.buf], self.small_key)
        self.dma("sp", gm.ap[:, 0:8], self.g_mix[l], (), [gm.buf], self.small_key)
        self.dma("sp", gm.ap[:, 8:16], self.g_mlp[l], (), [gm.buf], self.small_key)
        self.dma("sp", self.gq.ap, self.g_qk[l], (), [self.gq.buf], self.small_key)
        bk = self.bank[0]
        for piece in range(6):
            w = wt[piece % 2]
            self.dma("sp", w.ap.rearrange("p (f k) -> p f k", f=8),
                     self.Wb["ada"][l % 2][piece * 8:(piece + 1) * 8].rearrange("f p k -> p f k"),
                     [self.Wb_buf["ada"][l % 2]], [w.buf], kw[piece % 2])
            wv = w.ap.rearrange("p (f c k) -> p f c k", f=8, c=8)
            for f in range(8):
                col = piece * 8 + f
                for c in range(8):
                    self.mm(bk.ap[:, col:col + 1], wv[:, f, c, :], self.cact.ap[:, c:c + 1], c == 0, c == 7,
                            [w.buf, self.cact.buf], [bk.buf])
        self.tt("dve", self.mod.ap, bk.ap[:, 0:48], bfm.ap, ALU.add, [bk.buf, bfm.buf], [self.mod.buf])
        m = self.mod.ap
        self.stt("dve", self.A1.ap, m[:, 8:16], 1.0, gm.ap[:, 0:8], ALU.add, ALU.mult, [self.mod.buf, gm.buf],
                 [self.A1.buf])
        self.ts("dve", self.A1.ap, self.A1.ap, 32.0, 0.0, ALU.mult, ALU.add, [self.A1.buf], [self.A1.buf])
        self.stt("dve", self.A2.ap, m[:, 32:40], 1.0, gm.ap[:, 8:16], ALU.add, ALU.mult, [self.mod.buf, gm.buf],
                 [self.A2.buf])
        self.ts("dve", self.A2.ap, self.A2.ap, 32.0, 0.0, ALU.mult, ALU.add, [self.A2.buf], [self.A2.buf])
        S.barrier(self.cast_keys)
        A.release()

    def make_h(self, xt, hdst_view, hbuf, Avec, Bcol0, sq, rstd, tmp, bk):
        xv = xt.ap.rearrange("p (c t) -> p c t", c=8)
        self.act(sq.ap, xt.ap, AF.Square, [xt.buf], [sq.buf])
        sv = sq.ap.rearrange("p (c t) -> p c t", c=8)
        for c in range(8):
            self.mm(bk.ap, self.ones.ap, sv[:, c, :], c == 0, c == 7, [sq.buf, self.cst.buf], [bk.buf])
        self.act(rstd.ap, bk.ap, AF.Ln, [bk.buf], [rstd.buf], bias=float(D * EPS))
        self.act(rstd.ap, rstd.ap, AF.Exp, [rstd.buf], [rstd.buf], scale=-0.5)
        for c in range(8):
            tc_ = tmp[c % 2]
            self.stt("dve", tc_.ap, xv[:, c, :],
                     Avec.ap[:, c:c + 1], rstd.ap, ALU.mult, ALU.mult, [xt.buf, Avec.buf, rstd.buf], [tc_.buf])
            self.act(hdst_view[:, c, :], tc_.ap, AF.Identity, [tc_.buf, self.mod.buf], [hbuf],
                     bias=self.mod.ap[:, Bcol0 + c:Bcol0 + c + 1])

    def phase_proj(self, l):
        A, S = self.A, self.S
        A.mark()
        hT = A.alloc(8 * S_LEN, BF16, "hT_all")
        hv = hT.ap.rearrange("p (c t) -> p c t", c=8)
        tabs = [A.alloc(S_LEN, F32, "tab%d" % i) for i in range(4)]
        for i in range(4):
            self.dma("sp", tabs[i].ap, self.tabs_in[i], (), [tabs[i].buf], self.key("tab%d" % i))
        A.mark()
        xts = [A.alloc(8 * 512, F32, "xt%d" % i) for i in range(2)]
        kx = [self.key("r6_%d" % _i) for _i in range(2)]
        sq = A.alloc(8 * 512, BF16, "sq")
        rstd = A.alloc(512, F32, "rstd")
        tmp = [A.alloc(512, F32, "tmp%d" % i) for i in range(2)]
        for j in range(8):
            xt = xts[j % 2]
            self.dma("sp", xt.ap.rearrange("p (c t) -> p c t", c=8),
                     self.xT.rearrange("(c p) t -> p c t", p=128)[:, :, j * 512:(j + 1) * 512],
                     [self.xT_buf[j]], [xt.buf], kx[j % 2])
            self.make_h(xt, hv[:, :, j * 512:(j + 1) * 512], hT.buf, self.A1, 0, sq, rstd, tmp, self.bank[j % 2])
        S.barrier(self.cast_keys)
        A.release()
        wr = [A.alloc(1024, BF16, "wr%d" % i) for i in range(2)]
        ws = [A.alloc(1024, BF16, "ws%d" % i) for i in range(2)]
        kwr = [self.key("r7_%d" % _i) for _i in range(2)]
        kws = [self.key("r8_%d" % _i) for _i in range(2)]
        stg = [A.alloc(S_LEN, BF16, "stg%d" % i) for i in range(2)]
        kst = [self.key("r9_%d" % _i) for _i in range(2)]
        sqs = [A.alloc(512, BF16, "sqk%d" % i) for i in range(3)]
        rs = [A.alloc(512, F32, "rsk%d" % i) for i in range(3)]
        u1 = [A.alloc(512, F32, "u1%d" % i) for i in range(3)]
        u2 = [A.alloc(512, F32, "u2%d" % i) for i in range(3)]
        it = 0
        pend = []
        for ci in range(NQK):
            w_r, w_s = wr[ci % 2], ws[ci % 2]
            self.dma("sp", w_r.ap, self.Wb["qk"][l % 2][ci], [self.Wb_buf["qk"][l % 2]], [w_r.buf], kwr[ci % 2])
            self.dma("sp", w_s.ap, self.Wb["qks"][l % 2][ci], [self.Wb_buf["qks"][l % 2]], [w_s.buf], kws[ci % 2])
            wrv = w_r.ap.rearrange("p (c f) -> p c f", c=8)
            wsv = w_s.ap.rearrange("p (c f) -> p c f", c=8)
            st = stg[ci % 2]
            axial = ci < 5
            Ct, St = (tabs[2], tabs[3]) if axial else (tabs[0], tabs[1])
            dil = 1 if ci < 5 else B_DIL[(ci - 5) // 4]
            for j in range(8):
                k3 = it % 3
                kE = it % 2
                it += 1
                bR, bS, bE = self.bank[k3 * 2], self.bank[k3 * 2 + 1], self.bank[6 + kE]
                for c in range(8):
                    self.mm(bR.ap, wrv[:, c, :], hv[:, c, j * 512:(j + 1) * 512], c == 0, c == 7,
                            [w_r.buf, hT.buf], [bR.buf])
                for c in range(8):
                    self.mm(bS.ap, wsv[:, c, :], hv[:, c, j * 512:(j + 1) * 512], c == 0, c == 7,
                            [w_s.buf, hT.buf], [bS.buf])
                if pend:
                    pend.pop(0)()
                sqk, rk, a1, a2 = sqs[k3], rs[k3], u1[k3], u2[k3]
                tsl = slice(j * 512, (j + 1) * 512)
                self.act(sqk.ap, bR.ap, AF.Square, [bR.buf], [sqk.buf])
                self.act(a1.ap, bR.ap, AF.Identity, [bR.buf, self.gq.buf], [a1.buf], scale=self.gq.ap[:, ci:ci + 1])
                self.act(a2.ap, bS.ap, AF.Identity, [bS.buf, self.gq.buf], [a2.buf],
                         scale=self.gq.ap[:, NQK + ci:NQK + ci + 1])
                self.tt("dve", a1.ap, a1.ap, Ct.ap[:, tsl], ALU.mult, [a1.buf, Ct.buf], [a1.buf])
                self.tt("dve", a2.ap, a2.ap, St.ap[:, tsl], ALU.mult, [a2.buf, St.buf], [a2.buf])
                self.tt("pool", a1.ap, a1.ap, a2.ap, ALU.add, [a1.buf, a2.buf], [a1.buf])

                def tail(bE=bE, sqk=sqk, rk=rk, a1=a1, st=st, j=j, tsl=tsl, dil=dil, ci=ci):
                    self.mm(bE.ap, self.e64.ap, sqk.ap, True, True, [sqk.buf, self.cst.buf], [bE.buf])
                    self.act(rk.ap, bE.ap, AF.Ln, [bE.buf], [rk.buf], bias=float(64 * EPS))
                    self.act(rk.ap, rk.ap, AF.Exp, [rk.buf], [rk.buf], scale=-0.5)
                    if dil == 1:
                        dst = st.ap[:, tsl]
                        src1, src2 = a1.ap, rk.ap
                    else:
                        n = 512 // dil
                        dst = st.ap.rearrange("p (r m) -> p r m", r=dil)[:, :, j * n:(j + 1) * n]
                        src1 = a1.ap.rearrange("p (m r) -> p r m", r=dil)
                        src2 = rk.ap.rearrange("p (m r) -> p r m", r=dil)
                    self.tt("pool", dst, src1, src2, ALU.mult, [a1.buf, rk.buf], [st.buf])
                    if j == 7:
                        self.dma("sp", self.qkT[ci * 128:(ci + 1) * 128, :], st.ap, [st.buf], [self.qkT_buf[ci]],
                                 kst[ci % 2])

                pend.append(tail)
        while pend:
            pend.pop(0)()
        wv = A.alloc(8 * 896, BF16, "wv")
        self.dma("sp", wv.ap, self.Wb["v"][l % 2], [self.Wb_buf["v"][l % 2]], [wv.buf], self.key("k22"))
        wvv = wv.ap.rearrange("p (c n) -> p c n", c=8)
        vst = [A.alloc(896, BF16, "vst%d" % i) for i in range(2)]
        kv = [self.key("r10_%d" % _i) for _i in range(2)]
        for tb in range(32):
            vs = vst[tb % 2]
            for half in range(2):
                bk = self.bank[6 + half]
                for c in range(8):
                    self.mm(bk.ap[:, 0:448], hv[:, c, tb * 128:(tb + 1) * 128], wvv[:, c, half * 448:(half + 1) * 448],
                            c == 0, c == 7, [hT.buf, wv.buf], [bk.buf])
                self.copy("act" if half == 0 else "dve", vs.ap[:, half * 448:(half + 1) * 448], bk.ap[:, 0:448],
                          [bk.buf], [vs.buf])
            self.dma("sp", self.vtm[tb * 128:(tb + 1) * 128, :], vs.ap, [vs.buf], [self.vtm_buf], kv[tb % 2])
        S.barrier(self.cast_keys)
        A.release()

    def phase_gqa(self, l):
        A, S = self.A, self.S
        self.attn_mark = True
        A.mark()
        self.aT = A.alloc(4 * S_LEN, BF16, "attn_aT")
        self.bT = A.alloc(2 * S_LEN, BF16, "attn_bT")
        aTv = self.aT.ap.rearrange("p (c t) -> p c t", c=4)
        A.mark()
        kT = A.alloc(S_LEN, BF16, "kTa")
        self.dma("sp", kT.ap, self.qkT[4 * 128:5 * 128, :], [self.qkT_buf[4]], [kT.buf], self.key("k23"))
        vstage = A.alloc(32 * 128, BF16, "vstage")
        self.dma("sp", vstage.ap.rearrange("p (b n) -> p b n", b=32),
                 self.vtm.rearrange("(b p) n -> p b n", p=128)[:, :, 0:128], [self.vtm_buf], [vstage.buf], self.key("k24"))
        vaug = A.alloc(32 * 2 * 192, BF16, "vaug")
        vav = vaug.ap.rearrange("p (b g n) -> p b g n", b=32, g=2)
        self.memset("pool", vaug.ap, 1.0, [vaug.buf])
        vsv = vstage.ap.rearrange("p (b g n) -> p b g n", b=32, g=2)
        self.copy("dve", vav[:, :, :, 0:64], vsv, [vstage.buf], [vaug.buf])
        self.copy("pool", vav[:, :, :, 128:192], vsv, [vstage.buf], [vaug.buf])
        NQB = 6
        qp = [A.alloc(512, BF16, "qpad%d" % i) for i in range(NQB)]
        kq = [self.key("gq%d" % _i) for _i in range(NQB)]
        for q in qp:
            self.memset("pool", q.ap, 0.0, [q.buf])
        iters = [(j, h) for j in range(8) for h in range(8)]
        qsel = []
        cntg = [0, 0]
        for (j, h) in iters:
            g = h // 4
            qsel.append(g * 3 + cntg[g] % 3)
            cntg[g] += 1

        def qload(i):
            j, h = iters[i]
            g = h // 4
            q = qp[qsel[i]]
            r0 = (h // 2) * 128 + (h % 2) * 64
            self.dma("sp", q.ap[g * 64:(g + 1) * 64, :], self.qkT[r0:r0 + 64, j * 512:(j + 1) * 512],
                     [self.qkT_buf[h // 2]], [q.buf], kq[qsel[i]])

        NPT = 3
        pt = [A.alloc(1024, BF16, "pt%d" % i) for i in range(NPT)]
        rec = [A.alloc(512, F32, "rec%d" % i) for i in range(2)]
        stb = [T(self.ps[:, i * 1024:(i + 1) * 1024], Buf("st%d" % i)) for i in range(3)]
        otb = [self.bank[6], self.bank[7]]
        it = 0
        step = 0
        LOOK = 2
        for i0 in range(LOOK):
            qload(i0)
        for j in range(8):
            for h in range(8):
                g = h // 4
                ii = j * 8 + h
                if ii + LOOK < len(iters):
                    qload(ii + LOOK)
                q = qp[qsel[ii]]
                ot = otb[it % 2]
                it += 1
                odd = h % 2
                pend = []

                def do_mm2(s2, ptb):
                    for half in range(2):
                        kb = 2 * s2 + half
                        self.mm(ot.ap, vav[:, kb, g, odd * 64:odd * 64 + 128], ptb.ap[:, half * 512:(half + 1) * 512],
                                kb == 0, kb == 31, [vaug.buf, ptb.buf], [ot.buf])

                for s2 in range(16):
                    sb = stb[step % 3]
                    ptb = pt[step % NPT]
                    step += 1
                    for half in range(2):
                        kb = 2 * s2 + half
                        self.mm(sb.ap[:, half * 512:(half + 1) * 512], kT.ap[:, kb * 128:(kb + 1) * 128], q.ap,
                                True, True, [kT.buf, q.buf], [sb.buf])
                    self.act(ptb.ap, sb.ap, AF.Exp, [sb.buf], [ptb.buf], scale=8.0)
                    pend.append((s2, ptb))
                    if len(pend) > 1:
                        do_mm2(*pend.pop(0))
                while pend:
                    do_mm2(*pend.pop(0))
                r = rec[it % 2]
                if odd == 0:
                    self.recip(r.ap[0:64, :], ot.ap[64:128, :], [ot.buf], [r.buf])
                    self.tt("dve", aTv[0:64, h // 2, j * 512:(j + 1) * 512], ot.ap[0:64, :], r.ap[0:64, :], ALU.mult,
                            [ot.buf, r.buf], [self.aT.buf])
                else:
                    self.recip(r.ap[64:128, :], ot.ap[0:64, :], [ot.buf], [r.buf])
                    self.tt("dve", aTv[64:128, h // 2, j * 512:(j + 1) * 512], ot.ap[64:128, :], r.ap[64:128, :], ALU.mult,
                            [ot.buf, r.buf], [self.aT.buf])
                if ii % 3 == 1 and self.pending_cast:
                    self.cast_emit(self.pending_cast.pop(0), pace=[r.buf])
        S.barrier(self.cast_keys)
        A.release()

    def phase_dil(self, l):
        A, S = self.A, self.S
        A.mark()
        bTv = self.bT.ap.rearrange("p (c t) -> p c t", c=2)
        HS = S_LEN // 2
        U = A.alloc(4 * HS, F32, "U")
        Uv = U.ap.rearrange("p (h t) -> p h t", h=4)
        LM = HS
        NBM = LM // 128 + 1
        NSET = 2
        sets = []
        for i in range(NSET):
            st = dict(
                qe=A.alloc(2 * LM, BF16, "qe%d" % i), qo=A.alloc(2 * LM, BF16, "qo%d" % i),
                kr=A.alloc(2 * (LM + 128), BF16, "kr%d" % i),
                va=A.alloc(NBM * 512, BF16, "va%d" % i),
                vstg=A.alloc(NBM * 256, BF16, "vstg%d" % i),
                kqe=self.key("dqe%d" % i), kqo=self.key("dqo%d" % i), kkr=self.key("dkr%d" % i),
                kva=self.key("dva%d" % i))
            self.memset("pool", st["qe"].ap, 0.0, [st["qe"].buf])
            self.memset("pool", st["qo"].ap, 0.0, [st["qo"].buf])
            self.memset("pool", st["vstg"].ap, 0.0, [st["vstg"].buf])
            sets.append(st)
        NPT = 3
        pt = [A.alloc(1024, BF16, "ptd%d" % i) for i in range(NPT)]
        recs = [A.alloc(HS, F32, "recd%d" % i) for i in range(2)]
        stb = [T(self.ps[:, i * 1024:(i + 1) * 1024], Buf("std%d" % i)) for i in range(3)]
        otb = [self.bank[6], self.bank[7]]
        NOT = 2
        maskv = self.masks.ap.rearrange("p (k q) -> p k q", k=2)
        mask4 = maskv.unsqueeze(1).broadcast_to([128, 4, 2, 128])

        jobs = [(s, g, r) for s in range(2) for g in range(3) for r in range(B_DIL[g])]

        first_s1 = min(i for i, jb in enumerate(jobs) if jb[0] == 1)

        def init_ones(st_):
            v5 = st_["va"].ap.rearrange("p (b h two n) -> p b h two n", h=4, two=2, n=64)
            self.memset("pool", v5[:, :, :, 1, :], 1.0, [st_["va"].buf])

        def geom(s, g):
            d = B_DIL[g]
            L = S_LEN // d
            Lh = L // 2
            nb = Lh // 128
            m0 = s * Lh
            lo = max(0, m0 - 64)
            hi = min(L, m0 + Lh + 64)
            return d, L, Lh, nb, m0, lo, hi, lo - (m0 - 64), hi - (m0 - 64)

        def loads(ji):
            s, g, r = jobs[ji]
            st = sets[ji % NSET]
            d, L, Lh, nb, m0, lo, hi, ulo, uhi = geom(s, g)
            base = 5 + 4 * g
            qe, qo, kr, va, vstg = st["qe"], st["qo"], st["kr"], st["va"], st["vstg"]
            if s == 1 and ji - first_s1 < NSET:
                init_ones(st)
            krv = kr.ap.rearrange("p (c u) -> p c u", c=2)
            vav = va.ap.rearrange("p (b n) -> p b n", n=512)
            vsg = vstg.ap.rearrange("p (b n) -> p b n", n=256)
            qsrc = self.qkT.rearrange("(i p) (r m) -> p i r m", p=128, r=d)
            qev = qe.ap.rearrange("p (c m) -> p c m", c=2)
            qov = qo.ap.rearrange("p (c m) -> p c m", c=2)
            self.dma("sp", qev[0:64, :, 0:Lh], qsrc[0:64, base:base + 2, r, m0:m0 + Lh],
                     [self.qkT_buf[base], self.qkT_buf[base + 1]], [qe.buf], st["kqe"])
            self.dma("sp", qov[64:128, :, 0:Lh], qsrc[64:128, base:base + 2, r, m0:m0 + Lh],
                     [self.qkT_buf[base], self.qkT_buf[base + 1]], [qo.buf], st["kqo"])
            if ulo > 0:
                self.memset("pool", krv[:, :, 0:ulo], 0.0, [kr.buf])
            if uhi < Lh + 128:
                self.memset("pool", krv[:, :, uhi:Lh + 128], 0.0, [kr.buf])
            self.dma("sp", krv[:, :, ulo:uhi], qsrc[:, base + 2:base + 4, r, lo:hi],
                     [self.qkT_buf[base + 2], self.qkT_buf[base + 3]], [kr.buf], st["kkr"])
            vsrc = self.vtm.rearrange("(m r) n -> r m n", r=d)[r]
            c0 = 128 + 256 * g
            if ulo > 0:
                self.dma("sp", vsg[64:128, 0, :], vsrc[m0:m0 + 64, c0:c0 + 256], [self.vtm_buf], [vstg.buf], st["kva"])
                b_start = 1
            else:
                b_start = 0
            if uhi < Lh + 128:
                self.dma("sp", vsg[0:64, nb, :], vsrc[m0 + Lh - 64:m0 + Lh, c0:c0 + 256], [self.vtm_buf],
                         [vstg.buf], st["kva"])
                b_end = nb
            else:
                b_end = nb + 1
            mlo = m0 - 64 + 128 * b_start
            self.dma("sp", vsg[:, b_start:b_end, :],
                     vsrc[mlo:mlo + 128 * (b_end - b_start), c0:c0 + 256].rearrange("(b p) n -> p b n", p=128),
                     [self.vtm_buf], [vstg.buf], st["kva"])
            self.copy("pool", vav[:, 0:nb + 1, :].rearrange("p b (h two n) -> p b h two n", h=4, two=2)[:, :, :, 0, :],
                      vsg[:, 0:nb + 1, :].rearrange("p b (h n) -> p b h n", h=4), [vstg.buf], [va.buf])
            if ulo > 0:
                self.memset("pool", vav[0:64, 0, :], 0.0, [va.buf])
            if uhi < Lh + 128:
                self.memset("pool", vav[64:128, nb, :], 0.0, [va.buf])

        for st_ in sets:
            init_ones(st_)
        pend = []
        it = 0
        loads(0)
        for ji, (s, g, r) in enumerate(jobs):
            if g == 0 and r == 0:
                self.memset("pool", U.ap, 0.0, [U.buf])
            st = sets[ji % NSET]
            d, L, Lh, nb, m0, lo, hi, ulo, uhi = geom(s, g)
            qe, qo, kr, va = st["qe"], st["qo"], st["kr"], st["va"]
            krv = kr.ap.rearrange("p (c u) -> p c u", c=2)
            for mb in range(nb):
                sb = stb[it % 3]
                ptb = pt[it % NPT]
                ot = otb[it % 2]
                it += 1
                for h in range(4):
                    qsrc_t = qe if h % 2 == 0 else qo
                    qv = qsrc_t.ap.rearrange("p (c m) -> p c m", c=2)
                    for kk in range(2):
                        col = (h * 2 + kk) * 128
                        self.mm(sb.ap[:, col:col + 128], krv[:, h // 2, 128 * (mb + kk):128 * (mb + kk) + 128],
                                qv[:, h // 2, mb * 128:(mb + 1) * 128], True, True,
                                [kr.buf, qsrc_t.buf], [sb.buf])
                self.act(ptb.ap, sb.ap, AF.Exp, [sb.buf], [ptb.buf], scale=8.0)
                p4 = ptb.ap.rearrange("p (h k q) -> p h k q", h=4, k=2)
                self.tt("dve", p4, p4, mask4, ALU.mult, [ptb.buf, self.cst.buf], [ptb.buf])

                def stage2(ptb=ptb, ot=ot, va=va, mb=mb, d=d, r=r):
                    for h in range(4):
                        for kk in range(2):
                            col = (h * 2 + kk) * 128
                            o = (mb + kk) * 512 + (128 * h if h % 2 == 0 else 128 * h - 64)
                            self.mm(ot.ap[:, h * 128:(h + 1) * 128], va.ap[:, o:o + 128],
                                    ptb.ap[:, col:col + 128], kk == 0, kk == 1, [va.buf, ptb.buf], [ot.buf])
                    if d == 1:
                        uview = Uv[:, :, mb * 128:(mb + 1) * 128]
                    else:
                        uview = Uv.rearrange("p h (m r) -> p h r m", r=d)[:, :, r, mb * 128:(mb + 1) * 128]
                    self.tt("dve", uview, ot.ap.rearrange("p (h q) -> p h q", h=4), uview, ALU.add,
                            [ot.buf, U.buf], [U.buf])

                pend.append((ji, stage2))
                if len(pend) > 2:
                    pend.pop(0)[1]()
                if mb == 0 and ji + 1 < len(jobs):
                    while pend and pend[0][0] < ji:
                        pend.pop(0)[1]()
                    loads(ji + 1)
            if g == 2 and r == B_DIL[2] - 1:
                while pend:
                    pend.pop(0)[1]()
                for h in range(4):
                    rc = recs[h % 2]
                    if h % 2 == 0:
                        self.act(rc.ap[0:64, :], Uv[64:128, h, :], AF.Ln, [U.buf], [rc.buf])
                        self.act(rc.ap[0:64, :], rc.ap[0:64, :], AF.Exp, [rc.buf], [rc.buf], scale=-1.0)
                        self.tt("dve", bTv[0:64, h // 2, s * HS:(s + 1) * HS], Uv[0:64, h, :], rc.ap[0:64, :], ALU.mult,
                                [U.buf, rc.buf], [self.bT.buf])
                    else:
                        self.act(rc.ap[64:128, :], Uv[0:64, h, :], AF.Ln, [U.buf], [rc.buf])
                        self.act(rc.ap[64:128, :], rc.ap[64:128, :], AF.Exp, [rc.buf], [rc.buf], scale=-1.0)
                        self.tt("dve", bTv[64:128, h // 2, s * HS:(s + 1) * HS], Uv[64:128, h, :], rc.ap[64:128, :],
                                ALU.mult, [U.buf, rc.buf], [self.bT.buf])
        S.barrier(self.cast_keys)
        A.release()

    def va_lhs(self, vav, blk, h):
        o = blk * 512 + (128 * h if h % 2 == 0 else 128 * h - 64)
        return self.va_flat[:, o:o + 128]

    def phase_mix(self, l):
        A, S = self.A, self.S
        A.mark()
        aTv = self.aT.ap.rearrange("p (c t) -> p c t", c=4)
        bTv = self.bT.ap.rearrange("p (c t) -> p c t", c=2)
        wg = A.alloc(16 * 1024, BF16, "wg")
        self.dma("sp", wg.ap.rearrange("p (a k) -> p a k", a=16), self.Wb["g"][l % 2].rearrange("a p k -> p a k"),
                 [self.Wb_buf["g"][l % 2]], [wg.buf], self.key("k29"))
        wgv = wg.ap.rearrange("p (a c f) -> p a c f", a=16, c=8)
        wba = A.alloc(4096, BF16, "wba")
        self.dma("sp", wba.ap, self.Wb["ba"][l % 2], [self.Wb_buf["ba"][l % 2]], [wba.buf], self.key("k30"))
        wbav = wba.ap.rearrange("p (c n) -> p c n", c=4)
        wbb = A.alloc(2048, BF16, "wbb")
        self.dma("sp", wbb.ap, self.Wb["bb"][l % 2], [self.Wb_buf["bb"][l % 2]], [wbb.buf], self.key("k31"))
        wbbv = wbb.ap.rearrange("p (c n) -> p c n", c=2)
        wo = A.alloc(8192, BF16, "wo")
        self.dma("sp", wo.ap, self.Wb["o"][l % 2], [self.Wb_buf["o"][l % 2]], [wo.buf], self.key("k32"))
        wov = wo.ap.rearrange("p (c n) -> p c n", c=8)
        xts = [A.alloc(8 * 512, F32, "xtm%d" % i) for i in range(2)]
        kx = [self.key("r12_%d" % _i) for _i in range(2)]
        sq = A.alloc(8 * 512, BF16, "sqm")
        rstd = A.alloc(512, F32, "rstdm")
        tmp = [A.alloc(512, F32, "tmpm%d" % i) for i in range(2)]
        hts = [A.alloc(8 * 512, BF16, "htm%d" % i) for i in range(2)]
        gates = A.alloc(16 * 512, BF16, "gates")
        gv = gates.ap.rearrange("p (a t) -> p a t", a=16)
        mixed = A.alloc(8 * 512, BF16, "mixed")
        mv = mixed.ap.rearrange("p (c t) -> p c t", c=8)
        t1 = [A.alloc(512, F32, "t1%d" % i) for i in range(2)]
        t2 = [A.alloc(512, F32, "t2%d" % i) for i in range(2)]
        nb = 0

        def prep(j):
            xt = xts[j % 2]
            ht = hts[j % 2]
            self.dma("sp", xt.ap.rearrange("p (c t) -> p c t", c=8),
                     self.xT.rearrange("(c p) t -> p c t", p=128)[:, :, j * 512:(j + 1) * 512],
                     [self.xT_buf[j]], [xt.buf], kx[j % 2])
            self.make_h(xt, ht.ap.rearrange("p (c t) -> p c t", c=8), ht.buf, self.A1, 0, sq, rstd, tmp, self.bank[7])

        prep(0)
        for j in range(8):
            xt = xts[j % 2]
            ht = hts[j % 2]
            tsl = slice(j * 512, (j + 1) * 512)
            hv = ht.ap.rearrange("p (c t) -> p c t", c=8)
            for a in range(16):
                bk = self.bank[nb % 6]
                nb += 1
                for c in range(8):
                    self.mm(bk.ap, wgv[:, a, c, :], hv[:, c, :], c == 0, c == 7, [wg.buf, ht.buf], [bk.buf])
                self.act(gv[:, a, :], bk.ap, AF.Sigmoid, [bk.buf], [gates.buf])
            for oc in range(8):
                bA = self.bank[nb % 6]
                bB = self.bank[(nb + 1) % 6]
                nb += 2
                for c in range(4):
                    self.mm(bA.ap, wbav[:, c, oc * 128:(oc + 1) * 128], aTv[:, c, tsl], c == 0, c == 3,
                            [wba.buf, self.aT.buf], [bA.buf])
                for c in range(2):
                    self.mm(bB.ap, wbbv[:, c, oc * 128:(oc + 1) * 128], bTv[:, c, tsl], c == 0, c == 1,
                            [wbb.buf, self.bT.buf], [bB.buf])
                a1, a2 = t1[oc % 2], t2[oc % 2]
                self.tt("dve", a1.ap, bA.ap, gv[:, oc, :], ALU.mult, [bA.buf, gates.buf], [a1.buf])
                self.tt("dve", a2.ap, bB.ap, gv[:, 8 + oc, :], ALU.mult, [bB.buf, gates.buf], [a2.buf])
                self.tt("pool", mv[:, oc, :], a1.ap, a2.ap, ALU.add, [a1.buf, a2.buf], [mixed.buf])
            if j + 1 < 8:
                prep(j + 1)
            xv = xt.ap.rearrange("p (c t) -> p c t", c=8)
            for oc in range(8):
                bk = self.bank[nb % 6]
                nb += 1
                for c in range(8):
                    self.mm(bk.ap, wov[:, c, oc * 128:(oc + 1) * 128], mv[:, c, :], c == 0, c == 7,
                            [wo.buf, mixed.buf], [bk.buf])
                self.stt("dve", xv[:, oc, :], bk.ap, self.mod.ap[:, 16 + oc:17 + oc], xv[:, oc, :], ALU.mult, ALU.add,
                         [bk.buf, self.mod.buf, xt.buf], [xt.buf])
            self.dma("sp", self.xT.rearrange("(c p) t -> p c t", p=128)[:, :, tsl],
                     xt.ap.rearrange("p (c t) -> p c t", c=8), [xt.buf], [self.xT_buf[j]], kx[j % 2])
        S.barrier(self.cast_keys)
        A.release()
        A.release()

    def phase_ffn(self, l):
        A, S = self.A, self.S
        A.mark()
        TG = 1024
        NSUB = TG // 512
        NG = S_LEN // TG
        xts = [A.alloc(8 * 512, F32, "xtf%d" % i) for i in range(NSUB)]
        kx = [self.key("r13_%d" % _i) for _i in range(NSUB)]
        xpre = A.alloc(8 * 512, F32, "xpre")
        kxp = self.key("xpre")
        sq = A.alloc(8 * 512, BF16, "sqf")
        rstd = A.alloc(512, F32, "rstdf")
        tmp = [A.alloc(512, F32, "tmpf%d" % i) for i in range(2)]
        hts = [A.alloc(8 * TG, BF16, "htf%d" % i) for i in range(2)]
        uT = A.alloc(32 * TG, BF16, "uT")
        uv = uT.ap.rearrange("p (k t) -> p k t", k=32)
        w1 = [A.alloc(4 * 1024, BF16, "w1_%d" % i) for i in range(2)]
        k1 = [self.key("r14_%d" % _i) for _i in range(2)]
        w2 = [A.alloc(4096, BF16, "w2_%d" % i) for i in range(2)]
        k2 = [self.key("r15_%d" % _i) for _i in range(2)]
        rl = [A.alloc(512, F32, "rl%d" % i) for i in range(3)]
        nb = 0
        nr = 0
        xTv = self.xT.rearrange("(c p) t -> p c t", p=128)

        def prefetch_h(tg):
            ht = hts[tg % 2]
            hv_ = ht.ap.rearrange("p (c t) -> p c t", c=8)
            for sub in range(NSUB):
                j = tg * NSUB + sub
                self.dma("sp", xpre.ap.rearrange("p (c t) -> p c t", c=8), xTv[:, :, j * 512:(j + 1) * 512],
                         [self.xT_buf[j]], [xpre.buf], kxp)
                self.make_h(xpre, hv_[:, :, sub * 512:(sub + 1) * 512], ht.buf, self.A2, 24, sq, rstd, tmp, self.bank[7])

        prefetch_h(0)
        for tg in range(NG):
            ht = hts[tg % 2]
            hv = ht.ap.rearrange("p (c t) -> p c t", c=8)
            for sub in range(NSUB):
                j = tg * NSUB + sub
                xt = xts[sub]
                self.dma("sp", xt.ap.rearrange("p (c t) -> p c t", c=8), xTv[:, :, j * 512:(j + 1) * 512],
                         [self.xT_buf[j]], [xt.buf], kx[sub])
            for q in range(8):
                w = w1[q % 2]
                self.dma("sp", w.ap.rearrange("p (a k) -> p a k", a=4),
                         self.Wb["f1"][l % 2][q * 4:(q + 1) * 4].rearrange("a p k -> p a k"),
                         [self.Wb_buf["f1"][l % 2]], [w.buf], k1[q % 2])
                wv_ = w.ap.rearrange("p (a c f) -> p a c f", a=4, c=8)
                for a in range(4):
                    hc = q * 4 + a
                    for sub in range(NSUB):
                        bk = self.bank[nb % 7]
                        nb += 1
                        for c in range(8):
                            self.mm(bk.ap, wv_[:, a, c, :], hv[:, c, sub * 512:(sub + 1) * 512], c == 0, c == 7,
                                    [w.buf, ht.buf], [bk.buf])
                        r = rl[nr % 3]
                        nr += 1
                        self.act(r.ap, bk.ap, AF.Relu, [bk.buf], [r.buf])
                        self.tt("pool", uv[:, hc, sub * 512:(sub + 1) * 512], r.ap, r.ap, ALU.mult, [r.buf], [uT.buf])
            if tg + 1 < NG:
                prefetch_h(tg + 1)
            for oc in range(8):
                w = w2[oc % 2]
                self.dma("sp", w.ap, self.Wb["f2"][l % 2][oc], [self.Wb_buf["f2"][l % 2]], [w.buf], k2[oc % 2])
                wv_ = w.ap.rearrange("p (k f) -> p k f", k=32)
                for sub in range(NSUB):
                    xt = xts[sub]
                    xv = xt.ap.rearrange("p (c t) -> p c t", c=8)
                    bk = self.bank[nb % 7]
                    nb += 1
                    for kc in range(32):
                        self.mm(bk.ap, wv_[:, kc, :], uv[:, kc, sub * 512:(sub + 1) * 512], kc == 0, kc == 31,
                                [w.buf, uT.buf], [bk.buf])
                    self.stt("dve", xv[:, oc, :], bk.ap, self.mod.ap[:, 40 + oc:41 + oc], xv[:, oc, :], ALU.mult, ALU.add,
                             [bk.buf, self.mod.buf, xt.buf], [xt.buf])
            for sub in range(NSUB):
                j = tg * NSUB + sub
                xt = xts[sub]
                self.dma("sp", xTv[:, :, j * 512:(j + 1) * 512],
                         xt.ap.rearrange("p (c t) -> p c t", c=8), [xt.buf], [self.xT_buf[j]], kx[sub])
        S.barrier(self.cast_keys)
        A.release()


def _fm(v, n):
    return np.ascontiguousarray(v.reshape(v.shape[:-1] + (n, 128)).swapaxes(-1, -2))


def _lhs_chunks(W):
    K, N = W.shape
    return np.ascontiguousarray(W.reshape(K // 128, 128, N // 128, 128).transpose(2, 1, 0, 3)).reshape(N // 128, 128, K)


def _rhs_rows(W):
    K, N = W.shape
    return np.ascontiguousarray(W.reshape(K // 128, 128, N).transpose(1, 0, 2)).reshape(128, (K // 128) * N)


_PERM_AX = np.concatenate([np.arange(0, 16), np.arange(32, 48), np.arange(16, 32), np.arange(48, 64)])
_SWAP = np.concatenate([np.arange(32, 64), np.arange(0, 32)])


def _consts():
    f = np.float32
    ident = np.eye(128, dtype=f)
    ones = np.ones((128, 128), f)
    e64 = np.kron(np.eye(2, dtype=f), np.ones((64, 64), f))
    i = np.arange(128)[:, None]
    j = np.arange(128)[None, :]
    mask0 = (i >= j).astype(f)
    mask1 = (i <= j).astype(f)
    cst = np.ascontiguousarray(np.stack([ones, e64, mask0, mask1], axis=1))
    theta = np.float32(10000.0)
    pos = np.arange(S_LEN)
    inv32 = (theta ** (-np.arange(0, 64, 2, dtype=f) / f(64))).astype(f)
    ang = pos.astype(f)[:, None] * inv32[None, :]
    cs, sn = np.cos(ang).astype(f), np.sin(ang).astype(f)
    c64 = np.concatenate([cs, cs], axis=1)
    s64 = np.concatenate([-sn, sn], axis=1)
    C_seq = np.ascontiguousarray(np.tile(c64, (1, 2)).T)
    S_seq = np.ascontiguousarray(np.tile(s64, (1, 2)).T)
    inv16 = (theta ** (-np.arange(0, 32, 2, dtype=f) / f(32))).astype(f)
    row = (pos // 64).astype(f)
    col = (pos % 64).astype(f)
    ar = row[:, None] * inv16[None, :]
    ac = col[:, None] * inv16[None, :]
    c32 = np.concatenate([np.cos(ar), np.cos(ac)], axis=1).astype(f)
    s32 = np.concatenate([np.sin(ar), np.sin(ac)], axis=1).astype(f)
    c64a = np.concatenate([c32, c32], axis=1)
    s64a = np.concatenate([-s32, s32], axis=1)
    C_ax = np.ascontiguousarray(np.tile(c64a, (1, 2)).T)
    S_ax = np.ascontiguousarray(np.tile(s64a, (1, 2)).T)
    tabs = np.ascontiguousarray(np.stack([C_seq, S_seq, C_ax, S_ax]))
    return ident, cst, tabs


def _prep_weights(inp, layers):
    f = np.float32
    out = {k: [] for k in ("b_ada", "g_mix", "g_mlp", "g_qk", "w_ada", "w_qk", "w_qks", "w_v", "w_g", "w_ba", "w_bb",
                           "w_o", "w_f1", "w_f2")}
    for l in layers:
        w_in = np.asarray(inp["w_in"][l], f)
        cols = {}
        off = 0
        names = ["qa", "ka", "va"] + [n + str(g) for g in range(3) for n in ("qb", "kb", "vb")] + ["ga", "gb"]
        sizes = [512, 128, 128] + [256] * 9 + [1024, 1024]
        for n, sz in zip(names, sizes):
            cols[n] = w_in[:, off:off + sz]
            off += sz

        def perm_heads(Wc, perm):
            K, N = Wc.shape
            return Wc.reshape(K, N // 64, 64)[:, :, perm].reshape(K, N)

        qn_a = np.asarray(inp["q_norm_a"][l], f)
        kn_a = np.asarray(inp["k_norm_a"][l], f)
        qn_b = np.asarray(inp["q_norm_b"][l], f)
        kn_b = np.asarray(inp["k_norm_b"][l], f)
        raw_cols, sw_cols, g_raw, g_sw = [], [], [], []

        def add(Wc, gain, axial):
            Wp = perm_heads(Wc, _PERM_AX) if axial else Wc
            gp = gain[_PERM_AX] if axial else gain
            raw_cols.append(Wp)
            sw_cols.append(perm_heads(Wp, _SWAP))
            nh = Wc.shape[1] // 64
            g_raw.append(np.tile(gp, nh))
            g_sw.append(np.tile(gp[_SWAP], nh))

        add(cols["qa"], qn_a, True)
        add(cols["ka"], kn_a, True)
        for g in range(3):
            add(cols["qb%d" % g], qn_b[g], False)
            add(cols["kb%d" % g], kn_b[g], False)
        Wqk = np.concatenate(raw_cols, axis=1)
        Wqks = np.concatenate(sw_cols, axis=1)
        gr = np.concatenate(g_raw)
        gs = np.concatenate(g_sw)
        out["g_qk"].append(np.concatenate([_fm(gr, NQK), _fm(gs, NQK)], axis=1))
        out["w_qk"].append(_lhs_chunks(Wqk))
        out["w_qks"].append(_lhs_chunks(Wqks))
        Wv = np.concatenate([cols["va"], cols["vb0"], cols["vb1"], cols["vb2"]], axis=1)
        out["w_v"].append(_rhs_rows(Wv))
        out["w_g"].append(_lhs_chunks(np.concatenate([cols["ga"], cols["gb"]], axis=1)))
        out["w_ba"].append(_rhs_rows(np.asarray(inp["w_branch_a"][l], f)))
        out["w_bb"].append(_rhs_rows(np.asarray(inp["w_branch_b"][l], f)))
        out["w_o"].append(_rhs_rows(np.asarray(inp["w_out"][l], f)))
        out["w_f1"].append(_lhs_chunks(np.asarray(inp["w_ff1"][l], f)))
        out["w_f2"].append(_lhs_chunks(np.asarray(inp["w_ff2"][l], f)))
        out["w_ada"].append(_lhs_chunks(np.asarray(inp["w_ada"][l], f)))
        out["b_ada"].append(_fm(np.asarray(inp["b_ada"][l], f), 48))
        out["g_mix"].append(_fm(np.asarray(inp["g_mix"][l], f), 8))
        out["g_mlp"].append(_fm(np.asarray(inp["g_mlp"][l], f), 8))
    return {k: np.ascontiguousarray(np.stack(v)) for k, v in out.items()}


_PROG_CACHE = {}


def _get_prog(NL, debug=False):
    key = (NL, debug)
    if key not in _PROG_CACHE:
        p = Prog(NL, debug)
        p.build()
        _PROG_CACHE[key] = p
    return _PROG_CACHE[key]


def run_layers(x, c, inp, layers, cores=NCORES, debug=False):
    prog = _get_prog(len(layers), debug)
    ident, cst, tabs = _consts()
    wts = _prep_weights(inp, layers)
    in_maps = []
    for b in range(cores):
        m = {"x": np.ascontiguousarray(x[b]), "c_fm": _fm(np.asarray(c[b], np.float32), 8), "ident": ident,
             "cst_bf": cst, "tabs": tabs}
        m.update(wts)
        in_maps.append(m)
    res = run_bass_kernel_spmd(prog.nc, in_maps, core_ids=list(range(cores)))
    if debug:
        return res
    return np.stack([np.asarray(r["out"]) for r in res.results])


FUSED = True


def kernel(x, c, w_ada, b_ada, g_mix, g_mlp, w_in, q_norm_a, k_norm_a, q_norm_b, k_norm_b,
           w_branch_a, w_branch_b, w_out, w_ff1, w_ff2):
    inp = dict(w_ada=w_ada, b_ada=b_ada, g_mix=g_mix, g_mlp=g_mlp, w_in=w_in, q_norm_a=q_norm_a, k_norm_a=k_norm_a,
               q_norm_b=q_norm_b, k_norm_b=k_norm_b, w_branch_a=w_branch_a, w_branch_b=w_branch_b, w_out=w_out,
               w_ff1=w_ff1, w_ff2=w_ff2)
    x = np.asarray(x, np.float32)
    c = np.asarray(c, np.float32)
    if FUSED:
        return run_layers(x, c, inp, list(range(DEPTH))).astype(np.float32)
    for l in range(DEPTH):
        x = run_layers(x, c, inp, [l])
    return x.astype(np.float32)
```

```python
import contextlib
import numpy as np
import concourse.bass as bass
import concourse.mybir as mybir
from concourse.bass_utils import run_bass_kernel_spmd

F32 = mybir.dt.float32
BF16 = mybir.dt.bfloat16
AF = mybir.ActivationFunctionType
ALU = mybir.AluOpType

D = 1024
S_LEN = 4096
DEPTH = 4
NCORES = 8
EPS = 1e-6
NQK = 17
B_DIL = (1, 4, 16)

ENGS = ("pe", "act", "dve", "pool", "sp")
NE = len(ENGS)
EIDX = {e: i for i, e in enumerate(ENGS)}


class Buf:
    __slots__ = ("name", "last_w", "readers")

    def __init__(self, name=""):
        self.name = name
        self.last_w = None
        self.readers = []


class Op:
    __slots__ = ("eng", "idx", "fn", "deps", "dma_sem", "dma_val", "signalled", "count", "waits", "clock")

    def __init__(self, eng, idx, fn):
        self.eng = eng
        self.idx = idx
        self.fn = fn
        self.deps = []
        self.dma_sem = None
        self.dma_val = 0
        self.signalled = False
        self.count = 0
        self.waits = None
        self.clock = None


class Sched:
    def __init__(self):
        self.ops = {e: [] for e in ENGS}
        self.all_ops = []
        self.dma_sem_count = {}
        self.dma_last = {}

    def op(self, eng, fn, reads=(), writes=(), dma=None):
        o = Op(eng, len(self.ops[eng]), fn)
        deps = []
        for b in reads:
            if b.last_w is not None:
                deps.append(b.last_w)
        for b in writes:
            if b.last_w is not None:
                deps.append(b.last_w)
            deps.extend(b.readers)
        o.deps = deps
        for b in reads:
            b.readers.append(o)
        for b in writes:
            b.last_w = o
            b.readers = []
        if dma is not None:
            v = self.dma_sem_count.get(dma, 0) + 16
            self.dma_sem_count[dma] = v
            o.dma_sem = dma
            o.dma_val = v
            self.dma_last[dma] = o
        self.ops[eng].append(o)
        self.all_ops.append(o)
        return o

    def barrier(self, exclude=()):
        last = [self.ops[e][-1] for e in ENGS if self.ops[e]] + \
               [o for k, o in self.dma_last.items() if k not in exclude]
        for e in ENGS:
            o = self.op(e, lambda h: h.nop())
            o.deps = list(last)

    def resolve(self):
        clock = {e: [-1] * NE for e in ENGS}
        dma_seen = {e: {} for e in ENGS}
        for o in self.all_ops:
            e = o.eng
            ck = clock[e]
            ei = EIDX[e]
            if e == "pe":
                ck[ei] = o.idx - 1
            ewait = {}
            dwait = {}
            seen = dma_seen[e]
            for d in o.deps:
                if d.dma_sem is not None:
                    if d.dma_val > seen.get(d.dma_sem, 0) and d.dma_val > dwait.get(d.dma_sem, 0):
                        dwait[d.dma_sem] = d.dma_val
                else:
                    di = EIDX[d.eng]
                    if ck[di] >= d.idx:
                        continue
                    w = ewait.get(di)
                    if w is None or d.idx > w.idx:
                        ewait[di] = d
            wl = []
            if dwait:
                for d in o.deps:
                    if d.dma_sem is not None and dwait.get(d.dma_sem, 0) >= d.dma_val:
                        dc = d.clock
                        qi = EIDX[d.eng]
                        for j in range(NE):
                            if j != qi and dc[j] > ck[j]:
                                ck[j] = dc[j]
                for k, v in dwait.items():
                    wl.append((0, k, v))
                    seen[k] = v
            for di, d in ewait.items():
                if ck[di] >= d.idx:
                    continue
                d.signalled = True
                wl.append((1, d.eng, d))
                dc = d.clock
                for j in range(NE):
                    if dc[j] > ck[j]:
                        ck[j] = dc[j]
            o.waits = wl
            c = list(ck)
            if o.dma_sem is None and o.idx > c[ei]:
                c[ei] = o.idx
            o.clock = c
            o.deps = None
        for e in ENGS:
            n = 0
            for o in self.ops[e]:
                if o.signalled:
                    n += 1
                o.count = n

    def emit_engine(self, e, handle, eng_sems, dma_sems, final_wait_all=False):
        for o in self.ops[e]:
            for w in o.waits:
                if w[0] == 0:
                    handle.wait_ge(dma_sems[w[1]], w[2])
                else:
                    handle.wait_ge(eng_sems[w[1]], w[2].count)
            ins = o.fn(handle)
            if o.dma_sem is not None:
                ins.then_inc(dma_sems[o.dma_sem], 16)
            elif o.signalled:
                ins.then_inc(eng_sems[e], 1)
        if final_wait_all:
            for k, v in self.dma_sem_count.items():
                handle.wait_ge(dma_sems[k], v)


class T:
    __slots__ = ("ap", "buf")

    def __init__(self, ap, buf):
        self.ap = ap
        self.buf = buf

    def v(self, pattern, **kw):
        return self.ap.rearrange(pattern, **kw)


class Arena:
    def __init__(self, base_ap, n_words):
        self.base = base_ap
        self.n = n_words
        self.off = 0
        self.marks = []

    def alloc(self, n_elems, dtype=F32, name=""):
        nbytes = n_elems * (4 if dtype == F32 else 2)
        nw = (nbytes + 3) // 4
        nw = (nw + 15) // 16 * 16
        assert self.off + nw <= self.n, f"arena overflow {name}: {self.off}+{nw}>{self.n}"
        ap = self.base[:, self.off:self.off + nw]
        self.off += nw
        if dtype != F32:
            ap = ap.bitcast(dtype)
        ap = ap[:, 0:n_elems]
        return T(ap, Buf(name))

    def mark(self):
        self.marks.append(self.off)

    def release(self):
        self.off = self.marks.pop()


class Prog:
    def __init__(self, NL, debug=False):
        self.NL = NL
        self.debug = debug
        self.nc = bass.Bass("TRN2", target_bir_lowering=False)
        self.S = Sched()
        self.nkeys = 0
        self.named_keys = {}
        self._auto = 0

    def key(self, name=None):
        if name is None:
            raise ValueError("key needs a name")
        if name not in self.named_keys:
            self.named_keys[name] = self.nkeys
            self.nkeys += 1
        return self.named_keys[name]

    def dma(self, eng, out, in_, reads, writes, key):
        return self.S.op(eng, lambda h: h.dma_start(out=out, in_=in_), reads, writes, dma=key)

    def mm(self, out, lhsT, rhs, start, stop, reads, writes):
        return self.S.op("pe", lambda h: h.matmul(out, lhsT=lhsT, rhs=rhs, start=start, stop=stop), reads, writes)

    def act(self, out, in_, func, reads, writes, scale=1.0, bias=0.0):
        return self.S.op("act", lambda h: h.activation(out=out, in_=in_, func=func, bias=bias, scale=scale),
                         reads, writes)

    def tt(self, eng, out, in0, in1, op, reads, writes):
        return self.S.op(eng, lambda h: h.tensor_tensor(out=out, in0=in0, in1=in1, op=op), reads, writes)

    def stt(self, eng, out, in0, scalar, in1, op0, op1, reads, writes):
        return self.S.op(eng, lambda h: h.scalar_tensor_tensor(out=out, in0=in0, scalar=scalar, in1=in1,
                                                                op0=op0, op1=op1), reads, writes)

    def ts(self, eng, out, in0, s1, s2, op0, op1, reads, writes):
        return self.S.op(eng, lambda h: h.tensor_scalar(out=out, in0=in0, scalar1=s1, scalar2=s2, op0=op0, op1=op1),
                         reads, writes)

    def copy(self, eng, out, in_, reads, writes):
        if eng == "act":
            return self.act(out, in_, AF.Copy, reads, writes)
        return self.S.op(eng, lambda h: h.tensor_copy(out=out, in_=in_), reads, writes)

    def memset(self, eng, ap, val, writes):
        return self.S.op(eng, lambda h: h.memset(ap, val), (), writes)

    def recip(self, out, in_, reads, writes):
        return self.S.op("dve", lambda h: h.reciprocal(out=out, in_=in_), reads, writes)

    def declare(self):
        nc, NL = self.nc, self.NL

        def inp(name, shape, dt=F32):
            return nc.dram_tensor(name, list(shape), dt, kind="ExternalInput").ap()

        def scr(name, shape, dt):
            kind = "ExternalOutput" if (self.debug and name in ("qkT", "vtm", "xT", "dbg")) else "Internal"
            return nc.dram_tensor(name, list(shape), dt, kind=kind).ap()

        self.x_in = inp("x", [S_LEN, D])
        self.out = nc.dram_tensor("out", [S_LEN, D], F32, kind="ExternalOutput").ap()
        self.c_in = inp("c_fm", [128, 8])
        self.ident_in = inp("ident", [128, 128])
        self.cst_bf_in = inp("cst_bf", [128, 4, 128])
        self.tabs_in = inp("tabs", [4, 128, S_LEN])
        self.b_ada = inp("b_ada", [NL, 128, 48])
        self.g_mix = inp("g_mix", [NL, 128, 8])
        self.g_mlp = inp("g_mlp", [NL, 128, 8])
        self.g_qk = inp("g_qk", [NL, 128, 2 * NQK])
        W = {}
        W["ada"] = (inp("w_ada", [NL, 48, 128, 8 * 128]), [48, 128, 1024])
        W["qk"] = (inp("w_qk", [NL, NQK, 128, 1024]), [NQK, 128, 1024])
        W["qks"] = (inp("w_qks", [NL, NQK, 128, 1024]), [NQK, 128, 1024])
        W["v"] = (inp("w_v", [NL, 128, 8 * 896]), [128, 8 * 896])
        W["g"] = (inp("w_g", [NL, 16, 128, 1024]), [16, 128, 1024])
        W["ba"] = (inp("w_ba", [NL, 128, 4 * 1024]), [128, 4096])
        W["bb"] = (inp("w_bb", [NL, 128, 2 * 1024]), [128, 2048])
        W["o"] = (inp("w_o", [NL, 128, 8 * 1024]), [128, 8192])
        W["f1"] = (inp("w_f1", [NL, 32, 128, 1024]), [32, 128, 1024])
        W["f2"] = (inp("w_f2", [NL, 8, 128, 4096]), [8, 128, 4096])
        self.W = W
        self.Wb = {}
        self.Wb_buf = {}
        self.Wb_key = {}
        for k, (ap, shp) in W.items():
            self.Wb[k] = [scr("wb%d_%s" % (i, k), shp, BF16) for i in range(2)]
            self.Wb_buf[k] = [Buf("wb%d_%s" % (i, k)) for i in range(2)]
            self.Wb_key[k] = [self.key("wb%d_%s" % (i, k)) for i in range(2)]
        self.cast_keys = set(k for ks in self.Wb_key.values() for k in ks)
        self.xT = scr("xT", [D, S_LEN], F32)
        self.xT_buf = [Buf("xT%d" % j) for j in range(8)]
        self.qkT = scr("qkT", [NQK * 128, S_LEN], BF16)
        self.qkT_buf = [Buf("qkT%d" % i) for i in range(NQK)]
        self.vtm = scr("vtm", [S_LEN, 896], BF16)
        self.vtm_buf = Buf("vtm")

    def cast_plan(self, l, names, step=2048):
        plan = []
        for k in names:
            src, shp = self.W[k]
            n = int(np.prod(shp))
            sap = src[l]
            dap = self.Wb[k][l % 2]
            if len(shp) == 3:
                sap = sap.rearrange("a p (r b) -> (a p r) b", b=1024)
                dap = dap.rearrange("a p (r b) -> (a p r) b", b=1024)
            else:
                sap = sap.rearrange("p (r b) -> (p r) b", b=1024)
                dap = dap.rearrange("p (r b) -> (p r) b", b=1024)
            rows = n // 1024
            for r0 in range(0, rows, step):
                r1 = min(rows, r0 + step)
                plan.append((dap[r0:r1, :], sap[r0:r1, :], self.Wb_buf[k][l % 2], self.Wb_key[k][l % 2]))
        return plan

    def cast_emit(self, item, pace=()):
        dap, sap, buf, key = item
        self.dma("pool", dap, sap, pace, [buf], key)

    def cast_weights(self, l, names):
        for item in self.cast_plan(l, names, step=8192):
            self.cast_emit(item)

    def build(self):
        nc = self.nc
        self.declare()
        NW = 52992
        with (nc.sbuf_tensor("arena", [128, NW], F32) as arena_t,
              nc.psum_tensor("ps", [128, 4096], F32) as ps_t):
            self.A = Arena(arena_t, NW)
            self.ps = ps_t
            self.bank = [T(ps_t[:, b * 512:(b + 1) * 512], Buf("bank%d" % b)) for b in range(8)]
            self.body()
            self.S.resolve()
            with contextlib.ExitStack() as es:
                eng_sems = {e: es.enter_context(nc.semaphore("s_" + e)) for e in ENGS}
                dma_sems = {k: es.enter_context(nc.semaphore("d%d" % k)) for k in self.S.dma_sem_count}
                block = es.enter_context(nc.Block())
                S = self.S

                @block.tensor
                def _(h):
                    S.emit_engine("pe", h, eng_sems, dma_sems)

                @block.scalar
                def _(h):
                    S.emit_engine("act", h, eng_sems, dma_sems)

                @block.vector
                def _(h):
                    S.emit_engine("dve", h, eng_sems, dma_sems)

                @block.gpsimd
                def _(h):
                    S.emit_engine("pool", h, eng_sems, dma_sems)

                @block.sync
                def _(h):
                    S.emit_engine("sp", h, eng_sems, dma_sems, final_wait_all=True)
        return nc

    def load_consts(self):
        A = self.A
        self.ident = A.alloc(128, F32, "ident")
        self.dma("sp", self.ident.ap, self.ident_in, (), [self.ident.buf], self.key("k17"))
        cst = A.alloc(4 * 128, BF16, "cst")
        self.dma("pool", cst.ap.rearrange("p (a b) -> p a b", a=4), self.cst_bf_in, (), [cst.buf], self.key("k18"))
        self.cst = cst
        self.ones = T(cst.ap[:, 0:128], cst.buf)
        self.e64 = T(cst.ap[:, 128:256], cst.buf)
        self.masks = T(cst.ap[:, 256:512], cst.buf)
        self.cfm = A.alloc(8, F32, "cfm")
        self.dma("sp", self.cfm.ap, self.c_in, (), [self.cfm.buf], self.key("k19"))
        self.cact = A.alloc(8, BF16, "cact")
        self.act(self.cact.ap, self.cfm.ap, AF.Silu, [self.cfm.buf], [self.cact.buf])
        self.mod = A.alloc(48, F32, "mod")
        self.A1 = A.alloc(8, F32, "A1")
        self.A2 = A.alloc(8, F32, "A2")
        self.gq = A.alloc(2 * NQK, F32, "gq")
        self.small_key = self.key("k20")

    def body(self):
        A = self.A
        self.load_consts()
        self.phase_in()
        WN = ["ada", "qk", "qks", "v", "g", "ba", "bb", "o", "f1", "f2"]
        self.cast_weights(0, WN)
        for l in range(self.NL):
            self.phase_ada(l)
            self.phase_proj(l)
            self.pending_cast = self.cast_plan(l + 1, WN) if l + 1 < self.NL else []
            self.phase_gqa(l)
            while self.pending_cast:
                self.cast_emit(self.pending_cast.pop(0))
            self.phase_dil(l)
            self.phase_mix(l)
            self.phase_ffn(l)
        self.phase_out()

    def phase_in(self):
        A, S = self.A, self.S
        A.mark()
        xin = [A.alloc(1024, F32, "xin%d" % i) for i in range(2)]
        xo = [A.alloc(8 * 512, F32, "xo%d" % i) for i in range(2)]
        kin = [self.key("r1_%d" % _i) for _i in range(2)]
        ko = [self.key("r2_%d" % _i) for _i in range(2)]
        for j in range(8):
            o = xo[j % 2]
            for tb in range(4):
                blk = j * 4 + tb
                xi = xin[blk % 2]
                self.dma("sp", xi.ap, self.x_in[blk * 128:(blk + 1) * 128, :], (), [xi.buf], kin[blk % 2])
                for half in range(2):
                    bk = self.bank[(blk * 2 + half) % 8]
                    for cc in range(4):
                        c = half * 4 + cc
                        self.mm(bk.ap[:, cc * 128:(cc + 1) * 128], xi.ap[:, c * 128:(c + 1) * 128], self.ident.ap,
                                True, True, [xi.buf, self.ident.buf], [bk.buf])
                    dst = o.ap.rearrange("p (c t) -> p c t", c=8)[:, half * 4:(half + 1) * 4, tb * 128:(tb + 1) * 128]
                    src = bk.ap.rearrange("p (c t) -> p c t", c=4)
                    self.copy("dve" if half == 0 else "act", dst, src, [bk.buf], [o.buf])
            self.dma("sp", self.xT.rearrange("(c p) t -> p c t", p=128)[:, :, j * 512:(j + 1) * 512],
                     o.ap.rearrange("p (c t) -> p c t", c=8), [o.buf], [self.xT_buf[j]], ko[j % 2])
        S.barrier(self.cast_keys)
        A.release()

    def phase_out(self):
        A, S = self.A, self.S
        A.mark()
        xi2 = [A.alloc(8 * 512, F32, "xo_in%d" % i) for i in range(2)]
        xo2 = [A.alloc(1024, F32, "xo_out%d" % i) for i in range(2)]
        kin = [self.key("r3_%d" % _i) for _i in range(2)]
        ko = [self.key("r4_%d" % _i) for _i in range(2)]
        for j in range(8):
            xi = xi2[j % 2]
            self.dma("sp", xi.ap.rearrange("p (c t) -> p c t", c=8),
                     self.xT.rearrange("(c p) t -> p c t", p=128)[:, :, j * 512:(j + 1) * 512],
                     [self.xT_buf[j]], [xi.buf], kin[j % 2])
            xv = xi.ap.rearrange("p (c t) -> p c t", c=8)
            for tb in range(4):
                blk = j * 4 + tb
                o = xo2[blk % 2]
                for half in range(2):
                    bk = self.bank[(blk * 2 + half) % 8]
                    for cc in range(4):
                        c = half * 4 + cc
                        self.mm(bk.ap[:, cc * 128:(cc + 1) * 128], xv[:, c, tb * 128:(tb + 1) * 128], self.ident.ap,
                                True, True, [xi.buf, self.ident.buf], [bk.buf])
                    self.copy("dve" if half == 0 else "act", o.ap[:, half * 512:(half + 1) * 512], bk.ap,
                              [bk.buf], [o.buf])
                self.dma("sp", self.out[blk * 128:(blk + 1) * 128, :], o.ap, [o.buf], (), ko[blk % 2])
        A.release()

    def phase_ada(self, l):
        A, S = self.A, self.S
        A.mark()
        wt = [A.alloc(8 * 1024, BF16, "wada%d" % i) for i in range(2)]
        kw = [self.key("r5_%d" % _i) for _i in range(2)]
        bfm = A.alloc(48, F32, "bfm")
        gm = A.alloc(16, F32, "gm")
        bfm.buf = gm.buf = self.gq.buf
        self.dma("sp", bfm.ap, self.b_ada[l], (), [bfm.buf], self.small_key)
        self.dma("sp", gm.ap[:, 0:8], self.g_mix[l], (), [gm.buf], self.small_key)
        self.dma("sp", gm.ap[:, 8:16], self.g_mlp[l], (), [gm.buf], self.small_key)
        self.dma("sp", self.gq.ap, self.g_qk[l], (), [self.gq.buf], self.small_key)
        bk = self.bank[0]
        for piece in range(6):
            w = wt[piece % 2]
            self.dma("sp", w.ap.rearrange("p (f k) -> p f k", f=8),
                     self.Wb["ada"][l % 2][piece * 8:(piece + 1) * 8].rearrange("f p k -> p f k"),
                     [self.Wb_buf["ada"][l % 2]], [w.buf], kw[piece % 2])
            wv = w.ap.rearrange("p (f c k) -> p f c k", f=8, c=8)
            for f in range(8):
                col = piece * 8 + f
                for c in range(8):
                    self.mm(bk.ap[:, col:col + 1], wv[:, f, c, :], self.cact.ap[:, c:c + 1], c == 0, c == 7,
                            [w.buf, self.cact.buf], [bk.buf])
        self.tt("dve", self.mod.ap, bk.ap[:, 0:48], bfm.ap, ALU.add, [bk.buf, bfm.buf], [self.mod.buf])
        m = self.mod.ap
        self.stt("dve", self.A1.ap, m[:, 8:16], 1.0, gm.ap[:, 0:8], ALU.add, ALU.mult, [self.mod.buf, gm.buf],
                 [self.A1.buf])
        self.ts("dve", self.A1.ap, self.A1.ap, 32.0, 0.0, ALU.mult, ALU.add, [self.A1.buf], [self.A1.buf])
        self.stt("dve", self.A2.ap, m[:, 32:40], 1.0, gm.ap[:, 8:16], ALU.add, ALU.mult, [self.mod.buf, gm.buf],
                 [self.A2.buf])
        self.ts("dve", self.A2.ap, self.A2.ap, 32.0, 0.0, ALU.mult, ALU.add, [self.A2.buf], [self.A2.buf])
        S.barrier(self.cast_keys)
        A.release()

    def make_h(self, xt, hdst_view, hbuf, Avec, Bcol0, sq, rstd, tmp, bk):
        for st in self.make_h_stages(xt, hdst_view, hbuf, Avec, Bcol0, sq, rstd, tmp, bk):
            st()

    def make_h_stages(self, xt, hdst_view, hbuf, Avec, Bcol0, sq, rstd, tmp, bk):
        xv = xt.ap.rearrange("p (c t) -> p c t", c=8)
        sv = sq.ap.rearrange("p (c t) -> p c t", c=8)

        def sA():
            self.act(sq.ap, xt.ap, AF.Square, [xt.buf], [sq.buf])

        def sB():
            for c in range(8):
                self.mm(bk.ap, self.ones.ap, sv[:, c, :], c == 0, c == 7, [sq.buf, self.cst.buf], [bk.buf])

        def sC():
            self.act(rstd.ap, bk.ap, AF.Ln, [bk.buf], [rstd.buf], bias=float(D * EPS))
            self.act(rstd.ap, rstd.ap, AF.Exp, [rstd.buf], [rstd.buf], scale=-0.5)
            for c in range(8):
                tc_ = tmp[c % 2]
                self.stt("dve", tc_.ap, xv[:, c, :],
                         Avec.ap[:, c:c + 1], rstd.ap, ALU.mult, ALU.mult, [xt.buf, Avec.buf, rstd.buf], [tc_.buf])
                self.act(hdst_view[:, c, :], tc_.ap, AF.Identity, [tc_.buf, self.mod.buf], [hbuf],
                         bias=self.mod.ap[:, Bcol0 + c:Bcol0 + c + 1])

        return sA, sB, sC

    def phase_proj(self, l):
        A, S = self.A, self.S
        A.mark()
        hT = A.alloc(8 * S_LEN, BF16, "hT_all")
        hv = hT.ap.rearrange("p (c t) -> p c t", c=8)
        tabs = [A.alloc(S_LEN, F32, "tab%d" % i) for i in range(4)]
        for i in range(4):
            self.dma("sp", tabs[i].ap, self.tabs_in[i], (), [tabs[i].buf], self.key("tab%d" % i))
        A.mark()
        xts = [A.alloc(8 * 512, F32, "xt%d" % i) for i in range(2)]
        kx = [self.key("r6_%d" % _i) for _i in range(2)]
        sq = A.alloc(8 * 512, BF16, "sq")
        rstd = A.alloc(512, F32, "rstd")
        tmp = [A.alloc(512, F32, "tmp%d" % i) for i in range(2)]
        for j in range(8):
            xt = xts[j % 2]
            self.dma("sp", xt.ap.rearrange("p (c t) -> p c t", c=8),
                     self.xT.rearrange("(c p) t -> p c t", p=128)[:, :, j * 512:(j + 1) * 512],
                     [self.xT_buf[j]], [xt.buf], kx[j % 2])
            self.make_h(xt, hv[:, :, j * 512:(j + 1) * 512], hT.buf, self.A1, 0, sq, rstd, tmp, self.bank[j % 2])
        S.barrier(self.cast_keys)
        A.release()
        wr = [A.alloc(1024, BF16, "wr%d" % i) for i in range(2)]
        ws = [A.alloc(1024, BF16, "ws%d" % i) for i in range(2)]
        kwr = [self.key("r7_%d" % _i) for _i in range(2)]
        kws = [self.key("r8_%d" % _i) for _i in range(2)]
        stg = [A.alloc(S_LEN, BF16, "stg%d" % i) for i in range(2)]
        kst = [self.key("r9_%d" % _i) for _i in range(2)]
        sqs = [A.alloc(512, BF16, "sqk%d" % i) for i in range(3)]
        rs = [A.alloc(512, F32, "rsk%d" % i) for i in range(3)]
        u1 = [A.alloc(512, F32, "u1%d" % i) for i in range(3)]
        u2 = [A.alloc(512, F32, "u2%d" % i) for i in range(3)]
        it = 0
        pend = []
        for ci in range(NQK):
            w_r, w_s = wr[ci % 2], ws[ci % 2]
            self.dma("sp", w_r.ap, self.Wb["qk"][l % 2][ci], [self.Wb_buf["qk"][l % 2]], [w_r.buf], kwr[ci % 2])
            self.dma("sp", w_s.ap, self.Wb["qks"][l % 2][ci], [self.Wb_buf["qks"][l % 2]], [w_s.buf], kws[ci % 2])
            wrv = w_r.ap.rearrange("p (c f) -> p c f", c=8)
            wsv = w_s.ap.rearrange("p (c f) -> p c f", c=8)
            st = stg[ci % 2]
            axial = ci < 5
            Ct, St = (tabs[2], tabs[3]) if axial else (tabs[0], tabs[1])
            dil = 1 if ci < 5 else B_DIL[(ci - 5) // 4]
            for j in range(8):
                k3 = it % 3
                kE = it % 2
                it += 1
                bR, bS, bE = self.bank[k3 * 2], self.bank[k3 * 2 + 1], self.bank[6 + kE]
                for c in range(8):
                    self.mm(bR.ap, wrv[:, c, :], hv[:, c, j * 512:(j + 1) * 512], c == 0, c == 7,
                            [w_r.buf, hT.buf], [bR.buf])
                for c in range(8):
                    self.mm(bS.ap, wsv[:, c, :], hv[:, c, j * 512:(j + 1) * 512], c == 0, c == 7,
                            [w_s.buf, hT.buf], [bS.buf])
                if pend:
                    pend.pop(0)()
                sqk, rk, a1, a2 = sqs[k3], rs[k3], u1[k3], u2[k3]
                tsl = slice(j * 512, (j + 1) * 512)
                self.act(sqk.ap, bR.ap, AF.Square, [bR.buf], [sqk.buf])
                self.act(a1.ap, bR.ap, AF.Identity, [bR.buf, self.gq.buf], [a1.buf], scale=self.gq.ap[:, ci:ci + 1])
                self.act(a2.ap, bS.ap, AF.Identity, [bS.buf, self.gq.buf], [a2.buf],
                         scale=self.gq.ap[:, NQK + ci:NQK + ci + 1])
                self.tt("dve", a1.ap, a1.ap, Ct.ap[:, tsl], ALU.mult, [a1.buf, Ct.buf], [a1.buf])
                self.tt("dve", a2.ap, a2.ap, St.ap[:, tsl], ALU.mult, [a2.buf, St.buf], [a2.buf])
                self.tt("pool", a1.ap, a1.ap, a2.ap, ALU.add, [a1.buf, a2.buf], [a1.buf])

                def tail(bE=bE, sqk=sqk, rk=rk, a1=a1, st=st, j=j, tsl=tsl, dil=dil, ci=ci):
                    self.mm(bE.ap, self.e64.ap, sqk.ap, True, True, [sqk.buf, self.cst.buf], [bE.buf])
                    self.act(rk.ap, bE.ap, AF.Ln, [bE.buf], [rk.buf], bias=float(64 * EPS))
                    self.act(rk.ap, rk.ap, AF.Exp, [rk.buf], [rk.buf], scale=-0.5)
                    if dil == 1:
                        dst = st.ap[:, tsl]
                        src1, src2 = a1.ap, rk.ap
                    else:
                        n = 512 // dil
                        dst = st.ap.rearrange("p (r m) -> p r m", r=dil)[:, :, j * n:(j + 1) * n]
                        src1 = a1.ap.rearrange("p (m r) -> p r m", r=dil)
                        src2 = rk.ap.rearrange("p (m r) -> p r m", r=dil)
                    self.tt("pool", dst, src1, src2, ALU.mult, [a1.buf, rk.buf], [st.buf])
                    if j == 7:
                        self.dma("sp", self.qkT[ci * 128:(ci + 1) * 128, :], st.ap, [st.buf], [self.qkT_buf[ci]],
                                 kst[ci % 2])

                pend.append(tail)
        while pend:
            pend.pop(0)()
        wv = A.alloc(8 * 896, BF16, "wv")
        self.dma("sp", wv.ap, self.Wb["v"][l % 2], [self.Wb_buf["v"][l % 2]], [wv.buf], self.key("k22"))
        wvv = wv.ap.rearrange("p (c n) -> p c n", c=8)
        vst = [A.alloc(896, BF16, "vst%d" % i) for i in range(2)]
        kv = [self.key("r10_%d" % _i) for _i in range(2)]
        for tb in range(32):
            vs = vst[tb % 2]
            for half in range(2):
                bk = self.bank[6 + half]
                for c in range(8):
                    self.mm(bk.ap[:, 0:448], hv[:, c, tb * 128:(tb + 1) * 128], wvv[:, c, half * 448:(half + 1) * 448],
                            c == 0, c == 7, [hT.buf, wv.buf], [bk.buf])
                self.copy("act" if half == 0 else "dve", vs.ap[:, half * 448:(half + 1) * 448], bk.ap[:, 0:448],
                          [bk.buf], [vs.buf])
            self.dma("sp", self.vtm[tb * 128:(tb + 1) * 128, :], vs.ap, [vs.buf], [self.vtm_buf], kv[tb % 2])
        S.barrier(self.cast_keys)
        A.release()

    def phase_gqa(self, l):
        A, S = self.A, self.S
        self.attn_mark = True
        A.mark()
        self.aT = A.alloc(4 * S_LEN, BF16, "attn_aT")
        self.bT = A.alloc(2 * S_LEN, BF16, "attn_bT")
        aTv = self.aT.ap.rearrange("p (c t) -> p c t", c=4)
        A.mark()
        kT = A.alloc(S_LEN, BF16, "kTa")
        self.dma("sp", kT.ap, self.qkT[4 * 128:5 * 128, :], [self.qkT_buf[4]], [kT.buf], self.key("k23"))
        vstage = A.alloc(32 * 128, BF16, "vstage")
        self.dma("sp", vstage.ap.rearrange("p (b n) -> p b n", b=32),
                 self.vtm.rearrange("(b p) n -> p b n", p=128)[:, :, 0:128], [self.vtm_buf], [vstage.buf], self.key("k24"))
        vaug = A.alloc(32 * 2 * 192, BF16, "vaug")
        vav = vaug.ap.rearrange("p (b g n) -> p b g n", b=32, g=2)
        self.memset("pool", vaug.ap, 1.0, [vaug.buf])
        vsv = vstage.ap.rearrange("p (b g n) -> p b g n", b=32, g=2)
        self.copy("dve", vav[:, :, :, 0:64], vsv, [vstage.buf], [vaug.buf])
        self.copy("pool", vav[:, :, :, 128:192], vsv, [vstage.buf], [vaug.buf])
        NQB = 6
        qp = [A.alloc(512, BF16, "qpad%d" % i) for i in range(NQB)]
        kq = [self.key("gq%d" % _i) for _i in range(NQB)]
        for q in qp:
            self.memset("pool", q.ap, 0.0, [q.buf])
        iters = [(j, h) for j in range(8) for h in range(8)]
        qsel = []
        cntg = [0, 0]
        for (j, h) in iters:
            g = h // 4
            qsel.append(g * 3 + cntg[g] % 3)
            cntg[g] += 1

        def qload(i):
            j, h = iters[i]
            g = h // 4
            q = qp[qsel[i]]
            r0 = (h // 2) * 128 + (h % 2) * 64
            self.dma("sp", q.ap[g * 64:(g + 1) * 64, :], self.qkT[r0:r0 + 64, j * 512:(j + 1) * 512],
                     [self.qkT_buf[h // 2]], [q.buf], kq[qsel[i]])

        NPT = 3
        pt = [A.alloc(1024, BF16, "pt%d" % i) for i in range(NPT)]
        rec = [A.alloc(512, F32, "rec%d" % i) for i in range(2)]
        stb = [T(self.ps[:, i * 1024:(i + 1) * 1024], Buf("st%d" % i)) for i in range(3)]
        otb = [self.bank[6], self.bank[7]]
        it = 0
        step = 0
        LOOK = 2
        for i0 in range(LOOK):
            qload(i0)
        for j in range(8):
            for h in range(8):
                g = h // 4
                ii = j * 8 + h
                if ii + LOOK < len(iters):
                    qload(ii + LOOK)
                q = qp[qsel[ii]]
                ot = otb[it % 2]
                it += 1
                odd = h % 2
                pend = []

                def do_mm2(s2, ptb):
                    for half in range(2):
                        kb = 2 * s2 + half
                        self.mm(ot.ap, vav[:, kb, g, odd * 64:odd * 64 + 128], ptb.ap[:, half * 512:(half + 1) * 512],
                                kb == 0, kb == 31, [vaug.buf, ptb.buf], [ot.buf])

                for s2 in range(16):
                    sb = stb[step % 3]
                    ptb = pt[step % NPT]
                    step += 1
                    for half in range(2):
                        kb = 2 * s2 + half
                        self.mm(sb.ap[:, half * 512:(half + 1) * 512], kT.ap[:, kb * 128:(kb + 1) * 128], q.ap,
                                True, True, [kT.buf, q.buf], [sb.buf])
                    self.act(ptb.ap, sb.ap, AF.Exp, [sb.buf], [ptb.buf], scale=8.0)
                    pend.append((s2, ptb))
                    if len(pend) > 1:
                        do_mm2(*pend.pop(0))
                while pend:
                    do_mm2(*pend.pop(0))
                r = rec[it % 2]
                if odd == 0:
                    self.recip(r.ap[0:64, :], ot.ap[64:128, :], [ot.buf], [r.buf])
                    self.tt("dve", aTv[0:64, h // 2, j * 512:(j + 1) * 512], ot.ap[0:64, :], r.ap[0:64, :], ALU.mult,
                            [ot.buf, r.buf], [self.aT.buf])
                else:
                    self.recip(r.ap[64:128, :], ot.ap[0:64, :], [ot.buf], [r.buf])
                    self.tt("dve", aTv[64:128, h // 2, j * 512:(j + 1) * 512], ot.ap[64:128, :], r.ap[64:128, :], ALU.mult,
                            [ot.buf, r.buf], [self.aT.buf])
                if ii % 3 == 1 and self.pending_cast:
                    self.cast_emit(self.pending_cast.pop(0), pace=[r.buf])
        S.barrier(self.cast_keys)
        A.release()

    def phase_dil(self, l):
        A, S = self.A, self.S
        A.mark()
        bTv = self.bT.ap.rearrange("p (c t) -> p c t", c=2)
        HS = S_LEN // 2
        U = A.alloc(4 * HS, F32, "U")
        Uv = U.ap.rearrange("p (h t) -> p h t", h=4)
        LM = HS
        NBM = LM // 128 + 1
        NSET = 2
        sets = []
        for i in range(NSET):
            st = dict(
                qe=A.alloc(2 * LM, BF16, "qe%d" % i), qo=A.alloc(2 * LM, BF16, "qo%d" % i),
                kr=A.alloc(2 * (LM + 128), BF16, "kr%d" % i),
                va=A.alloc(NBM * 512, BF16, "va%d" % i),
                vstg=A.alloc(NBM * 256, BF16, "vstg%d" % i),
                kqe=self.key("dqe%d" % i), kqo=self.key("dqo%d" % i), kkr=self.key("dkr%d" % i),
                kva=self.key("dva%d" % i))
            self.memset("pool", st["qe"].ap, 0.0, [st["qe"].buf])
            self.memset("pool", st["qo"].ap, 0.0, [st["qo"].buf])
            self.memset("pool", st["vstg"].ap, 0.0, [st["vstg"].buf])
            sets.append(st)
        NPT = 3
        pt = [A.alloc(1024, BF16, "ptd%d" % i) for i in range(NPT)]
        recs = [A.alloc(HS, F32, "recd%d" % i) for i in range(2)]
        stb = [T(self.ps[:, i * 1024:(i + 1) * 1024], Buf("std%d" % i)) for i in range(3)]
        otb = [self.bank[6], self.bank[7]]
        NOT = 2
        maskv = self.masks.ap.rearrange("p (k q) -> p k q", k=2)
        mask4 = maskv.unsqueeze(1).broadcast_to([128, 4, 2, 128])

        jobs = [(s, g, r) for s in range(2) for g in range(3) for r in range(B_DIL[g])]

        first_s1 = min(i for i, jb in enumerate(jobs) if jb[0] == 1)

        def init_ones(st_):
            v5 = st_["va"].ap.rearrange("p (b h two n) -> p b h two n", h=4, two=2, n=64)
            self.memset("pool", v5[:, :, :, 1, :], 1.0, [st_["va"].buf])

        def geom(s, g):
            d = B_DIL[g]
            L = S_LEN // d
            Lh = L // 2
            nb = Lh // 128
            m0 = s * Lh
            lo = max(0, m0 - 64)
            hi = min(L, m0 + Lh + 64)
            return d, L, Lh, nb, m0, lo, hi, lo - (m0 - 64), hi - (m0 - 64)

        def loads(ji):
            s, g, r = jobs[ji]
            st = sets[ji % NSET]
            d, L, Lh, nb, m0, lo, hi, ulo, uhi = geom(s, g)
            base = 5 + 4 * g
            qe, qo, kr, va, vstg = st["qe"], st["qo"], st["kr"], st["va"], st["vstg"]
            if s == 1 and ji - first_s1 < NSET:
                init_ones(st)
            krv = kr.ap.rearrange("p (c u) -> p c u", c=2)
            vav = va.ap.rearrange("p (b n) -> p b n", n=512)
            vsg = vstg.ap.rearrange("p (b n) -> p b n", n=256)
            qsrc = self.qkT.rearrange("(i p) (r m) -> p i r m", p=128, r=d)
            qev = qe.ap.rearrange("p (c m) -> p c m", c=2)
            qov = qo.ap.rearrange("p (c m) -> p c m", c=2)
            self.dma("sp", qev[0:64, :, 0:Lh], qsrc[0:64, base:base + 2, r, m0:m0 + Lh],
                     [self.qkT_buf[base], self.qkT_buf[base + 1]], [qe.buf], st["kqe"])
            self.dma("sp", qov[64:128, :, 0:Lh], qsrc[64:128, base:base + 2, r, m0:m0 + Lh],
                     [self.qkT_buf[base], self.qkT_buf[base + 1]], [qo.buf], st["kqo"])
            if ulo > 0:
                self.memset("pool", krv[:, :, 0:ulo], 0.0, [kr.buf])
            if uhi < Lh + 128:
                self.memset("pool", krv[:, :, uhi:Lh + 128], 0.0, [kr.buf])
            self.dma("sp", krv[:, :, ulo:uhi], qsrc[:, base + 2:base + 4, r, lo:hi],
                     [self.qkT_buf[base + 2], self.qkT_buf[base + 3]], [kr.buf], st["kkr"])
            vsrc = self.vtm.rearrange("(m r) n -> r m n", r=d)[r]
            c0 = 128 + 256 * g
            if ulo > 0:
                self.dma("sp", vsg[64:128, 0, :], vsrc[m0:m0 + 64, c0:c0 + 256], [self.vtm_buf], [vstg.buf], st["kva"])
                b_start = 1
            else:
                b_start = 0
            if uhi < Lh + 128:
                self.dma("sp", vsg[0:64, nb, :], vsrc[m0 + Lh - 64:m0 + Lh, c0:c0 + 256], [self.vtm_buf],
                         [vstg.buf], st["kva"])
                b_end = nb
            else:
                b_end = nb + 1
            mlo = m0 - 64 + 128 * b_start
            self.dma("sp", vsg[:, b_start:b_end, :],
                     vsrc[mlo:mlo + 128 * (b_end - b_start), c0:c0 + 256].rearrange("(b p) n -> p b n", p=128),
                     [self.vtm_buf], [vstg.buf], st["kva"])
            self.copy("pool", vav[:, 0:nb + 1, :].rearrange("p b (h two n) -> p b h two n", h=4, two=2)[:, :, :, 0, :],
                      vsg[:, 0:nb + 1, :].rearrange("p b (h n) -> p b h n", h=4), [vstg.buf], [va.buf])
            if ulo > 0:
                self.memset("pool", vav[0:64, 0, :], 0.0, [va.buf])
            if uhi < Lh + 128:
                self.memset("pool", vav[64:128, nb, :], 0.0, [va.buf])

        for st_ in sets:
            init_ones(st_)
        pend = []
        it = 0
        loads(0)
        for ji, (s, g, r) in enumerate(jobs):
            if g == 0 and r == 0:
                self.memset("pool", U.ap, 0.0, [U.buf])
            st = sets[ji % NSET]
            d, L, Lh, nb, m0, lo, hi, ulo, uhi = geom(s, g)
            qe, qo, kr, va = st["qe"], st["qo"], st["kr"], st["va"]
            krv = kr.ap.rearrange("p (c u) -> p c u", c=2)
            for mb in range(nb):
                sb = stb[it % 3]
                ptb = pt[it % NPT]
                ot = otb[it % 2]
                it += 1
                for h in range(4):
                    qsrc_t = qe if h % 2 == 0 else qo
                    qv = qsrc_t.ap.rearrange("p (c m) -> p c m", c=2)
                    for kk in range(2):
                        col = (h * 2 + kk) * 128
                        self.mm(sb.ap[:, col:col + 128], krv[:, h // 2, 128 * (mb + kk):128 * (mb + kk) + 128],
                                qv[:, h // 2, mb * 128:(mb + 1) * 128], True, True,
                                [kr.buf, qsrc_t.buf], [sb.buf])
                self.act(ptb.ap, sb.ap, AF.Exp, [sb.buf], [ptb.buf], scale=8.0)
                p4 = ptb.ap.rearrange("p (h k q) -> p h k q", h=4, k=2)
                self.tt("dve", p4, p4, mask4, ALU.mult, [ptb.buf, self.cst.buf], [ptb.buf])

                def stage2(ptb=ptb, ot=ot, va=va, mb=mb, d=d, r=r):
                    for h in range(4):
                        for kk in range(2):
                            col = (h * 2 + kk) * 128
                            o = (mb + kk) * 512 + (128 * h if h % 2 == 0 else 128 * h - 64)
                            self.mm(ot.ap[:, h * 128:(h + 1) * 128], va.ap[:, o:o + 128],
                                    ptb.ap[:, col:col + 128], kk == 0, kk == 1, [va.buf, ptb.buf], [ot.buf])
                    if d == 1:
                        uview = Uv[:, :, mb * 128:(mb + 1) * 128]
                    else:
                        uview = Uv.rearrange("p h (m r) -> p h r m", r=d)[:, :, r, mb * 128:(mb + 1) * 128]
                    self.tt("dve", uview, ot.ap.rearrange("p (h q) -> p h q", h=4), uview, ALU.add,
                            [ot.buf, U.buf], [U.buf])

                pend.append((ji, stage2))
                if len(pend) > 2:
                    pend.pop(0)[1]()
                if mb == 0 and ji + 1 < len(jobs):
                    while pend and pend[0][0] < ji:
                        pend.pop(0)[1]()
                    loads(ji + 1)
            if g == 2 and r == B_DIL[2] - 1:
                while pend:
                    pend.pop(0)[1]()
                for h in range(4):
                    rc = recs[h % 2]
                    if h % 2 == 0:
                        self.act(rc.ap[0:64, :], Uv[64:128, h, :], AF.Ln, [U.buf], [rc.buf])
                        self.act(rc.ap[0:64, :], rc.ap[0:64, :], AF.Exp, [rc.buf], [rc.buf], scale=-1.0)
                        self.tt("dve", bTv[0:64, h // 2, s * HS:(s + 1) * HS], Uv[0:64, h, :], rc.ap[0:64, :], ALU.mult,
                                [U.buf, rc.buf], [self.bT.buf])
                    else:
                        self.act(rc.ap[64:128, :], Uv[0:64, h, :], AF.Ln, [U.buf], [rc.buf])
                        self.act(rc.ap[64:128, :], rc.ap[64:128, :], AF.Exp, [rc.buf], [rc.buf], scale=-1.0)
                        self.tt("dve", bTv[64:128, h // 2, s * HS:(s + 1) * HS], Uv[64:128, h, :], rc.ap[64:128, :],
                                ALU.mult, [U.buf, rc.buf], [self.bT.buf])
        S.barrier(self.cast_keys)
        A.release()

    def va_lhs(self, vav, blk, h):
        o = blk * 512 + (128 * h if h % 2 == 0 else 128 * h - 64)
        return self.va_flat[:, o:o + 128]

    def phase_mix(self, l):
        A, S = self.A, self.S
        A.mark()
        aTv = self.aT.ap.rearrange("p (c t) -> p c t", c=4)
        bTv = self.bT.ap.rearrange("p (c t) -> p c t", c=2)
        wg = A.alloc(16 * 1024, BF16, "wg")
        self.dma("sp", wg.ap.rearrange("p (a k) -> p a k", a=16), self.Wb["g"][l % 2].rearrange("a p k -> p a k"),
                 [self.Wb_buf["g"][l % 2]], [wg.buf], self.key("k29"))
        wgv = wg.ap.rearrange("p (a c f) -> p a c f", a=16, c=8)
        wba = A.alloc(4096, BF16, "wba")
        self.dma("sp", wba.ap, self.Wb["ba"][l % 2], [self.Wb_buf["ba"][l % 2]], [wba.buf], self.key("k30"))
        wbav = wba.ap.rearrange("p (c n) -> p c n", c=4)
        wbb = A.alloc(2048, BF16, "wbb")
        self.dma("sp", wbb.ap, self.Wb["bb"][l % 2], [self.Wb_buf["bb"][l % 2]], [wbb.buf], self.key("k31"))
        wbbv = wbb.ap.rearrange("p (c n) -> p c n", c=2)
        wo = A.alloc(8192, BF16, "wo")
        self.dma("sp", wo.ap, self.Wb["o"][l % 2], [self.Wb_buf["o"][l % 2]], [wo.buf], self.key("k32"))
        wov = wo.ap.rearrange("p (c n) -> p c n", c=8)
        xts = [A.alloc(8 * 512, F32, "xtm%d" % i) for i in range(2)]
        kx = [self.key("r12_%d" % _i) for _i in range(2)]
        sq = A.alloc(8 * 512, BF16, "sqm")
        rstd = A.alloc(512, F32, "rstdm")
        tmp = [A.alloc(512, F32, "tmpm%d" % i) for i in range(2)]
        hts = [A.alloc(8 * 512, BF16, "htm%d" % i) for i in range(2)]
        gates = A.alloc(16 * 512, BF16, "gates")
        gv = gates.ap.rearrange("p (a t) -> p a t", a=16)
        mixed = A.alloc(8 * 512, BF16, "mixed")
        mv = mixed.ap.rearrange("p (c t) -> p c t", c=8)
        t1 = [A.alloc(512, F32, "t1%d" % i) for i in range(2)]
        t2 = [A.alloc(512, F32, "t2%d" % i) for i in range(2)]
        nb = 0

        def prep(j):
            xt = xts[j % 2]
            ht = hts[j % 2]
            self.dma("sp", xt.ap.rearrange("p (c t) -> p c t", c=8),
                     self.xT.rearrange("(c p) t -> p c t", p=128)[:, :, j * 512:(j + 1) * 512],
                     [self.xT_buf[j]], [xt.buf], kx[j % 2])
            self.make_h(xt, ht.ap.rearrange("p (c t) -> p c t", c=8), ht.buf, self.A1, 0, sq, rstd, tmp, self.bank[7])

        prep(0)
        for j in range(8):
            xt = xts[j % 2]
            ht = hts[j % 2]
            tsl = slice(j * 512, (j + 1) * 512)
            hv = ht.ap.rearrange("p (c t) -> p c t", c=8)
            for a in range(16):
                bk = self.bank[nb % 6]
                nb += 1
                for c in range(8):
                    self.mm(bk.ap, wgv[:, a, c, :], hv[:, c, :], c == 0, c == 7, [wg.buf, ht.buf], [bk.buf])
                self.act(gv[:, a, :], bk.ap, AF.Sigmoid, [bk.buf], [gates.buf])
            for oc in range(8):
                bA = self.bank[nb % 6]
                bB = self.bank[(nb + 1) % 6]
                nb += 2
                for c in range(4):
                    self.mm(bA.ap, wbav[:, c, oc * 128:(oc + 1) * 128], aTv[:, c, tsl], c == 0, c == 3,
                            [wba.buf, self.aT.buf], [bA.buf])
                for c in range(2):
                    self.mm(bB.ap, wbbv[:, c, oc * 128:(oc + 1) * 128], bTv[:, c, tsl], c == 0, c == 1,
                            [wbb.buf, self.bT.buf], [bB.buf])
                a1, a2 = t1[oc % 2], t2[oc % 2]
                self.tt("dve", a1.ap, bA.ap, gv[:, oc, :], ALU.mult, [bA.buf, gates.buf], [a1.buf])
                self.tt("dve", a2.ap, bB.ap, gv[:, 8 + oc, :], ALU.mult, [bB.buf, gates.buf], [a2.buf])
                self.tt("pool", mv[:, oc, :], a1.ap, a2.ap, ALU.add, [a1.buf, a2.buf], [mixed.buf])
            if j + 1 < 8:
                prep(j + 1)
            xv = xt.ap.rearrange("p (c t) -> p c t", c=8)
            for oc in range(8):
                bk = self.bank[nb % 6]
                nb += 1
                for c in range(8):
                    self.mm(bk.ap, wov[:, c, oc * 128:(oc + 1) * 128], mv[:, c, :], c == 0, c == 7,
                            [wo.buf, mixed.buf], [bk.buf])
                self.stt("dve", xv[:, oc, :], bk.ap, self.mod.ap[:, 16 + oc:17 + oc], xv[:, oc, :], ALU.mult, ALU.add,
                         [bk.buf, self.mod.buf, xt.buf], [xt.buf])
            self.dma("sp", self.xT.rearrange("(c p) t -> p c t", p=128)[:, :, tsl],
                     xt.ap.rearrange("p (c t) -> p c t", c=8), [xt.buf], [self.xT_buf[j]], kx[j % 2])
        S.barrier(self.cast_keys)
        A.release()
        A.release()

    def phase_ffn(self, l):
        A, S = self.A, self.S
        A.mark()
        TG = 1024
        NSUB = TG // 512
        NG = S_LEN // TG
        xts = [A.alloc(8 * 512, F32, "xtf%d" % i) for i in range(NSUB)]
        kx = [self.key("r13_%d" % _i) for _i in range(NSUB)]
        xpre = A.alloc(8 * 512, F32, "xpre")
        kxp = self.key("xpre")
        sq = A.alloc(8 * 512, BF16, "sqf")
        rstd = A.alloc(512, F32, "rstdf")
        tmp = [A.alloc(512, F32, "tmpf%d" % i) for i in range(2)]
        hts = [A.alloc(8 * TG, BF16, "htf%d" % i) for i in range(2)]
        uT = A.alloc(32 * TG, BF16, "uT")
        uv = uT.ap.rearrange("p (k t) -> p k t", k=32)
        w1 = [A.alloc(4 * 1024, BF16, "w1_%d" % i) for i in range(2)]
        k1 = [self.key("r14_%d" % _i) for _i in range(2)]
        w2 = [A.alloc(4096, BF16, "w2_%d" % i) for i in range(2)]
        k2 = [self.key("r15_%d" % _i) for _i in range(2)]
        rl = [A.alloc(512, F32, "rl%d" % i) for i in range(3)]
        nb = 0
        nr = 0
        xTv = self.xT.rearrange("(c p) t -> p c t", p=128)

        def prefetch_stages(tg, sub):
            ht = hts[tg % 2]
            hv_ = ht.ap.rearrange("p (c t) -> p c t", c=8)
            j = tg * NSUB + sub
            sA, sB, sC = self.make_h_stages(xpre, hv_[:, :, sub * 512:(sub + 1) * 512], ht.buf, self.A2, 24, sq, rstd,
                                            tmp, self.bank[7])

            def sA2():
                self.dma("sp", xpre.ap.rearrange("p (c t) -> p c t", c=8), xTv[:, :, j * 512:(j + 1) * 512],
                         [self.xT_buf[j]], [xpre.buf], kxp)
                sA()
            return sA2, sB, sC

        for sub in range(NSUB):
            for st in prefetch_stages(0, sub):
                st()
        for tg in range(NG):
            ht = hts[tg % 2]
            hv = ht.ap.rearrange("p (c t) -> p c t", c=8)
            for sub in range(NSUB):
                j = tg * NSUB + sub
                xt = xts[sub]
                self.dma("sp", xt.ap.rearrange("p (c t) -> p c t", c=8), xTv[:, :, j * 512:(j + 1) * 512],
                         [self.xT_buf[j]], [xt.buf], kx[sub])
            sched = {}
            if tg + 1 < NG:
                for sub in range(NSUB):
                    sA, sB, sC = prefetch_stages(tg + 1, sub)
                    q0 = sub * 4
                    sched.setdefault((q0, "pre"), []).append(sA)
                    sched.setdefault((q0 + 1, "post"), []).append(sB)
                    sched.setdefault((q0 + 2, "post"), []).append(sC)
            for q in range(8):
                for f_ in sched.get((q, "pre"), []):
                    f_()
                w = w1[q % 2]
                self.dma("sp", w.ap.rearrange("p (a k) -> p a k", a=4),
                         self.Wb["f1"][l % 2][q * 4:(q + 1) * 4].rearrange("a p k -> p a k"),
                         [self.Wb_buf["f1"][l % 2]], [w.buf], k1[q % 2])
                wv_ = w.ap.rearrange("p (a c f) -> p a c f", a=4, c=8)
                for a in range(4):
                    hc = q * 4 + a
                    for sub in range(NSUB):
                        bk = self.bank[nb % 7]
                        nb += 1
                        for c in range(8):
                            self.mm(bk.ap, wv_[:, a, c, :], hv[:, c, sub * 512:(sub + 1) * 512], c == 0, c == 7,
                                    [w.buf, ht.buf], [bk.buf])
                        r = rl[nr % 3]
                        nr += 1
                        self.act(r.ap, bk.ap, AF.Relu, [bk.buf], [r.buf])
                        self.tt("pool", uv[:, hc, sub * 512:(sub + 1) * 512], r.ap, r.ap, ALU.mult, [r.buf], [uT.buf])
                for f_ in sched.get((q, "post"), []):
                    f_()
            for oc in range(8):
                w = w2[oc % 2]
                self.dma("sp", w.ap, self.Wb["f2"][l % 2][oc], [self.Wb_buf["f2"][l % 2]], [w.buf], k2[oc % 2])
                wv_ = w.ap.rearrange("p (k f) -> p k f", k=32)
                for sub in range(NSUB):
                    xt = xts[sub]
                    xv = xt.ap.rearrange("p (c t) -> p c t", c=8)
                    bk = self.bank[nb % 7]
                    nb += 1
                    for kc in range(32):
                        self.mm(bk.ap, wv_[:, kc, :], uv[:, kc, sub * 512:(sub + 1) * 512], kc == 0, kc == 31,
                                [w.buf, uT.buf], [bk.buf])
                    self.stt("dve", xv[:, oc, :], bk.ap, self.mod.ap[:, 40 + oc:41 + oc], xv[:, oc, :], ALU.mult, ALU.add,
                             [bk.buf, self.mod.buf, xt.buf], [xt.buf])
            for sub in range(NSUB):
                j = tg * NSUB + sub
                xt = xts[sub]
                self.dma("sp", xTv[:, :, j * 512:(j + 1) * 512],
                         xt.ap.rearrange("p (c t) -> p c t", c=8), [xt.buf], [self.xT_buf[j]], kx[sub])
        S.barrier(self.cast_keys)
        A.release()


def _fm(v, n):
    return np.ascontiguousarray(v.reshape(v.shape[:-1] + (n, 128)).swapaxes(-1, -2))


def _lhs_chunks(W):
    K, N = W.shape
    return np.ascontiguousarray(W.reshape(K // 128, 128, N // 128, 128).transpose(2, 1, 0, 3)).reshape(N // 128, 128, K)


def _rhs_rows(W):
    K, N = W.shape
    return np.ascontiguousarray(W.reshape(K // 128, 128, N).transpose(1, 0, 2)).reshape(128, (K // 128) * N)


_PERM_AX = np.concatenate([np.arange(0, 16), np.arange(32, 48), np.arange(16, 32), np.arange(48, 64)])
_SWAP = np.concatenate([np.arange(32, 64), np.arange(0, 32)])


def _consts():
    f = np.float32
    ident = np.eye(128, dtype=f)
    ones = np.ones((128, 128), f)
    e64 = np.kron(np.eye(2, dtype=f), np.ones((64, 64), f))
    i = np.arange(128)[:, None]
    j = np.arange(128)[None, :]
    mask0 = (i >= j).astype(f)
    mask1 = (i <= j).astype(f)
    cst = np.ascontiguousarray(np.stack([ones, e64, mask0, mask1], axis=1))
    theta = np.float32(10000.0)
    pos = np.arange(S_LEN)
    inv32 = (theta ** (-np.arange(0, 64, 2, dtype=f) / f(64))).astype(f)
    ang = pos.astype(f)[:, None] * inv32[None, :]
    cs, sn = np.cos(ang).astype(f), np.sin(ang).astype(f)
    c64 = np.concatenate([cs, cs], axis=1)
    s64 = np.concatenate([-sn, sn], axis=1)
    C_seq = np.ascontiguousarray(np.tile(c64, (1, 2)).T)
    S_seq = np.ascontiguousarray(np.tile(s64, (1, 2)).T)
    inv16 = (theta ** (-np.arange(0, 32, 2, dtype=f) / f(32))).astype(f)
    row = (pos // 64).astype(f)
    col = (pos % 64).astype(f)
    ar = row[:, None] * inv16[None, :]
    ac = col[:, None] * inv16[None, :]
    c32 = np.concatenate([np.cos(ar), np.cos(ac)], axis=1).astype(f)
    s32 = np.concatenate([np.sin(ar), np.sin(ac)], axis=1).astype(f)
    c64a = np.concatenate([c32, c32], axis=1)
    s64a = np.concatenate([-s32, s32], axis=1)
    C_ax = np.ascontiguousarray(np.tile(c64a, (1, 2)).T)
    S_ax = np.ascontiguousarray(np.tile(s64a, (1, 2)).T)
    tabs = np.ascontiguousarray(np.stack([C_seq, S_seq, C_ax, S_ax]))
    return ident, cst, tabs


def _prep_weights(inp, layers):
    f = np.float32
    out = {k: [] for k in ("b_ada", "g_mix", "g_mlp", "g_qk", "w_ada", "w_qk", "w_qks", "w_v", "w_g", "w_ba", "w_bb",
                           "w_o", "w_f1", "w_f2")}
    for l in layers:
        w_in = np.asarray(inp["w_in"][l], f)
        cols = {}
        off = 0
        names = ["qa", "ka", "va"] + [n + str(g) for g in range(3) for n in ("qb", "kb", "vb")] + ["ga", "gb"]
        sizes = [512, 128, 128] + [256] * 9 + [1024, 1024]
        for n, sz in zip(names, sizes):
            cols[n] = w_in[:, off:off + sz]
            off += sz

        def perm_heads(Wc, perm):
            K, N = Wc.shape
            return Wc.reshape(K, N // 64, 64)[:, :, perm].reshape(K, N)

        qn_a = np.asarray(inp["q_norm_a"][l], f)
        kn_a = np.asarray(inp["k_norm_a"][l], f)
        qn_b = np.asarray(inp["q_norm_b"][l], f)
        kn_b = np.asarray(inp["k_norm_b"][l], f)
        raw_cols, sw_cols, g_raw, g_sw = [], [], [], []

        def add(Wc, gain, axial):
            Wp = perm_heads(Wc, _PERM_AX) if axial else Wc
            gp = gain[_PERM_AX] if axial else gain
            raw_cols.append(Wp)
            sw_cols.append(perm_heads(Wp, _SWAP))
            nh = Wc.shape[1] // 64
            g_raw.append(np.tile(gp, nh))
            g_sw.append(np.tile(gp[_SWAP], nh))

        add(cols["qa"], qn_a, True)
        add(cols["ka"], kn_a, True)
        for g in range(3):
            add(cols["qb%d" % g], qn_b[g], False)
            add(cols["kb%d" % g], kn_b[g], False)
        Wqk = np.concatenate(raw_cols, axis=1)
        Wqks = np.concatenate(sw_cols, axis=1)
        gr = np.concatenate(g_raw)
        gs = np.concatenate(g_sw)
        out["g_qk"].append(np.concatenate([_fm(gr, NQK), _fm(gs, NQK)], axis=1))
        out["w_qk"].append(_lhs_chunks(Wqk))
        out["w_qks"].append(_lhs_chunks(Wqks))
        Wv = np.concatenate([cols["va"], cols["vb0"], cols["vb1"], cols["vb2"]], axis=1)
        out["w_v"].append(_rhs_rows(Wv))
        out["w_g"].append(_lhs_chunks(np.concatenate([cols["ga"], cols["gb"]], axis=1)))
        out["w_ba"].append(_rhs_rows(np.asarray(inp["w_branch_a"][l], f)))
        out["w_bb"].append(_rhs_rows(np.asarray(inp["w_branch_b"][l], f)))
        out["w_o"].append(_rhs_rows(np.asarray(inp["w_out"][l], f)))
        out["w_f1"].append(_lhs_chunks(np.asarray(inp["w_ff1"][l], f)))
        out["w_f2"].append(_lhs_chunks(np.asarray(inp["w_ff2"][l], f)))
        out["w_ada"].append(_lhs_chunks(np.asarray(inp["w_ada"][l], f)))
        out["b_ada"].append(_fm(np.asarray(inp["b_ada"][l], f), 48))
        out["g_mix"].append(_fm(np.asarray(inp["g_mix"][l], f), 8))
        out["g_mlp"].append(_fm(np.asarray(inp["g_mlp"][l], f), 8))
    return {k: np.ascontiguousarray(np.stack(v)) for k, v in out.items()}


_PROG_CACHE = {}


def _get_prog(NL, debug=False):
    key = (NL, debug)
    if key not in _PROG_CACHE:
        p = Prog(NL, debug)
        p.build()
        _PROG_CACHE[key] = p
    return _PROG_CACHE[key]


def run_layers(x, c, inp, layers, cores=NCORES, debug=False):
    prog = _get_prog(len(layers), debug)
    ident, cst, tabs = _consts()
    wts = _prep_weights(inp, layers)
    in_maps = []
    for b in range(cores):
        m = {"x": np.ascontiguousarray(x[b]), "c_fm": _fm(np.asarray(c[b], np.float32), 8), "ident": ident,
             "cst_bf": cst, "tabs": tabs}
        m.update(wts)
        in_maps.append(m)
    res = run_bass_kernel_spmd(prog.nc, in_maps, core_ids=list(range(cores)))
    if debug:
        return res
    return np.stack([np.asarray(r["out"]) for r in res.results])


FUSED = True


def kernel(x, c, w_ada, b_ada, g_mix, g_mlp, w_in, q_norm_a, k_norm_a, q_norm_b, k_norm_b,
           w_branch_a, w_branch_b, w_out, w_ff1, w_ff2):
    inp = dict(w_ada=w_ada, b_ada=b_ada, g_mix=g_mix, g_mlp=g_mlp, w_in=w_in, q_norm_a=q_norm_a, k_norm_a=k_norm_a,
               q_norm_b=q_norm_b, k_norm_b=k_norm_b, w_branch_a=w_branch_a, w_branch_b=w_branch_b, w_out=w_out,
               w_ff1=w_ff1, w_ff2=w_ff2)
    x = np.asarray(x, np.float32)
    c = np.asarray(c, np.float32)
    if FUSED:
        return run_layers(x, c, inp, list(range(DEPTH))).astype(np.float32)
    for l in range(DEPTH):
        x = run_layers(x, c, inp, [l])
    return x.astype(np.float32)
```

```python
import contextlib
import numpy as np
import concourse.bass as bass
import concourse.mybir as mybir
from concourse.bass_utils import run_bass_kernel_spmd

F32 = mybir.dt.float32
BF16 = mybir.dt.bfloat16
AF = mybir.ActivationFunctionType
ALU = mybir.AluOpType

D = 1024
S_LEN = 4096
DEPTH = 4
NCORES = 8
EPS = 1e-6
NQK = 17
B_DIL = (1, 4, 16)

ENGS = ("pe", "act", "dve", "pool", "sp")
NE = len(ENGS)
EIDX = {e: i for i, e in enumerate(ENGS)}


class Buf:
    __slots__ = ("name", "last_w", "readers")

    def __init__(self, name=""):
        self.name = name
        self.last_w = None
        self.readers = []


class Op:
    __slots__ = ("eng", "idx", "fn", "deps", "dma_sem", "dma_val", "signalled", "count", "waits", "clock")

    def __init__(self, eng, idx, fn):
        self.eng = eng
        self.idx = idx
        self.fn = fn
        self.deps = []
        self.dma_sem = None
        self.dma_val = 0
        self.signalled = False
        self.count = 0
        self.waits = None
        self.clock = None


class Sched:
    def __init__(self):
        self.ops = {e: [] for e in ENGS}
        self.all_ops = []
        self.dma_sem_count = {}
        self.dma_last = {}

    def op(self, eng, fn, reads=(), writes=(), dma=None):
        o = Op(eng, len(self.ops[eng]), fn)
        deps = []
        for b in reads:
            if b.last_w is not None:
                deps.append(b.last_w)
        for b in writes:
            if b.last_w is not None:
                deps.append(b.last_w)
            deps.extend(b.readers)
        o.deps = deps
        for b in reads:
            b.readers.append(o)
        for b in writes:
            b.last_w = o
            b.readers = []
        if dma is not None:
            v = self.dma_sem_count.get(dma, 0) + 16
            self.dma_sem_count[dma] = v
            o.dma_sem = dma
            o.dma_val = v
            self.dma_last[dma] = o
        self.ops[eng].append(o)
        self.all_ops.append(o)
        return o

    def barrier(self, exclude=()):
        last = [self.ops[e][-1] for e in ENGS if self.ops[e]] + \
               [o for k, o in self.dma_last.items() if k not in exclude]
        for e in ENGS:
            o = self.op(e, lambda h: h.nop())
            o.deps = list(last)

    def resolve(self):
        clock = {e: [-1] * NE for e in ENGS}
        dma_seen = {e: {} for e in ENGS}
        for o in self.all_ops:
            e = o.eng
            ck = clock[e]
            ei = EIDX[e]
            if e == "pe":
                ck[ei] = o.idx - 1
            ewait = {}
            dwait = {}
            seen = dma_seen[e]
            for d in o.deps:
                if d.dma_sem is not None:
                    if d.dma_val > seen.get(d.dma_sem, 0) and d.dma_val > dwait.get(d.dma_sem, 0):
                        dwait[d.dma_sem] = d.dma_val
                else:
                    di = EIDX[d.eng]
                    if ck[di] >= d.idx:
                        continue
                    w = ewait.get(di)
                    if w is None or d.idx > w.idx:
                        ewait[di] = d
            wl = []
            if dwait:
                for d in o.deps:
                    if d.dma_sem is not None and dwait.get(d.dma_sem, 0) >= d.dma_val:
                        dc = d.clock
                        qi = EIDX[d.eng]
                        for j in range(NE):
                            if j != qi and dc[j] > ck[j]:
                                ck[j] = dc[j]
                for k, v in dwait.items():
                    wl.append((0, k, v))
                    seen[k] = v
            for di, d in ewait.items():
                if ck[di] >= d.idx:
                    continue
                d.signalled = True
                wl.append((1, d.eng, d))
                dc = d.clock
                for j in range(NE):
                    if dc[j] > ck[j]:
                        ck[j] = dc[j]
            o.waits = wl
            c = list(ck)
            if o.dma_sem is None and o.idx > c[ei]:
                c[ei] = o.idx
            o.clock = c
            o.deps = None
        for e in ENGS:
            n = 0
            for o in self.ops[e]:
                if o.signalled:
                    n += 1
                o.count = n

    def emit_engine(self, e, handle, eng_sems, dma_sems, final_wait_all=False):
        for o in self.ops[e]:
            for w in o.waits:
                if w[0] == 0:
                    handle.wait_ge(dma_sems[w[1]], w[2])
                else:
                    handle.wait_ge(eng_sems[w[1]], w[2].count)
            ins = o.fn(handle)
            if o.dma_sem is not None:
                ins.then_inc(dma_sems[o.dma_sem], 16)
            elif o.signalled:
                ins.then_inc(eng_sems[e], 1)
        if final_wait_all:
            for k, v in self.dma_sem_count.items():
                handle.wait_ge(dma_sems[k], v)


class T:
    __slots__ = ("ap", "buf")

    def __init__(self, ap, buf):
        self.ap = ap
        self.buf = buf

    def v(self, pattern, **kw):
        return self.ap.rearrange(pattern, **kw)


class Arena:
    def __init__(self, base_ap, n_words):
        self.base = base_ap
        self.n = n_words
        self.off = 0
        self.marks = []

    def alloc(self, n_elems, dtype=F32, name=""):
        nbytes = n_elems * (4 if dtype == F32 else 2)
        nw = (nbytes + 3) // 4
        nw = (nw + 15) // 16 * 16
        assert self.off + nw <= self.n, f"arena overflow {name}: {self.off}+{nw}>{self.n}"
        ap = self.base[:, self.off:self.off + nw]
        self.off += nw
        if dtype != F32:
            ap = ap.bitcast(dtype)
        ap = ap[:, 0:n_elems]
        return T(ap, Buf(name))

    def mark(self):
        self.marks.append(self.off)

    def release(self):
        self.off = self.marks.pop()


class Prog:
    def __init__(self, NL, debug=False):
        self.NL = NL
        self.debug = debug
        self.nc = bass.Bass("TRN2", target_bir_lowering=False)
        self.S = Sched()
        self.nkeys = 0
        self.named_keys = {}
        self._auto = 0

    def key(self, name=None):
        if name is None:
            raise ValueError("key needs a name")
        if name not in self.named_keys:
            self.named_keys[name] = self.nkeys
            self.nkeys += 1
        return self.named_keys[name]

    def dma(self, eng, out, in_, reads, writes, key):
        return self.S.op(eng, lambda h: h.dma_start(out=out, in_=in_), reads, writes, dma=key)

    def mm(self, out, lhsT, rhs, start, stop, reads, writes):
        return self.S.op("pe", lambda h: h.matmul(out, lhsT=lhsT, rhs=rhs, start=start, stop=stop), reads, writes)

    def act(self, out, in_, func, reads, writes, scale=1.0, bias=0.0):
        return self.S.op("act", lambda h: h.activation(out=out, in_=in_, func=func, bias=bias, scale=scale),
                         reads, writes)

    def tt(self, eng, out, in0, in1, op, reads, writes):
        return self.S.op(eng, lambda h: h.tensor_tensor(out=out, in0=in0, in1=in1, op=op), reads, writes)

    def stt(self, eng, out, in0, scalar, in1, op0, op1, reads, writes):
        return self.S.op(eng, lambda h: h.scalar_tensor_tensor(out=out, in0=in0, scalar=scalar, in1=in1,
                                                                op0=op0, op1=op1), reads, writes)

    def ts(self, eng, out, in0, s1, s2, op0, op1, reads, writes):
        return self.S.op(eng, lambda h: h.tensor_scalar(out=out, in0=in0, scalar1=s1, scalar2=s2, op0=op0, op1=op1),
                         reads, writes)

    def copy(self, eng, out, in_, reads, writes):
        if eng == "act":
            return self.act(out, in_, AF.Copy, reads, writes)
        return self.S.op(eng, lambda h: h.tensor_copy(out=out, in_=in_), reads, writes)

    def memset(self, eng, ap, val, writes):
        return self.S.op(eng, lambda h: h.memset(ap, val), (), writes)

    def recip(self, out, in_, reads, writes):
        return self.S.op("dve", lambda h: h.reciprocal(out=out, in_=in_), reads, writes)

    def declare(self):
        nc, NL = self.nc, self.NL

        def inp(name, shape, dt=F32):
            return nc.dram_tensor(name, list(shape), dt, kind="ExternalInput").ap()

        def scr(name, shape, dt):
            kind = "ExternalOutput" if (self.debug and name in ("qkT", "vtm", "xT", "dbg")) else "Internal"
            return nc.dram_tensor(name, list(shape), dt, kind=kind).ap()

        self.x_in = inp("x", [S_LEN, D])
        self.out = nc.dram_tensor("out", [S_LEN, D], F32, kind="ExternalOutput").ap()
        self.c_in = inp("c_fm", [128, 8])
        self.ident_in = inp("ident", [128, 128])
        self.cst_bf_in = inp("cst_bf", [128, 5, 128])
        self.tabs_in = inp("tabs", [4, 128, S_LEN])
        self.b_ada = inp("b_ada", [NL, 128, 48])
        self.g_mix = inp("g_mix", [NL, 128, 8])
        self.g_mlp = inp("g_mlp", [NL, 128, 8])
        self.g_qk = inp("g_qk", [NL, 128, 2 * NQK])
        W = {}
        W["ada"] = (inp("w_ada", [NL, 48, 128, 8 * 128]), [48, 128, 1024])
        W["qk"] = (inp("w_qk", [NL, NQK, 128, 1024]), [NQK, 128, 1024])
        W["v"] = (inp("w_v", [NL, 128, 8 * 896]), [128, 8 * 896])
        W["g"] = (inp("w_g", [NL, 16, 128, 1024]), [16, 128, 1024])
        W["ba"] = (inp("w_ba", [NL, 128, 4 * 1024]), [128, 4096])
        W["bb"] = (inp("w_bb", [NL, 128, 2 * 1024]), [128, 2048])
        W["o"] = (inp("w_o", [NL, 128, 8 * 1024]), [128, 8192])
        W["f1"] = (inp("w_f1", [NL, 32, 128, 1024]), [32, 128, 1024])
        W["f2"] = (inp("w_f2", [NL, 8, 128, 4096]), [8, 128, 4096])
        self.W = W
        self.Wb = {}
        self.Wb_buf = {}
        self.Wb_key = {}
        for k, (ap, shp) in W.items():
            self.Wb[k] = [scr("wb%d_%s" % (i, k), shp, BF16) for i in range(2)]
            self.Wb_buf[k] = [Buf("wb%d_%s" % (i, k)) for i in range(2)]
            self.Wb_key[k] = [self.key("wb%d_%s" % (i, k)) for i in range(2)]
        self.cast_keys = set(k for ks in self.Wb_key.values() for k in ks)
        self.xT = scr("xT", [D, S_LEN], F32)
        self.xT_buf = [Buf("xT%d" % j) for j in range(8)]
        self.qkT = scr("qkT", [NQK * 128, S_LEN], BF16)
        self.qkT_buf = [Buf("qkT%d" % i) for i in range(NQK)]
        self.vtm = scr("vtm", [S_LEN, 896], BF16)
        self.vtm_buf = Buf("vtm")

    def cast_plan(self, l, names, step=2048):
        plan = []
        for k in names:
            src, shp = self.W[k]
            n = int(np.prod(shp))
            sap = src[l]
            dap = self.Wb[k][l % 2]
            if len(shp) == 3:
                sap = sap.rearrange("a p (r b) -> (a p r) b", b=1024)
                dap = dap.rearrange("a p (r b) -> (a p r) b", b=1024)
            else:
                sap = sap.rearrange("p (r b) -> (p r) b", b=1024)
                dap = dap.rearrange("p (r b) -> (p r) b", b=1024)
            rows = n // 1024
            for r0 in range(0, rows, step):
                r1 = min(rows, r0 + step)
                plan.append((dap[r0:r1, :], sap[r0:r1, :], self.Wb_buf[k][l % 2], self.Wb_key[k][l % 2]))
        return plan

    def cast_emit(self, item, pace=()):
        dap, sap, buf, key = item
        self.dma("pool", dap, sap, pace, [buf], key)

    def cast_weights(self, l, names):
        for item in self.cast_plan(l, names, step=8192):
            self.cast_emit(item)

    def build(self):
        nc = self.nc
        self.declare()
        NW = 52992
        with (nc.sbuf_tensor("arena", [128, NW], F32) as arena_t,
              nc.psum_tensor("ps", [128, 4096], F32) as ps_t):
            self.A = Arena(arena_t, NW)
            self.ps = ps_t
            self.bank = [T(ps_t[:, b * 512:(b + 1) * 512], Buf("bank%d" % b)) for b in range(8)]
            self.body()
            self.S.resolve()
            with contextlib.ExitStack() as es:
                eng_sems = {e: es.enter_context(nc.semaphore("s_" + e)) for e in ENGS}
                dma_sems = {k: es.enter_context(nc.semaphore("d%d" % k)) for k in self.S.dma_sem_count}
                block = es.enter_context(nc.Block())
                S = self.S

                @block.tensor
                def _(h):
                    S.emit_engine("pe", h, eng_sems, dma_sems)

                @block.scalar
                def _(h):
                    S.emit_engine("act", h, eng_sems, dma_sems)

                @block.vector
                def _(h):
                    S.emit_engine("dve", h, eng_sems, dma_sems)

                @block.gpsimd
                def _(h):
                    S.emit_engine("pool", h, eng_sems, dma_sems)

                @block.sync
                def _(h):
                    S.emit_engine("sp", h, eng_sems, dma_sems, final_wait_all=True)
        return nc

    def load_consts(self):
        A = self.A
        self.ident = A.alloc(128, F32, "ident")
        self.dma("sp", self.ident.ap, self.ident_in, (), [self.ident.buf], self.key("k17"))
        cst = A.alloc(5 * 128, BF16, "cst")
        self.dma("pool", cst.ap.rearrange("p (a b) -> p a b", a=5), self.cst_bf_in, (), [cst.buf], self.key("k18"))
        self.perm = T(cst.ap[:, 512:640], cst.buf)
        self.cst = cst
        self.ones = T(cst.ap[:, 0:128], cst.buf)
        self.e64 = T(cst.ap[:, 128:256], cst.buf)
        self.masks = T(cst.ap[:, 256:512], cst.buf)
        self.cfm = A.alloc(8, F32, "cfm")
        self.dma("sp", self.cfm.ap, self.c_in, (), [self.cfm.buf], self.key("k19"))
        self.cact = A.alloc(8, BF16, "cact")
        self.act(self.cact.ap, self.cfm.ap, AF.Silu, [self.cfm.buf], [self.cact.buf])
        self.mod = A.alloc(48, F32, "mod")
        self.A1 = A.alloc(8, F32, "A1")
        self.A2 = A.alloc(8, F32, "A2")
        self.gq = A.alloc(2 * NQK, F32, "gq")
        self.small_key = self.key("k20")

    def body(self):
        A = self.A
        self.load_consts()
        self.phase_in()
        WN = ["ada", "qk", "v", "g", "ba", "bb", "o", "f1", "f2"]
        self.cast_weights(0, WN)
        for l in range(self.NL):
            self.phase_ada(l)
            self.phase_proj(l)
            self.pending_cast = self.cast_plan(l + 1, WN) if l + 1 < self.NL else []
            self.phase_gqa(l)
            while self.pending_cast:
                self.cast_emit(self.pending_cast.pop(0))
            self.phase_dil(l)
            self.phase_mix(l)
            self.phase_ffn(l)
        self.phase_out()

    def phase_in(self):
        A, S = self.A, self.S
        A.mark()
        xin = [A.alloc(1024, F32, "xin%d" % i) for i in range(2)]
        xo = [A.alloc(8 * 512, F32, "xo%d" % i) for i in range(2)]
        kin = [self.key("r1_%d" % _i) for _i in range(2)]
        ko = [self.key("r2_%d" % _i) for _i in range(2)]
        for j in range(8):
            o = xo[j % 2]
            for tb in range(4):
                blk = j * 4 + tb
                xi = xin[blk % 2]
                self.dma("sp", xi.ap, self.x_in[blk * 128:(blk + 1) * 128, :], (), [xi.buf], kin[blk % 2])
                for half in range(2):
                    bk = self.bank[(blk * 2 + half) % 8]
                    for cc in range(4):
                        c = half * 4 + cc
                        self.mm(bk.ap[:, cc * 128:(cc + 1) * 128], xi.ap[:, c * 128:(c + 1) * 128], self.ident.ap,
                                True, True, [xi.buf, self.ident.buf], [bk.buf])
                    dst = o.ap.rearrange("p (c t) -> p c t", c=8)[:, half * 4:(half + 1) * 4, tb * 128:(tb + 1) * 128]
                    src = bk.ap.rearrange("p (c t) -> p c t", c=4)
                    self.copy("dve" if half == 0 else "act", dst, src, [bk.buf], [o.buf])
            self.dma("sp", self.xT.rearrange("(c p) t -> p c t", p=128)[:, :, j * 512:(j + 1) * 512],
                     o.ap.rearrange("p (c t) -> p c t", c=8), [o.buf], [self.xT_buf[j]], ko[j % 2])
        S.barrier(self.cast_keys)
        A.release()

    def phase_out(self):
        A, S = self.A, self.S
        A.mark()
        xi2 = [A.alloc(8 * 512, F32, "xo_in%d" % i) for i in range(2)]
        xo2 = [A.alloc(1024, F32, "xo_out%d" % i) for i in range(2)]
        kin = [self.key("r3_%d" % _i) for _i in range(2)]
        ko = [self.key("r4_%d" % _i) for _i in range(2)]
        for j in range(8):
            xi = xi2[j % 2]
            self.dma("sp", xi.ap.rearrange("p (c t) -> p c t", c=8),
                     self.xT.rearrange("(c p) t -> p c t", p=128)[:, :, j * 512:(j + 1) * 512],
                     [self.xT_buf[j]], [xi.buf], kin[j % 2])
            xv = xi.ap.rearrange("p (c t) -> p c t", c=8)
            for tb in range(4):
                blk = j * 4 + tb
                o = xo2[blk % 2]
                for half in range(2):
                    bk = self.bank[(blk * 2 + half) % 8]
                    for cc in range(4):
                        c = half * 4 + cc
                        self.mm(bk.ap[:, cc * 128:(cc + 1) * 128], xv[:, c, tb * 128:(tb + 1) * 128], self.ident.ap,
                                True, True, [xi.buf, self.ident.buf], [bk.buf])
                    self.copy("dve" if half == 0 else "act", o.ap[:, half * 512:(half + 1) * 512], bk.ap,
                              [bk.buf], [o.buf])
                self.dma("sp", self.out[blk * 128:(blk + 1) * 128, :], o.ap, [o.buf], (), ko[blk % 2])
        A.release()

    def phase_ada(self, l):
        A, S = self.A, self.S
        A.mark()
        wt = [A.alloc(8 * 1024, BF16, "wada%d" % i) for i in range(2)]
        kw = [self.key("r5_%d" % _i) for _i in range(2)]
        bfm = A.alloc(48, F32, "bfm")
        gm = A.alloc(16, F32, "gm")
        bfm.buf = gm.buf = self.gq.buf
        self.dma("sp", bfm.ap, self.b_ada[l], (), [bfm.buf], self.small_key)
        self.dma("sp", gm.ap[:, 0:8], self.g_mix[l], (), [gm.buf], self.small_key)
        self.dma("sp", gm.ap[:, 8:16], self.g_mlp[l], (), [gm.buf], self.small_key)
        self.dma("sp", self.gq.ap, self.g_qk[l], (), [self.gq.buf], self.small_key)
        bk = self.bank[0]
        for piece in range(6):
            w = wt[piece % 2]
            self.dma("sp", w.ap.rearrange("p (f k) -> p f k", f=8),
                     self.Wb["ada"][l % 2][piece * 8:(piece + 1) * 8].rearrange("f p k -> p f k"),
                     [self.Wb_buf["ada"][l % 2]], [w.buf], kw[piece % 2])
            wv = w.ap.rearrange("p (f c k) -> p f c k", f=8, c=8)
            for f in range(8):
                col = piece * 8 + f
                for c in range(8):
                    self.mm(bk.ap[:, col:col + 1], wv[:, f, c, :], self.cact.ap[:, c:c + 1], c == 0, c == 7,
                            [w.buf, self.cact.buf], [bk.buf])
        self.tt("dve", self.mod.ap, bk.ap[:, 0:48], bfm.ap, ALU.add, [bk.buf, bfm.buf], [self.mod.buf])
        m = self.mod.ap
        self.stt("dve", self.A1.ap, m[:, 8:16], 1.0, gm.ap[:, 0:8], ALU.add, ALU.mult, [self.mod.buf, gm.buf],
                 [self.A1.buf])
        self.ts("dve", self.A1.ap, self.A1.ap, 32.0, 0.0, ALU.mult, ALU.add, [self.A1.buf], [self.A1.buf])
        self.stt("dve", self.A2.ap, m[:, 32:40], 1.0, gm.ap[:, 8:16], ALU.add, ALU.mult, [self.mod.buf, gm.buf],
                 [self.A2.buf])
        self.ts("dve", self.A2.ap, self.A2.ap, 32.0, 0.0, ALU.mult, ALU.add, [self.A2.buf], [self.A2.buf])
        S.barrier(self.cast_keys)
        A.release()

    def make_h(self, xt, hdst_view, hbuf, Avec, Bcol0, sq, rstd, tmp, bk):
        for st in self.make_h_stages(xt, hdst_view, hbuf, Avec, Bcol0, sq, rstd, tmp, bk):
            st()

    def make_h_stages(self, xt, hdst_view, hbuf, Avec, Bcol0, sq, rstd, tmp, bk):
        xv = xt.ap.rearrange("p (c t) -> p c t", c=8)
        sv = sq.ap.rearrange("p (c t) -> p c t", c=8)

        def sA():
            self.act(sq.ap, xt.ap, AF.Square, [xt.buf], [sq.buf])

        def sB():
            for c in range(8):
                self.mm(bk.ap, self.ones.ap, sv[:, c, :], c == 0, c == 7, [sq.buf, self.cst.buf], [bk.buf])

        def sC():
            self.act(rstd.ap, bk.ap, AF.Ln, [bk.buf], [rstd.buf], bias=float(D * EPS))
            self.act(rstd.ap, rstd.ap, AF.Exp, [rstd.buf], [rstd.buf], scale=-0.5)
            for c in range(8):
                tc_ = tmp[c % 2]
                self.stt("dve", tc_.ap, xv[:, c, :],
                         Avec.ap[:, c:c + 1], rstd.ap, ALU.mult, ALU.mult, [xt.buf, Avec.buf, rstd.buf], [tc_.buf])
                self.act(hdst_view[:, c, :], tc_.ap, AF.Identity, [tc_.buf, self.mod.buf], [hbuf],
                         bias=self.mod.ap[:, Bcol0 + c:Bcol0 + c + 1])

        return sA, sB, sC

    def phase_proj(self, l):
        A, S = self.A, self.S
        A.mark()
        hT = A.alloc(8 * S_LEN, BF16, "hT_all")
        hv = hT.ap.rearrange("p (c t) -> p c t", c=8)
        tabs = [A.alloc(S_LEN, F32, "tab%d" % i) for i in range(4)]
        for i in range(4):
            self.dma("sp", tabs[i].ap, self.tabs_in[i], (), [tabs[i].buf], self.key("tab%d" % i))
        A.mark()
        xts = [A.alloc(8 * 512, F32, "xt%d" % i) for i in range(2)]
        kx = [self.key("r6_%d" % _i) for _i in range(2)]
        sq = A.alloc(8 * 512, BF16, "sq")
        rstd = A.alloc(512, F32, "rstd")
        tmp = [A.alloc(512, F32, "tmp%d" % i) for i in range(2)]
        for j in range(8):
            xt = xts[j % 2]
            self.dma("sp", xt.ap.rearrange("p (c t) -> p c t", c=8),
                     self.xT.rearrange("(c p) t -> p c t", p=128)[:, :, j * 512:(j + 1) * 512],
                     [self.xT_buf[j]], [xt.buf], kx[j % 2])
            self.make_h(xt, hv[:, :, j * 512:(j + 1) * 512], hT.buf, self.A1, 0, sq, rstd, tmp, self.bank[j % 2])
        S.barrier(self.cast_keys)
        A.release()
        wr = [A.alloc(1024, BF16, "wr%d" % i) for i in range(2)]
        kwr = [self.key("r7_%d" % _i) for _i in range(2)]
        stg = [A.alloc(S_LEN, BF16, "stg%d" % i) for i in range(2)]
        kst = [self.key("r9_%d" % _i) for _i in range(2)]
        qbf = [A.alloc(512, BF16, "qbf%d" % i) for i in range(3)]
        sqs = [A.alloc(512, BF16, "sqk%d" % i) for i in range(3)]
        rs = [A.alloc(512, F32, "rsk%d" % i) for i in range(3)]
        u1 = [A.alloc(512, F32, "u1%d" % i) for i in range(3)]
        u2 = [A.alloc(512, F32, "u2%d" % i) for i in range(3)]
        it = 0
        pend = []
        for ci in range(NQK):
            w_r = wr[ci % 2]
            self.dma("sp", w_r.ap, self.Wb["qk"][l % 2][ci], [self.Wb_buf["qk"][l % 2]], [w_r.buf], kwr[ci % 2])
            wrv = w_r.ap.rearrange("p (c f) -> p c f", c=8)
            st = stg[ci % 2]
            axial = ci < 5
            Ct, St = (tabs[2], tabs[3]) if axial else (tabs[0], tabs[1])
            dil = 1 if ci < 5 else B_DIL[(ci - 5) // 4]
            for j in range(8):
                k3 = it % 3
                k2_ = it % 2
                it += 1
                bR, bS, bE = self.bank[k3], self.bank[3 + k2_], self.bank[5 + k2_]
                for c in range(8):
                    self.mm(bR.ap, wrv[:, c, :], hv[:, c, j * 512:(j + 1) * 512], c == 0, c == 7,
                            [w_r.buf, hT.buf], [bR.buf])
                if pend:
                    pend.pop(0)()
                qb, sqk, rk, a1, a2 = qbf[k3], sqs[k3], rs[k3], u1[k3], u2[k3]
                tsl = slice(j * 512, (j + 1) * 512)
                self.act(qb.ap, bR.ap, AF.Copy, [bR.buf], [qb.buf])
                self.tt("dve", sqk.ap, qb.ap, qb.ap, ALU.mult, [qb.buf], [sqk.buf])
                self.stt("dve", a1.ap, qb.ap, self.gq.ap[:, ci:ci + 1], Ct.ap[:, tsl], ALU.mult, ALU.mult,
                         [qb.buf, self.gq.buf, Ct.buf], [a1.buf])

                def tail(bS=bS, bE=bE, qb=qb, sqk=sqk, rk=rk, a1=a1, a2=a2, st=st, j=j, tsl=tsl, dil=dil, ci=ci, St=St):
                    self.mm(bS.ap, self.perm.ap, qb.ap, True, True, [qb.buf, self.cst.buf], [bS.buf])
                    self.mm(bE.ap, self.e64.ap, sqk.ap, True, True, [sqk.buf, self.cst.buf], [bE.buf])
                    self.act(rk.ap, bE.ap, AF.Ln, [bE.buf], [rk.buf], bias=float(64 * EPS))
                    self.act(rk.ap, rk.ap, AF.Exp, [rk.buf], [rk.buf], scale=-0.5)
                    self.stt("dve", a2.ap, bS.ap, self.gq.ap[:, NQK + ci:NQK + ci + 1], St.ap[:, tsl], ALU.mult, ALU.mult,
                             [bS.buf, self.gq.buf, St.buf], [a2.buf])
                    self.tt("pool", a1.ap, a1.ap, a2.ap, ALU.add, [a1.buf, a2.buf], [a1.buf])
                    if dil == 1:
                        dst = st.ap[:, tsl]
                        src1, src2 = a1.ap, rk.ap
                    else:
                        n = 512 // dil
                        dst = st.ap.rearrange("p (r m) -> p r m", r=dil)[:, :, j * n:(j + 1) * n]
                        src1 = a1.ap.rearrange("p (m r) -> p r m", r=dil)
                        src2 = rk.ap.rearrange("p (m r) -> p r m", r=dil)
                    self.tt("pool", dst, src1, src2, ALU.mult, [a1.buf, rk.buf], [st.buf])
                    if j == 7:
                        self.dma("sp", self.qkT[ci * 128:(ci + 1) * 128, :], st.ap, [st.buf], [self.qkT_buf[ci]],
                                 kst[ci % 2])

                pend.append(tail)
        while pend:
            pend.pop(0)()
        wv = A.alloc(8 * 896, BF16, "wv")
        self.dma("sp", wv.ap, self.Wb["v"][l % 2], [self.Wb_buf["v"][l % 2]], [wv.buf], self.key("k22"))
        wvv = wv.ap.rearrange("p (c n) -> p c n", c=8)
        vst = [A.alloc(896, BF16, "vst%d" % i) for i in range(2)]
        kv = [self.key("r10_%d" % _i) for _i in range(2)]
        for tb in range(32):
            vs = vst[tb % 2]
            for half in range(2):
                bk = self.bank[6 + half]
                for c in range(8):
                    self.mm(bk.ap[:, 0:448], hv[:, c, tb * 128:(tb + 1) * 128], wvv[:, c, half * 448:(half + 1) * 448],
                            c == 0, c == 7, [hT.buf, wv.buf], [bk.buf])
                self.copy("act" if half == 0 else "dve", vs.ap[:, half * 448:(half + 1) * 448], bk.ap[:, 0:448],
                          [bk.buf], [vs.buf])
            self.dma("sp", self.vtm[tb * 128:(tb + 1) * 128, :], vs.ap, [vs.buf], [self.vtm_buf], kv[tb % 2])
        S.barrier(self.cast_keys)
        A.release()

    def phase_gqa(self, l):
        A, S = self.A, self.S
        self.attn_mark = True
        A.mark()
        self.aT = A.alloc(4 * S_LEN, BF16, "attn_aT")
        self.bT = A.alloc(2 * S_LEN, BF16, "attn_bT")
        aTv = self.aT.ap.rearrange("p (c t) -> p c t", c=4)
        A.mark()
        kT = A.alloc(S_LEN, BF16, "kTa")
        self.dma("sp", kT.ap, self.qkT[4 * 128:5 * 128, :], [self.qkT_buf[4]], [kT.buf], self.key("k23"))
        vstage = A.alloc(32 * 128, BF16, "vstage")
        self.dma("sp", vstage.ap.rearrange("p (b n) -> p b n", b=32),
                 self.vtm.rearrange("(b p) n -> p b n", p=128)[:, :, 0:128], [self.vtm_buf], [vstage.buf], self.key("k24"))
        vaug = A.alloc(32 * 2 * 192, BF16, "vaug")
        vav = vaug.ap.rearrange("p (b g n) -> p b g n", b=32, g=2)
        self.memset("pool", vaug.ap, 1.0, [vaug.buf])
        vsv = vstage.ap.rearrange("p (b g n) -> p b g n", b=32, g=2)
        self.copy("dve", vav[:, :, :, 0:64], vsv, [vstage.buf], [vaug.buf])
        self.copy("pool", vav[:, :, :, 128:192], vsv, [vstage.buf], [vaug.buf])
        NQB = 6
        qp = [A.alloc(512, BF16, "qpad%d" % i) for i in range(NQB)]
        kq = [self.key("gq%d" % _i) for _i in range(NQB)]
        for q in qp:
            self.memset("pool", q.ap, 0.0, [q.buf])
        iters = [(j, h) for j in range(8) for h in range(8)]
        qsel = []
        cntg = [0, 0]
        for (j, h) in iters:
            g = h // 4
            qsel.append(g * 3 + cntg[g] % 3)
            cntg[g] += 1

        def qload(i):
            j, h = iters[i]
            g = h // 4
            q = qp[qsel[i]]
            r0 = (h // 2) * 128 + (h % 2) * 64
            self.dma("sp", q.ap[g * 64:(g + 1) * 64, :], self.qkT[r0:r0 + 64, j * 512:(j + 1) * 512],
                     [self.qkT_buf[h // 2]], [q.buf], kq[qsel[i]])

        NPT = 3
        pt = [A.alloc(1024, BF16, "pt%d" % i) for i in range(NPT)]
        rec = [A.alloc(512, F32, "rec%d" % i) for i in range(2)]
        stb = [T(self.ps[:, i * 1024:(i + 1) * 1024], Buf("st%d" % i)) for i in range(3)]
        otb = [self.bank[6], self.bank[7]]
        it = 0
        step = 0
        LOOK = 2
        for i0 in range(LOOK):
            qload(i0)
        for j in range(8):
            for h in range(8):
                g = h // 4
                ii = j * 8 + h
                if ii + LOOK < len(iters):
                    qload(ii + LOOK)
                q = qp[qsel[ii]]
                ot = otb[it % 2]
                it += 1
                odd = h % 2
                pend = []

                def do_mm2(s2, ptb):
                    for half in range(2):
                        kb = 2 * s2 + half
                        self.mm(ot.ap, vav[:, kb, g, odd * 64:odd * 64 + 128], ptb.ap[:, half * 512:(half + 1) * 512],
                                kb == 0, kb == 31, [vaug.buf, ptb.buf], [ot.buf])

                for s2 in range(16):
                    sb = stb[step % 3]
                    ptb = pt[step % NPT]
                    step += 1
                    for half in range(2):
                        kb = 2 * s2 + half
                        self.mm(sb.ap[:, half * 512:(half + 1) * 512], kT.ap[:, kb * 128:(kb + 1) * 128], q.ap,
                                True, True, [kT.buf, q.buf], [sb.buf])
                    self.act(ptb.ap, sb.ap, AF.Exp, [sb.buf], [ptb.buf], scale=8.0)
                    pend.append((s2, ptb))
                    if len(pend) > 1:
                        do_mm2(*pend.pop(0))
                while pend:
                    do_mm2(*pend.pop(0))
                r = rec[it % 2]
                if odd == 0:
                    self.recip(r.ap[0:64, :], ot.ap[64:128, :], [ot.buf], [r.buf])
                    self.tt("dve", aTv[0:64, h // 2, j * 512:(j + 1) * 512], ot.ap[0:64, :], r.ap[0:64, :], ALU.mult,
                            [ot.buf, r.buf], [self.aT.buf])
                else:
                    self.recip(r.ap[64:128, :], ot.ap[0:64, :], [ot.buf], [r.buf])
                    self.tt("dve", aTv[64:128, h // 2, j * 512:(j + 1) * 512], ot.ap[64:128, :], r.ap[64:128, :], ALU.mult,
                            [ot.buf, r.buf], [self.aT.buf])
                if ii % 3 == 1 and self.pending_cast:
                    self.cast_emit(self.pending_cast.pop(0), pace=[r.buf])
        S.barrier(self.cast_keys)
        A.release()

    def phase_dil(self, l):
        A, S = self.A, self.S
        A.mark()
        bTv = self.bT.ap.rearrange("p (c t) -> p c t", c=2)
        HS = S_LEN // 2
        U = A.alloc(4 * HS, F32, "U")
        Uv = U.ap.rearrange("p (h t) -> p h t", h=4)
        LM = HS
        NBM = LM // 128 + 1
        NSET = 2
        sets = []
        for i in range(NSET):
            st = dict(
                qe=A.alloc(2 * LM, BF16, "qe%d" % i), qo=A.alloc(2 * LM, BF16, "qo%d" % i),
                kr=A.alloc(2 * (LM + 128), BF16, "kr%d" % i),
                va=A.alloc(NBM * 512, BF16, "va%d" % i),
                vstg=A.alloc(NBM * 256, BF16, "vstg%d" % i),
                kqe=self.key("dqe%d" % i), kqo=self.key("dqo%d" % i), kkr=self.key("dkr%d" % i),
                kva=self.key("dva%d" % i))
            self.memset("pool", st["qe"].ap, 0.0, [st["qe"].buf])
            self.memset("pool", st["qo"].ap, 0.0, [st["qo"].buf])
            self.memset("pool", st["vstg"].ap, 0.0, [st["vstg"].buf])
            sets.append(st)
        NPT = 3
        pt = [A.alloc(1024, BF16, "ptd%d" % i) for i in range(NPT)]
        recs = [A.alloc(HS, F32, "recd%d" % i) for i in range(2)]
        stb = [T(self.ps[:, i * 1024:(i + 1) * 1024], Buf("std%d" % i)) for i in range(3)]
        otb = [self.bank[6], self.bank[7]]
        NOT = 2
        maskv = self.masks.ap.rearrange("p (k q) -> p k q", k=2)
        mask4 = maskv.unsqueeze(1).broadcast_to([128, 4, 2, 128])

        jobs = [(s, g, r) for s in range(2) for g in range(3) for r in range(B_DIL[g])]

        first_s1 = min(i for i, jb in enumerate(jobs) if jb[0] == 1)

        def init_ones(st_):
            v5 = st_["va"].ap.rearrange("p (b h two n) -> p b h two n", h=4, two=2, n=64)
            self.memset("pool", v5[:, :, :, 1, :], 1.0, [st_["va"].buf])

        def geom(s, g):
            d = B_DIL[g]
            L = S_LEN // d
            Lh = L // 2
            nb = Lh // 128
            m0 = s * Lh
            lo = max(0, m0 - 64)
            hi = min(L, m0 + Lh + 64)
            return d, L, Lh, nb, m0, lo, hi, lo - (m0 - 64), hi - (m0 - 64)

        def loads(ji):
            s, g, r = jobs[ji]
            st = sets[ji % NSET]
            d, L, Lh, nb, m0, lo, hi, ulo, uhi = geom(s, g)
            base = 5 + 4 * g
            qe, qo, kr, va, vstg = st["qe"], st["qo"], st["kr"], st["va"], st["vstg"]
            if s == 1 and ji - first_s1 < NSET:
                init_ones(st)
            krv = kr.ap.rearrange("p (c u) -> p c u", c=2)
            vav = va.ap.rearrange("p (b n) -> p b n", n=512)
            vsg = vstg.ap.rearrange("p (b n) -> p b n", n=256)
            qsrc = self.qkT.rearrange("(i p) (r m) -> p i r m", p=128, r=d)
            qev = qe.ap.rearrange("p (c m) -> p c m", c=2)
            qov = qo.ap.rearrange("p (c m) -> p c m", c=2)
            self.dma("sp", qev[0:64, :, 0:Lh], qsrc[0:64, base:base + 2, r, m0:m0 + Lh],
                     [self.qkT_buf[base], self.qkT_buf[base + 1]], [qe.buf], st["kqe"])
            self.dma("sp", qov[64:128, :, 0:Lh], qsrc[64:128, base:base + 2, r, m0:m0 + Lh],
                     [self.qkT_buf[base], self.qkT_buf[base + 1]], [qo.buf], st["kqo"])
            if ulo > 0:
                self.memset("pool", krv[:, :, 0:ulo], 0.0, [kr.buf])
            if uhi < Lh + 128:
                self.memset("pool", krv[:, :, uhi:Lh + 128], 0.0, [kr.buf])
            self.dma("sp", krv[:, :, ulo:uhi], qsrc[:, base + 2:base + 4, r, lo:hi],
                     [self.qkT_buf[base + 2], self.qkT_buf[base + 3]], [kr.buf], st["kkr"])
            vsrc = self.vtm.rearrange("(m r) n -> r m n", r=d)[r]
            c0 = 128 + 256 * g
            if ulo > 0:
                self.dma("sp", vsg[64:128, 0, :], vsrc[m0:m0 + 64, c0:c0 + 256], [self.vtm_buf], [vstg.buf], st["kva"])
                b_start = 1
            else:
                b_start = 0
            if uhi < Lh + 128:
                self.dma("sp", vsg[0:64, nb, :], vsrc[m0 + Lh - 64:m0 + Lh, c0:c0 + 256], [self.vtm_buf],
                         [vstg.buf], st["kva"])
                b_end = nb
            else:
                b_end = nb + 1
            mlo = m0 - 64 + 128 * b_start
            self.dma("sp", vsg[:, b_start:b_end, :],
                     vsrc[mlo:mlo + 128 * (b_end - b_start), c0:c0 + 256].rearrange("(b p) n -> p b n", p=128),
                     [self.vtm_buf], [vstg.buf], st["kva"])
            self.copy("pool", vav[:, 0:nb + 1, :].rearrange("p b (h two n) -> p b h two n", h=4, two=2)[:, :, :, 0, :],
                      vsg[:, 0:nb + 1, :].rearrange("p b (h n) -> p b h n", h=4), [vstg.buf], [va.buf])
            if ulo > 0:
                self.memset("pool", vav[0:64, 0, :], 0.0, [va.buf])
            if uhi < Lh + 128:
                self.memset("pool", vav[64:128, nb, :], 0.0, [va.buf])

        for st_ in sets:
            init_ones(st_)
        pend = []
        it = 0
        loads(0)
        for ji, (s, g, r) in enumerate(jobs):
            if g == 0 and r == 0:
                self.memset("pool", U.ap, 0.0, [U.buf])
            st = sets[ji % NSET]
            d, L, Lh, nb, m0, lo, hi, ulo, uhi = geom(s, g)
            qe, qo, kr, va = st["qe"], st["qo"], st["kr"], st["va"]
            krv = kr.ap.rearrange("p (c u) -> p c u", c=2)
            for mb in range(nb):
                sb = stb[it % 3]
                ptb = pt[it % NPT]
                ot = otb[it % 2]
                it += 1
                for h in range(4):
                    qsrc_t = qe if h % 2 == 0 else qo
                    qv = qsrc_t.ap.rearrange("p (c m) -> p c m", c=2)
                    for kk in range(2):
                        col = (h * 2 + kk) * 128
                        self.mm(sb.ap[:, col:col + 128], krv[:, h // 2, 128 * (mb + kk):128 * (mb + kk) + 128],
                                qv[:, h // 2, mb * 128:(mb + 1) * 128], True, True,
                                [kr.buf, qsrc_t.buf], [sb.buf])
                self.act(ptb.ap, sb.ap, AF.Exp, [sb.buf], [ptb.buf], scale=8.0)
                p4 = ptb.ap.rearrange("p (h k q) -> p h k q", h=4, k=2)
                self.tt("dve", p4, p4, mask4, ALU.mult, [ptb.buf, self.cst.buf], [ptb.buf])

                def stage2(ptb=ptb, ot=ot, va=va, mb=mb, d=d, r=r):
                    for h in range(4):
                        for kk in range(2):
                            col = (h * 2 + kk) * 128
                            o = (mb + kk) * 512 + (128 * h if h % 2 == 0 else 128 * h - 64)
                            self.mm(ot.ap[:, h * 128:(h + 1) * 128], va.ap[:, o:o + 128],
                                    ptb.ap[:, col:col + 128], kk == 0, kk == 1, [va.buf, ptb.buf], [ot.buf])
                    if d == 1:
                        uview = Uv[:, :, mb * 128:(mb + 1) * 128]
                    else:
                        uview = Uv.rearrange("p h (m r) -> p h r m", r=d)[:, :, r, mb * 128:(mb + 1) * 128]
                    self.tt("dve", uview, ot.ap.rearrange("p (h q) -> p h q", h=4), uview, ALU.add,
                            [ot.buf, U.buf], [U.buf])

                pend.append((ji, stage2))
                if len(pend) > 2:
                    pend.pop(0)[1]()
                if mb == 0 and ji + 1 < len(jobs):
                    while pend and pend[0][0] < ji:
                        pend.pop(0)[1]()
                    loads(ji + 1)
            if g == 2 and r == B_DIL[2] - 1:
                while pend:
                    pend.pop(0)[1]()
                for h in range(4):
                    rc = recs[h % 2]
                    if h % 2 == 0:
                        self.act(rc.ap[0:64, :], Uv[64:128, h, :], AF.Ln, [U.buf], [rc.buf])
                        self.act(rc.ap[0:64, :], rc.ap[0:64, :], AF.Exp, [rc.buf], [rc.buf], scale=-1.0)
                        self.tt("dve", bTv[0:64, h // 2, s * HS:(s + 1) * HS], Uv[0:64, h, :], rc.ap[0:64, :], ALU.mult,
                                [U.buf, rc.buf], [self.bT.buf])
                    else:
                        self.act(rc.ap[64:128, :], Uv[0:64, h, :], AF.Ln, [U.buf], [rc.buf])
                        self.act(rc.ap[64:128, :], rc.ap[64:128, :], AF.Exp, [rc.buf], [rc.buf], scale=-1.0)
                        self.tt("dve", bTv[64:128, h // 2, s * HS:(s + 1) * HS], Uv[64:128, h, :], rc.ap[64:128, :],
                                ALU.mult, [U.buf, rc.buf], [self.bT.buf])
        S.barrier(self.cast_keys)
        A.release()

    def va_lhs(self, vav, blk, h):
        o = blk * 512 + (128 * h if h % 2 == 0 else 128 * h - 64)
        return self.va_flat[:, o:o + 128]

    def phase_mix(self, l):
        A, S = self.A, self.S
        A.mark()
        aTv = self.aT.ap.rearrange("p (c t) -> p c t", c=4)
        bTv = self.bT.ap.rearrange("p (c t) -> p c t", c=2)
        wg = A.alloc(16 * 1024, BF16, "wg")
        self.dma("sp", wg.ap.rearrange("p (a k) -> p a k", a=16), self.Wb["g"][l % 2].rearrange("a p k -> p a k"),
                 [self.Wb_buf["g"][l % 2]], [wg.buf], self.key("k29"))
        wgv = wg.ap.rearrange("p (a c f) -> p a c f", a=16, c=8)
        wba = A.alloc(4096, BF16, "wba")
        self.dma("sp", wba.ap, self.Wb["ba"][l % 2], [self.Wb_buf["ba"][l % 2]], [wba.buf], self.key("k30"))
        wbav = wba.ap.rearrange("p (c n) -> p c n", c=4)
        wbb = A.alloc(2048, BF16, "wbb")
        self.dma("sp", wbb.ap, self.Wb["bb"][l % 2], [self.Wb_buf["bb"][l % 2]], [wbb.buf], self.key("k31"))
        wbbv = wbb.ap.rearrange("p (c n) -> p c n", c=2)
        wo = A.alloc(8192, BF16, "wo")
        self.dma("sp", wo.ap, self.Wb["o"][l % 2], [self.Wb_buf["o"][l % 2]], [wo.buf], self.key("k32"))
        wov = wo.ap.rearrange("p (c n) -> p c n", c=8)
        xts = [A.alloc(8 * 512, F32, "xtm%d" % i) for i in range(2)]
        kx = [self.key("r12_%d" % _i) for _i in range(2)]
        sq = A.alloc(8 * 512, BF16, "sqm")
        rstd = A.alloc(512, F32, "rstdm")
        tmp = [A.alloc(512, F32, "tmpm%d" % i) for i in range(2)]
        hts = [A.alloc(8 * 512, BF16, "htm%d" % i) for i in range(2)]
        gates = A.alloc(16 * 512, BF16, "gates")
        gv = gates.ap.rearrange("p (a t) -> p a t", a=16)
        mixed = A.alloc(8 * 512, BF16, "mixed")
        mv = mixed.ap.rearrange("p (c t) -> p c t", c=8)
        t1 = [A.alloc(512, F32, "t1%d" % i) for i in range(2)]
        t2 = [A.alloc(512, F32, "t2%d" % i) for i in range(2)]
        nb = 0

        def prep(j):
            xt = xts[j % 2]
            ht = hts[j % 2]
            self.dma("sp", xt.ap.rearrange("p (c t) -> p c t", c=8),
                     self.xT.rearrange("(c p) t -> p c t", p=128)[:, :, j * 512:(j + 1) * 512],
                     [self.xT_buf[j]], [xt.buf], kx[j % 2])
            self.make_h(xt, ht.ap.rearrange("p (c t) -> p c t", c=8), ht.buf, self.A1, 0, sq, rstd, tmp, self.bank[7])

        prep(0)
        for j in range(8):
            xt = xts[j % 2]
            ht = hts[j % 2]
            tsl = slice(j * 512, (j + 1) * 512)
            hv = ht.ap.rearrange("p (c t) -> p c t", c=8)
            for a in range(16):
                bk = self.bank[nb % 6]
                nb += 1
                for c in range(8):
                    self.mm(bk.ap, wgv[:, a, c, :], hv[:, c, :], c == 0, c == 7, [wg.buf, ht.buf], [bk.buf])
                self.act(gv[:, a, :], bk.ap, AF.Sigmoid, [bk.buf], [gates.buf])
            for oc in range(8):
                bA = self.bank[nb % 6]
                bB = self.bank[(nb + 1) % 6]
                nb += 2
                for c in range(4):
                    self.mm(bA.ap, wbav[:, c, oc * 128:(oc + 1) * 128], aTv[:, c, tsl], c == 0, c == 3,
                            [wba.buf, self.aT.buf], [bA.buf])
                for c in range(2):
                    self.mm(bB.ap, wbbv[:, c, oc * 128:(oc + 1) * 128], bTv[:, c, tsl], c == 0, c == 1,
                            [wbb.buf, self.bT.buf], [bB.buf])
                a1, a2 = t1[oc % 2], t2[oc % 2]
                self.tt("dve", a1.ap, bA.ap, gv[:, oc, :], ALU.mult, [bA.buf, gates.buf], [a1.buf])
                self.tt("dve", a2.ap, bB.ap, gv[:, 8 + oc, :], ALU.mult, [bB.buf, gates.buf], [a2.buf])
                self.tt("pool", mv[:, oc, :], a1.ap, a2.ap, ALU.add, [a1.buf, a2.buf], [mixed.buf])
            if j + 1 < 8:
                prep(j + 1)
            xv = xt.ap.rearrange("p (c t) -> p c t", c=8)
            for oc in range(8):
                bk = self.bank[nb % 6]
                nb += 1
                for c in range(8):
                    self.mm(bk.ap, wov[:, c, oc * 128:(oc + 1) * 128], mv[:, c, :], c == 0, c == 7,
                            [wo.buf, mixed.buf], [bk.buf])
                self.stt("dve", xv[:, oc, :], bk.ap, self.mod.ap[:, 16 + oc:17 + oc], xv[:, oc, :], ALU.mult, ALU.add,
                         [bk.buf, self.mod.buf, xt.buf], [xt.buf])
            self.dma("sp", self.xT.rearrange("(c p) t -> p c t", p=128)[:, :, tsl],
                     xt.ap.rearrange("p (c t) -> p c t", c=8), [xt.buf], [self.xT_buf[j]], kx[j % 2])
        S.barrier(self.cast_keys)
        A.release()
        A.release()

    def phase_ffn(self, l):
        A, S = self.A, self.S
        A.mark()
        TG = 1024
        NSUB = TG // 512
        NG = S_LEN // TG
        xts = [A.alloc(8 * 512, F32, "xtf%d" % i) for i in range(NSUB)]
        kx = [self.key("r13_%d" % _i) for _i in range(NSUB)]
        xpre = A.alloc(8 * 512, F32, "xpre")
        kxp = self.key("xpre")
        sq = A.alloc(8 * 512, BF16, "sqf")
        rstd = A.alloc(512, F32, "rstdf")
        tmp = [A.alloc(512, F32, "tmpf%d" % i) for i in range(2)]
        hts = [A.alloc(8 * TG, BF16, "htf%d" % i) for i in range(2)]
        uT = A.alloc(32 * TG, BF16, "uT")
        uv = uT.ap.rearrange("p (k t) -> p k t", k=32)
        w1 = [A.alloc(4 * 1024, BF16, "w1_%d" % i) for i in range(2)]
        k1 = [self.key("r14_%d" % _i) for _i in range(2)]
        w2 = [A.alloc(4096, BF16, "w2_%d" % i) for i in range(2)]
        k2 = [self.key("r15_%d" % _i) for _i in range(2)]
        rl = [A.alloc(512, F32, "rl%d" % i) for i in range(3)]
        nb = 0
        nr = 0
        xTv = self.xT.rearrange("(c p) t -> p c t", p=128)

        def prefetch_stages(tg, sub):
            ht = hts[tg % 2]
            hv_ = ht.ap.rearrange("p (c t) -> p c t", c=8)
            j = tg * NSUB + sub
            sA, sB, sC = self.make_h_stages(xpre, hv_[:, :, sub * 512:(sub + 1) * 512], ht.buf, self.A2, 24, sq, rstd,
                                            tmp, self.bank[7])

            def sA2():
                self.dma("sp", xpre.ap.rearrange("p (c t) -> p c t", c=8), xTv[:, :, j * 512:(j + 1) * 512],
                         [self.xT_buf[j]], [xpre.buf], kxp)
                sA()
            return sA2, sB, sC

        for sub in range(NSUB):
            for st in prefetch_stages(0, sub):
                st()
        for tg in range(NG):
            ht = hts[tg % 2]
            hv = ht.ap.rearrange("p (c t) -> p c t", c=8)
            for sub in range(NSUB):
                j = tg * NSUB + sub
                xt = xts[sub]
                self.dma("sp", xt.ap.rearrange("p (c t) -> p c t", c=8), xTv[:, :, j * 512:(j + 1) * 512],
                         [self.xT_buf[j]], [xt.buf], kx[sub])
            sched = {}
            if tg + 1 < NG:
                for sub in range(NSUB):
                    sA, sB, sC = prefetch_stages(tg + 1, sub)
                    q0 = sub * 4
                    sched.setdefault((q0, "pre"), []).append(sA)
                    sched.setdefault((q0 + 1, "post"), []).append(sB)
                    sched.setdefault((q0 + 2, "post"), []).append(sC)
            for q in range(8):
                for f_ in sched.get((q, "pre"), []):
                    f_()
                w = w1[q % 2]
                self.dma("sp", w.ap.rearrange("p (a k) -> p a k", a=4),
                         self.Wb["f1"][l % 2][q * 4:(q + 1) * 4].rearrange("a p k -> p a k"),
                         [self.Wb_buf["f1"][l % 2]], [w.buf], k1[q % 2])
                wv_ = w.ap.rearrange("p (a c f) -> p a c f", a=4, c=8)
                for a in range(4):
                    hc = q * 4 + a
                    for sub in range(NSUB):
                        bk = self.bank[nb % 7]
                        nb += 1
                        for c in range(8):
                            self.mm(bk.ap, wv_[:, a, c, :], hv[:, c, sub * 512:(sub + 1) * 512], c == 0, c == 7,
                                    [w.buf, ht.buf], [bk.buf])
                        r = rl[nr % 3]
                        nr += 1
                        self.act(r.ap, bk.ap, AF.Relu, [bk.buf], [r.buf])
                        self.tt("pool", uv[:, hc, sub * 512:(sub + 1) * 512], r.ap, r.ap, ALU.mult, [r.buf], [uT.buf])
                for f_ in sched.get((q, "post"), []):
                    f_()
            for oc in range(8):
                w = w2[oc % 2]
                self.dma("sp", w.ap, self.Wb["f2"][l % 2][oc], [self.Wb_buf["f2"][l % 2]], [w.buf], k2[oc % 2])
                wv_ = w.ap.rearrange("p (k f) -> p k f", k=32)
                for sub in range(NSUB):
                    xt = xts[sub]
                    xv = xt.ap.rearrange("p (c t) -> p c t", c=8)
                    bk = self.bank[nb % 7]
                    nb += 1
                    for kc in range(32):
                        self.mm(bk.ap, wv_[:, kc, :], uv[:, kc, sub * 512:(sub + 1) * 512], kc == 0, kc == 31,
                                [w.buf, uT.buf], [bk.buf])
                    self.stt("dve", xv[:, oc, :], bk.ap, self.mod.ap[:, 40 + oc:41 + oc], xv[:, oc, :], ALU.mult, ALU.add,
                             [bk.buf, self.mod.buf, xt.buf], [xt.buf])
            for sub in range(NSUB):
                j = tg * NSUB + sub
                xt = xts[sub]
                self.dma("sp", xTv[:, :, j * 512:(j + 1) * 512],
                         xt.ap.rearrange("p (c t) -> p c t", c=8), [xt.buf], [self.xT_buf[j]], kx[sub])
        S.barrier(self.cast_keys)
        A.release()


def _fm(v, n):
    return np.ascontiguousarray(v.reshape(v.shape[:-1] + (n, 128)).swapaxes(-1, -2))


def _lhs_chunks(W):
    K, N = W.shape
    return np.ascontiguousarray(W.reshape(K // 128, 128, N // 128, 128).transpose(2, 1, 0, 3)).reshape(N // 128, 128, K)


def _rhs_rows(W):
    K, N = W.shape
    return np.ascontiguousarray(W.reshape(K // 128, 128, N).transpose(1, 0, 2)).reshape(128, (K // 128) * N)


_PERM_AX = np.concatenate([np.arange(0, 16), np.arange(32, 48), np.arange(16, 32), np.arange(48, 64)])
_SWAP = np.concatenate([np.arange(32, 64), np.arange(0, 32)])


def _consts():
    f = np.float32
    ident = np.eye(128, dtype=f)
    ones = np.ones((128, 128), f)
    e64 = np.kron(np.eye(2, dtype=f), np.ones((64, 64), f))
    i = np.arange(128)[:, None]
    j = np.arange(128)[None, :]
    mask0 = (i >= j).astype(f)
    mask1 = (i <= j).astype(f)
    m = np.arange(128)
    partner = 64 * (m // 64) + ((m % 64) + 32) % 64
    perm = np.zeros((128, 128), f)
    perm[partner, m] = 1.0
    cst = np.ascontiguousarray(np.stack([ones, e64, mask0, mask1, perm], axis=1))
    theta = np.float32(10000.0)
    pos = np.arange(S_LEN)
    inv32 = (theta ** (-np.arange(0, 64, 2, dtype=f) / f(64))).astype(f)
    ang = pos.astype(f)[:, None] * inv32[None, :]
    cs, sn = np.cos(ang).astype(f), np.sin(ang).astype(f)
    c64 = np.concatenate([cs, cs], axis=1)
    s64 = np.concatenate([-sn, sn], axis=1)
    C_seq = np.ascontiguousarray(np.tile(c64, (1, 2)).T)
    S_seq = np.ascontiguousarray(np.tile(s64, (1, 2)).T)
    inv16 = (theta ** (-np.arange(0, 32, 2, dtype=f) / f(32))).astype(f)
    row = (pos // 64).astype(f)
    col = (pos % 64).astype(f)
    ar = row[:, None] * inv16[None, :]
    ac = col[:, None] * inv16[None, :]
    c32 = np.concatenate([np.cos(ar), np.cos(ac)], axis=1).astype(f)
    s32 = np.concatenate([np.sin(ar), np.sin(ac)], axis=1).astype(f)
    c64a = np.concatenate([c32, c32], axis=1)
    s64a = np.concatenate([-s32, s32], axis=1)
    C_ax = np.ascontiguousarray(np.tile(c64a, (1, 2)).T)
    S_ax = np.ascontiguousarray(np.tile(s64a, (1, 2)).T)
    tabs = np.ascontiguousarray(np.stack([C_seq, S_seq, C_ax, S_ax]))
    return ident, cst, tabs


def _prep_weights(inp, layers):
    f = np.float32
    out = {k: [] for k in ("b_ada", "g_mix", "g_mlp", "g_qk", "w_ada", "w_qk", "w_v", "w_g", "w_ba", "w_bb",
                           "w_o", "w_f1", "w_f2")}
    for l in layers:
        w_in = np.asarray(inp["w_in"][l], f)
        cols = {}
        off = 0
        names = ["qa", "ka", "va"] + [n + str(g) for g in range(3) for n in ("qb", "kb", "vb")] + ["ga", "gb"]
        sizes = [512, 128, 128] + [256] * 9 + [1024, 1024]
        for n, sz in zip(names, sizes):
            cols[n] = w_in[:, off:off + sz]
            off += sz

        def perm_heads(Wc, perm):
            K, N = Wc.shape
            return Wc.reshape(K, N // 64, 64)[:, :, perm].reshape(K, N)

        qn_a = np.asarray(inp["q_norm_a"][l], f)
        kn_a = np.asarray(inp["k_norm_a"][l], f)
        qn_b = np.asarray(inp["q_norm_b"][l], f)
        kn_b = np.asarray(inp["k_norm_b"][l], f)
        raw_cols, sw_cols, g_raw, g_sw = [], [], [], []

        def add(Wc, gain, axial):
            Wp = perm_heads(Wc, _PERM_AX) if axial else Wc
            gp = gain[_PERM_AX] if axial else gain
            raw_cols.append(Wp)
            sw_cols.append(perm_heads(Wp, _SWAP))
            nh = Wc.shape[1] // 64
            g_raw.append(np.tile(gp, nh))
            g_sw.append(np.tile(gp[_SWAP], nh))

        add(cols["qa"], qn_a, True)
        add(cols["ka"], kn_a, True)
        for g in range(3):
            add(cols["qb%d" % g], qn_b[g], False)
            add(cols["kb%d" % g], kn_b[g], False)
        Wqk = np.concatenate(raw_cols, axis=1)
        Wqks = np.concatenate(sw_cols, axis=1)
        gr = np.concatenate(g_raw)
        gs = np.concatenate(g_sw)
        out["g_qk"].append(np.concatenate([_fm(gr, NQK), _fm(gs, NQK)], axis=1))
        out["w_qk"].append(_lhs_chunks(Wqk))
        Wv = np.concatenate([cols["va"], cols["vb0"], cols["vb1"], cols["vb2"]], axis=1)
        out["w_v"].append(_rhs_rows(Wv))
        out["w_g"].append(_lhs_chunks(np.concatenate([cols["ga"], cols["gb"]], axis=1)))
        out["w_ba"].append(_rhs_rows(np.asarray(inp["w_branch_a"][l], f)))
        out["w_bb"].append(_rhs_rows(np.asarray(inp["w_branch_b"][l], f)))
        out["w_o"].append(_rhs_rows(np.asarray(inp["w_out"][l], f)))
        out["w_f1"].append(_lhs_chunks(np.asarray(inp["w_ff1"][l], f)))
        out["w_f2"].append(_lhs_chunks(np.asarray(inp["w_ff2"][l], f)))
        out["w_ada"].append(_lhs_chunks(np.asarray(inp["w_ada"][l], f)))
        out["b_ada"].append(_fm(np.asarray(inp["b_ada"][l], f), 48))
        out["g_mix"].append(_fm(np.asarray(inp["g_mix"][l], f), 8))
        out["g_mlp"].append(_fm(np.asarray(inp["g_mlp"][l], f), 8))
    return {k: np.ascontiguousarray(np.stack(v)) for k, v in out.items()}


_PROG_CACHE = {}


def _get_prog(NL, debug=False):
    key = (NL, debug)
    if key not in _PROG_CACHE:
        p = Prog(NL, debug)
        p.build()
        _PROG_CACHE[key] = p
    return _PROG_CACHE[key]


def run_layers(x, c, inp, layers, cores=NCORES, debug=False):
    prog = _get_prog(len(layers), debug)
    ident, cst, tabs = _consts()
    wts = _prep_weights(inp, layers)
    in_maps = []
    for b in range(cores):
        m = {"x": np.ascontiguousarray(x[b]), "c_fm": _fm(np.asarray(c[b], np.float32), 8), "ident": ident,
             "cst_bf": cst, "tabs": tabs}
        m.update(wts)
        in_maps.append(m)
    res = run_bass_kernel_spmd(prog.nc, in_maps, core_ids=list(range(cores)))
    if debug:
        return res
    return np.stack([np.asarray(r["out"]) for r in res.results])


FUSED = True


def kernel(x, c, w_ada, b_ada, g_mix, g_mlp, w_in, q_norm_a, k_norm_a, q_norm_b, k_norm_b,
           w_branch_a, w_branch_b, w_out, w_ff1, w_ff2):
    inp = dict(w_ada=w_ada, b_ada=b_ada, g_mix=g_mix, g_mlp=g_mlp, w_in=w_in, q_norm_a=q_norm_a, k_norm_a=k_norm_a,
               q_norm_b=q_norm_b, k_norm_b=k_norm_b, w_branch_a=w_branch_a, w_branch_b=w_branch_b, w_out=w_out,
               w_ff1=w_ff1, w_ff2=w_ff2)
    x = np.asarray(x, np.float32)
    c = np.asarray(c, np.float32)
    if FUSED:
        return run_layers(x, c, inp, list(range(DEPTH))).astype(np.float32)
    for l in range(DEPTH):
        x = run_layers(x, c, inp, [l])
    return x.astype(np.float32)
```

```python
import contextlib
import numpy as np
import concourse.bass as bass
import concourse.mybir as mybir
from concourse.bass_utils import run_bass_kernel_spmd

F32 = mybir.dt.float32
BF16 = mybir.dt.bfloat16
AF = mybir.ActivationFunctionType
ALU = mybir.AluOpType

D = 1024
S_LEN = 4096
DEPTH = 4
NCORES = 8
EPS = 1e-6
NQK = 17
B_DIL = (1, 4, 16)

ENGS = ("pe", "act", "dve", "pool", "sp")
NE = len(ENGS)
EIDX = {e: i for i, e in enumerate(ENGS)}


class Buf:
    __slots__ = ("name", "last_w", "readers")

    def __init__(self, name=""):
        self.name = name
        self.last_w = None
        self.readers = []


class Op:
    __slots__ = ("eng", "idx", "fn", "deps", "dma_sem", "dma_val", "signalled", "count", "waits", "clock")

    def __init__(self, eng, idx, fn):
        self.eng = eng
        self.idx = idx
        self.fn = fn
        self.deps = []
        self.dma_sem = None
        self.dma_val = 0
        self.signalled = False
        self.count = 0
        self.waits = None
        self.clock = None


class Sched:
    def __init__(self):
        self.ops = {e: [] for e in ENGS}
        self.all_ops = []
        self.dma_sem_count = {}
        self.dma_last = {}

    def op(self, eng, fn, reads=(), writes=(), dma=None):
        o = Op(eng, len(self.ops[eng]), fn)
        deps = []
        for b in reads:
            if b.last_w is not None:
                deps.append(b.last_w)
        for b in writes:
            if b.last_w is not None:
                deps.append(b.last_w)
            deps.extend(b.readers)
        o.deps = deps
        for b in reads:
            b.readers.append(o)
        for b in writes:
            b.last_w = o
            b.readers = []
        if dma is not None:
            v = self.dma_sem_count.get(dma, 0) + 16
            self.dma_sem_count[dma] = v
            o.dma_sem = dma
            o.dma_val = v
            self.dma_last[dma] = o
        self.ops[eng].append(o)
        self.all_ops.append(o)
        return o

    def barrier(self, exclude=()):
        last = [self.ops[e][-1] for e in ENGS if self.ops[e]] + \
               [o for k, o in self.dma_last.items() if k not in exclude]
        for e in ENGS:
            o = self.op(e, lambda h: h.nop())
            o.deps = list(last)

    def resolve(self):
        clock = {e: [-1] * NE for e in ENGS}
        dma_seen = {e: {} for e in ENGS}
        for o in self.all_ops:
            e = o.eng
            ck = clock[e]
            ei = EIDX[e]
            if e == "pe":
                ck[ei] = o.idx - 1
            ewait = {}
            dwait = {}
            seen = dma_seen[e]
            for d in o.deps:
                if d.dma_sem is not None:
                    if d.dma_val > seen.get(d.dma_sem, 0) and d.dma_val > dwait.get(d.dma_sem, 0):
                        dwait[d.dma_sem] = d.dma_val
                else:
                    di = EIDX[d.eng]
                    if ck[di] >= d.idx:
                        continue
                    w = ewait.get(di)
                    if w is None or d.idx > w.idx:
                        ewait[di] = d
            wl = []
            if dwait:
                for d in o.deps:
                    if d.dma_sem is not None and dwait.get(d.dma_sem, 0) >= d.dma_val:
                        dc = d.clock
                        qi = EIDX[d.eng]
                        for j in range(NE):
                            if j != qi and dc[j] > ck[j]:
                                ck[j] = dc[j]
                for k, v in dwait.items():
                    wl.append((0, k, v))
                    seen[k] = v
            for di, d in ewait.items():
                if ck[di] >= d.idx:
                    continue
                d.signalled = True
                wl.append((1, d.eng, d))
                dc = d.clock
                for j in range(NE):
                    if dc[j] > ck[j]:
                        ck[j] = dc[j]
            o.waits = wl
            c = list(ck)
            if o.dma_sem is None and o.idx > c[ei]:
                c[ei] = o.idx
            o.clock = c
            o.deps = None
        for e in ENGS:
            n = 0
            for o in self.ops[e]:
                if o.signalled:
                    n += 1
                o.count = n

    def emit_engine(self, e, handle, eng_sems, dma_sems, final_wait_all=False):
        for o in self.ops[e]:
            for w in o.waits:
                if w[0] == 0:
                    handle.wait_ge(dma_sems[w[1]], w[2])
                else:
                    handle.wait_ge(eng_sems[w[1]], w[2].count)
            ins = o.fn(handle)
            if o.dma_sem is not None:
                ins.then_inc(dma_sems[o.dma_sem], 16)
            elif o.signalled:
                ins.then_inc(eng_sems[e], 1)
        if final_wait_all:
            for k, v in self.dma_sem_count.items():
                handle.wait_ge(dma_sems[k], v)


class T:
    __slots__ = ("ap", "buf")

    def __init__(self, ap, buf):
        self.ap = ap
        self.buf = buf

    def v(self, pattern, **kw):
        return self.ap.rearrange(pattern, **kw)


class Arena:
    def __init__(self, base_ap, n_words):
        self.base = base_ap
        self.n = n_words
        self.off = 0
        self.marks = []

    def alloc(self, n_elems, dtype=F32, name=""):
        nbytes = n_elems * (4 if dtype == F32 else 2)
        nw = (nbytes + 3) // 4
        nw = (nw + 15) // 16 * 16
        assert self.off + nw <= self.n, f"arena overflow {name}: {self.off}+{nw}>{self.n}"
        ap = self.base[:, self.off:self.off + nw]
        self.off += nw
        if dtype != F32:
            ap = ap.bitcast(dtype)
        ap = ap[:, 0:n_elems]
        return T(ap, Buf(name))

    def mark(self):
        self.marks.append(self.off)

    def release(self):
        self.off = self.marks.pop()


class Prog:
    def __init__(self, NL, debug=False):
        self.NL = NL
        self.debug = debug
        self.nc = bass.Bass("TRN2", target_bir_lowering=False)
        self.S = Sched()
        self.nkeys = 0
        self.named_keys = {}
        self._auto = 0

    def key(self, name=None):
        if name is None:
            raise ValueError("key needs a name")
        if name not in self.named_keys:
            self.named_keys[name] = self.nkeys
            self.nkeys += 1
        return self.named_keys[name]

    def dma(self, eng, out, in_, reads, writes, key):
        return self.S.op(eng, lambda h: h.dma_start(out=out, in_=in_), reads, writes, dma=key)

    def mm(self, out, lhsT, rhs, start, stop, reads, writes):
        return self.S.op("pe", lambda h: h.matmul(out, lhsT=lhsT, rhs=rhs, start=start, stop=stop), reads, writes)

    def act(self, out, in_, func, reads, writes, scale=1.0, bias=0.0):
        return self.S.op("act", lambda h: h.activation(out=out, in_=in_, func=func, bias=bias, scale=scale),
                         reads, writes)

    def tt(self, eng, out, in0, in1, op, reads, writes):
        return self.S.op(eng, lambda h: h.tensor_tensor(out=out, in0=in0, in1=in1, op=op), reads, writes)

    def stt(self, eng, out, in0, scalar, in1, op0, op1, reads, writes):
        return self.S.op(eng, lambda h: h.scalar_tensor_tensor(out=out, in0=in0, scalar=scalar, in1=in1,
                                                                op0=op0, op1=op1), reads, writes)

    def ts(self, eng, out, in0, s1, s2, op0, op1, reads, writes):
        return self.S.op(eng, lambda h: h.tensor_scalar(out=out, in0=in0, scalar1=s1, scalar2=s2, op0=op0, op1=op1),
                         reads, writes)

    def copy(self, eng, out, in_, reads, writes):
        if eng == "act":
            return self.act(out, in_, AF.Copy, reads, writes)
        return self.S.op(eng, lambda h: h.tensor_copy(out=out, in_=in_), reads, writes)

    def memset(self, eng, ap, val, writes):
        return self.S.op(eng, lambda h: h.memset(ap, val), (), writes)

    def recip(self, out, in_, reads, writes):
        return self.S.op("dve", lambda h: h.reciprocal(out=out, in_=in_), reads, writes)

    def declare(self):
        nc, NL = self.nc, self.NL

        def inp(name, shape, dt=F32):
            return nc.dram_tensor(name, list(shape), dt, kind="ExternalInput").ap()

        def scr(name, shape, dt):
            kind = "ExternalOutput" if (self.debug and name in ("qkT", "vtm", "xT", "dbg")) else "Internal"
            return nc.dram_tensor(name, list(shape), dt, kind=kind).ap()

        self.x_in = inp("x", [S_LEN, D])
        self.out = nc.dram_tensor("out", [S_LEN, D], F32, kind="ExternalOutput").ap()
        self.c_in = inp("c_fm", [128, 8])
        self.ident_in = inp("ident", [128, 128])
        self.cst_bf_in = inp("cst_bf", [128, 5, 128])
        self.tabs_in = inp("tabs", [4, 128, S_LEN])
        self.b_ada = inp("b_ada", [NL, 128, 48])
        self.g_mix = inp("g_mix", [NL, 128, 8])
        self.g_mlp = inp("g_mlp", [NL, 128, 8])
        self.g_qk = inp("g_qk", [NL, 128, 2 * NQK])
        W = {}
        W["ada"] = (inp("w_ada", [NL, 48, 128, 8 * 128]), [48, 128, 1024])
        W["qk"] = (inp("w_qk", [NL, NQK, 128, 1024]), [NQK, 128, 1024])
        W["v"] = (inp("w_v", [NL, 128, 8 * 896]), [128, 8 * 896])
        W["g"] = (inp("w_g", [NL, 16, 128, 1024]), [16, 128, 1024])
        W["ba"] = (inp("w_ba", [NL, 128, 4 * 1024]), [128, 4096])
        W["bb"] = (inp("w_bb", [NL, 128, 2 * 1024]), [128, 2048])
        W["o"] = (inp("w_o", [NL, 128, 8 * 1024]), [128, 8192])
        W["f1"] = (inp("w_f1", [NL, 32, 128, 1024]), [32, 128, 1024])
        W["f2"] = (inp("w_f2", [NL, 8, 128, 4096]), [8, 128, 4096])
        self.W = W
        self.Wb = {}
        self.Wb_buf = {}
        self.Wb_key = {}
        for k, (ap, shp) in W.items():
            self.Wb[k] = [scr("wb%d_%s" % (i, k), shp, BF16) for i in range(2)]
            self.Wb_buf[k] = [Buf("wb%d_%s" % (i, k)) for i in range(2)]
            self.Wb_key[k] = [self.key("wb%d_%s" % (i, k)) for i in range(2)]
        self.cast_keys = set(k for ks in self.Wb_key.values() for k in ks)
        self.xT = scr("xT", [D, S_LEN], F32)
        self.xT_buf = [Buf("xT%d" % j) for j in range(8)]
        self.qkT = scr("qkT", [NQK * 128, S_LEN], BF16)
        self.qkT_buf = [Buf("qkT%d" % i) for i in range(NQK)]
        self.vtm = scr("vtm", [S_LEN, 896], BF16)
        self.vtm_buf = Buf("vtm")

    def cast_plan(self, l, names, step=2048):
        plan = []
        for k in names:
            src, shp = self.W[k]
            n = int(np.prod(shp))
            sap = src[l]
            dap = self.Wb[k][l % 2]
            if len(shp) == 3:
                sap = sap.rearrange("a p (r b) -> (a p r) b", b=1024)
                dap = dap.rearrange("a p (r b) -> (a p r) b", b=1024)
            else:
                sap = sap.rearrange("p (r b) -> (p r) b", b=1024)
                dap = dap.rearrange("p (r b) -> (p r) b", b=1024)
            rows = n // 1024
            for r0 in range(0, rows, step):
                r1 = min(rows, r0 + step)
                plan.append((dap[r0:r1, :], sap[r0:r1, :], self.Wb_buf[k][l % 2], self.Wb_key[k][l % 2]))
        return plan

    def cast_emit(self, item, pace=()):
        dap, sap, buf, key = item
        self.dma("pool", dap, sap, pace, [buf], key)

    def cast_weights(self, l, names):
        for item in self.cast_plan(l, names, step=8192):
            self.cast_emit(item)

    def build(self):
        nc = self.nc
        self.declare()
        NW = 52992
        with (nc.sbuf_tensor("arena", [128, NW], F32) as arena_t,
              nc.psum_tensor("ps", [128, 4096], F32) as ps_t):
            self.A = Arena(arena_t, NW)
            self.ps = ps_t
            self.bank = [T(ps_t[:, b * 512:(b + 1) * 512], Buf("bank%d" % b)) for b in range(8)]
            self.body()
            self.S.resolve()
            with contextlib.ExitStack() as es:
                eng_sems = {e: es.enter_context(nc.semaphore("s_" + e)) for e in ENGS}
                dma_sems = {k: es.enter_context(nc.semaphore("d%d" % k)) for k in self.S.dma_sem_count}
                block = es.enter_context(nc.Block())
                S = self.S

                @block.tensor
                def _(h):
                    S.emit_engine("pe", h, eng_sems, dma_sems)

                @block.scalar
                def _(h):
                    S.emit_engine("act", h, eng_sems, dma_sems)

                @block.vector
                def _(h):
                    S.emit_engine("dve", h, eng_sems, dma_sems)

                @block.gpsimd
                def _(h):
                    S.emit_engine("pool", h, eng_sems, dma_sems)

                @block.sync
                def _(h):
                    S.emit_engine("sp", h, eng_sems, dma_sems, final_wait_all=True)
        return nc

    def load_consts(self):
        A = self.A
        self.ident = A.alloc(128, F32, "ident")
        self.dma("sp", self.ident.ap, self.ident_in, (), [self.ident.buf], self.key("k17"))
        cst = A.alloc(5 * 128, BF16, "cst")
        self.dma("pool", cst.ap.rearrange("p (a b) -> p a b", a=5), self.cst_bf_in, (), [cst.buf], self.key("k18"))
        self.perm = T(cst.ap[:, 512:640], cst.buf)
        self.cst = cst
        self.ones = T(cst.ap[:, 0:128], cst.buf)
        self.e64 = T(cst.ap[:, 128:256], cst.buf)
        self.masks = T(cst.ap[:, 256:512], cst.buf)
        self.cfm = A.alloc(8, F32, "cfm")
        self.dma("sp", self.cfm.ap, self.c_in, (), [self.cfm.buf], self.key("k19"))
        self.cact = A.alloc(8, BF16, "cact")
        self.act(self.cact.ap, self.cfm.ap, AF.Silu, [self.cfm.buf], [self.cact.buf])
        self.mod = A.alloc(48, F32, "mod")
        self.A1 = A.alloc(8, F32, "A1")
        self.A2 = A.alloc(8, F32, "A2")
        self.gq = A.alloc(2 * NQK, F32, "gq")
        self.small_key = self.key("k20")

    def body(self):
        A = self.A
        self.load_consts()
        self.phase_in()
        WN = ["ada", "qk", "v", "g", "ba", "bb", "o", "f1", "f2"]
        self.cast_weights(0, WN)
        for l in range(self.NL):
            self.phase_ada(l)
            self.phase_proj(l)
            self.pending_cast = self.cast_plan(l + 1, WN) if l + 1 < self.NL else []
            self.phase_gqa(l)
            while self.pending_cast:
                self.cast_emit(self.pending_cast.pop(0))
            self.phase_dil(l)
            self.phase_mix(l)
            self.phase_ffn(l)
        self.phase_out()

    def phase_in(self):
        A, S = self.A, self.S
        A.mark()
        xin = [A.alloc(1024, F32, "xin%d" % i) for i in range(2)]
        xo = [A.alloc(8 * 512, F32, "xo%d" % i) for i in range(2)]
        kin = [self.key("r1_%d" % _i) for _i in range(2)]
        ko = [self.key("r2_%d" % _i) for _i in range(2)]
        for j in range(8):
            o = xo[j % 2]
            for tb in range(4):
                blk = j * 4 + tb
                xi = xin[blk % 2]
                self.dma("sp", xi.ap, self.x_in[blk * 128:(blk + 1) * 128, :], (), [xi.buf], kin[blk % 2])
                for half in range(2):
                    bk = self.bank[(blk * 2 + half) % 8]
                    for cc in range(4):
                        c = half * 4 + cc
                        self.mm(bk.ap[:, cc * 128:(cc + 1) * 128], xi.ap[:, c * 128:(c + 1) * 128], self.ident.ap,
                                True, True, [xi.buf, self.ident.buf], [bk.buf])
                    dst = o.ap.rearrange("p (c t) -> p c t", c=8)[:, half * 4:(half + 1) * 4, tb * 128:(tb + 1) * 128]
                    src = bk.ap.rearrange("p (c t) -> p c t", c=4)
                    self.copy("dve" if half == 0 else "act", dst, src, [bk.buf], [o.buf])
            self.dma("sp", self.xT.rearrange("(c p) t -> p c t", p=128)[:, :, j * 512:(j + 1) * 512],
                     o.ap.rearrange("p (c t) -> p c t", c=8), [o.buf], [self.xT_buf[j]], ko[j % 2])
        S.barrier(self.cast_keys)
        A.release()

    def phase_out(self):
        A, S = self.A, self.S
        A.mark()
        xi2 = [A.alloc(8 * 512, F32, "xo_in%d" % i) for i in range(2)]
        xo2 = [A.alloc(1024, F32, "xo_out%d" % i) for i in range(2)]
        kin = [self.key("r3_%d" % _i) for _i in range(2)]
        ko = [self.key("r4_%d" % _i) for _i in range(2)]
        for j in range(8):
            xi = xi2[j % 2]
            self.dma("sp", xi.ap.rearrange("p (c t) -> p c t", c=8),
                     self.xT.rearrange("(c p) t -> p c t", p=128)[:, :, j * 512:(j + 1) * 512],
                     [self.xT_buf[j]], [xi.buf], kin[j % 2])
            xv = xi.ap.rearrange("p (c t) -> p c t", c=8)
            for tb in range(4):
                blk = j * 4 + tb
                o = xo2[blk % 2]
                for half in range(2):
                    bk = self.bank[(blk * 2 + half) % 8]
                    for cc in range(4):
                        c = half * 4 + cc
                        self.mm(bk.ap[:, cc * 128:(cc + 1) * 128], xv[:, c, tb * 128:(tb + 1) * 128], self.ident.ap,
                                True, True, [xi.buf, self.ident.buf], [bk.buf])
                    self.copy("dve" if half == 0 else "act", o.ap[:, half * 512:(half + 1) * 512], bk.ap,
                              [bk.buf], [o.buf])
                self.dma("sp", self.out[blk * 128:(blk + 1) * 128, :], o.ap, [o.buf], (), ko[blk % 2])
        A.release()

    def phase_ada(self, l):
        A, S = self.A, self.S
        A.mark()
        wt = [A.alloc(8 * 1024, BF16, "wada%d" % i) for i in range(2)]
        kw = [self.key("r5_%d" % _i) for _i in range(2)]
        bfm = A.alloc(48, F32, "bfm")
        gm = A.alloc(16, F32, "gm")
        bfm.buf = gm.buf = self.gq.buf
        self.dma("sp", bfm.ap, self.b_ada[l], (), [bfm.buf], self.small_key)
        self.dma("sp", gm.ap[:, 0:8], self.g_mix[l], (), [gm.buf], self.small_key)
        self.dma("sp", gm.ap[:, 8:16], self.g_mlp[l], (), [gm.buf], self.small_key)
        self.dma("sp", self.gq.ap, self.g_qk[l], (), [self.gq.buf], self.small_key)
        bk = self.bank[0]
        for piece in range(6):
            w = wt[piece % 2]
            self.dma("sp", w.ap.rearrange("p (f k) -> p f k", f=8),
                     self.Wb["ada"][l % 2][piece * 8:(piece + 1) * 8].rearrange("f p k -> p f k"),
                     [self.Wb_buf["ada"][l % 2]], [w.buf], kw[piece % 2])
            wv = w.ap.rearrange("p (f c k) -> p f c k", f=8, c=8)
            for f in range(8):
                col = piece * 8 + f
                for c in range(8):
                    self.mm(bk.ap[:, col:col + 1], wv[:, f, c, :], self.cact.ap[:, c:c + 1], c == 0, c == 7,
                            [w.buf, self.cact.buf], [bk.buf])
        self.tt("dve", self.mod.ap, bk.ap[:, 0:48], bfm.ap, ALU.add, [bk.buf, bfm.buf], [self.mod.buf])
        m = self.mod.ap
        self.stt("dve", self.A1.ap, m[:, 8:16], 1.0, gm.ap[:, 0:8], ALU.add, ALU.mult, [self.mod.buf, gm.buf],
                 [self.A1.buf])
        self.ts("dve", self.A1.ap, self.A1.ap, 32.0, 0.0, ALU.mult, ALU.add, [self.A1.buf], [self.A1.buf])
        self.stt("dve", self.A2.ap, m[:, 32:40], 1.0, gm.ap[:, 8:16], ALU.add, ALU.mult, [self.mod.buf, gm.buf],
                 [self.A2.buf])
        self.ts("dve", self.A2.ap, self.A2.ap, 32.0, 0.0, ALU.mult, ALU.add, [self.A2.buf], [self.A2.buf])
        S.barrier(self.cast_keys)
        A.release()

    def make_h(self, xt, hdst_view, hbuf, Avec, Bcol0, sq, rstd, tmp, bk):
        for st in self.make_h_stages(xt, hdst_view, hbuf, Avec, Bcol0, sq, rstd, tmp, bk):
            st()

    def make_h_stages(self, xt, hdst_view, hbuf, Avec, Bcol0, sq, rstd, tmp, bk):
        xv = xt.ap.rearrange("p (c t) -> p c t", c=8)
        sv = sq.ap.rearrange("p (c t) -> p c t", c=8)

        def sA():
            self.act(sq.ap, xt.ap, AF.Square, [xt.buf], [sq.buf])

        def sB():
            for c in range(8):
                self.mm(bk.ap, self.ones.ap, sv[:, c, :], c == 0, c == 7, [sq.buf, self.cst.buf], [bk.buf])

        def sC():
            self.act(rstd.ap, bk.ap, AF.Ln, [bk.buf], [rstd.buf], bias=float(D * EPS))
            self.act(rstd.ap, rstd.ap, AF.Exp, [rstd.buf], [rstd.buf], scale=-0.5)
            for c in range(8):
                tc_ = tmp[c % 2]
                self.stt("dve", tc_.ap, xv[:, c, :],
                         Avec.ap[:, c:c + 1], rstd.ap, ALU.mult, ALU.mult, [xt.buf, Avec.buf, rstd.buf], [tc_.buf])
                self.act(hdst_view[:, c, :], tc_.ap, AF.Identity, [tc_.buf, self.mod.buf], [hbuf],
                         bias=self.mod.ap[:, Bcol0 + c:Bcol0 + c + 1])

        return sA, sB, sC

    def phase_proj(self, l):
        A, S = self.A, self.S
        A.mark()
        hT = A.alloc(8 * S_LEN, BF16, "hT_all")
        hv = hT.ap.rearrange("p (c t) -> p c t", c=8)
        tabs = [A.alloc(S_LEN, F32, "tab%d" % i) for i in range(4)]
        for i in range(4):
            self.dma("sp", tabs[i].ap, self.tabs_in[i], (), [tabs[i].buf], self.key("tab%d" % i))
        A.mark()
        xts = [A.alloc(8 * 512, F32, "xt%d" % i) for i in range(2)]
        kx = [self.key("r6_%d" % _i) for _i in range(2)]
        sq = A.alloc(8 * 512, BF16, "sq")
        rstd = A.alloc(512, F32, "rstd")
        tmp = [A.alloc(512, F32, "tmp%d" % i) for i in range(2)]
        for j in range(8):
            xt = xts[j % 2]
            self.dma("sp", xt.ap.rearrange("p (c t) -> p c t", c=8),
                     self.xT.rearrange("(c p) t -> p c t", p=128)[:, :, j * 512:(j + 1) * 512],
                     [self.xT_buf[j]], [xt.buf], kx[j % 2])
            self.make_h(xt, hv[:, :, j * 512:(j + 1) * 512], hT.buf, self.A1, 0, sq, rstd, tmp, self.bank[j % 2])
        S.barrier(self.cast_keys)
        A.release()
        wr = [A.alloc(1024, BF16, "wr%d" % i) for i in range(2)]
        kwr = [self.key("r7_%d" % _i) for _i in range(2)]
        stg = [A.alloc(S_LEN, BF16, "stg%d" % i) for i in range(2)]
        kst = [self.key("r9_%d" % _i) for _i in range(2)]
        qbf = [A.alloc(512, BF16, "qbf%d" % i) for i in range(3)]
        sqs = [A.alloc(512, BF16, "sqk%d" % i) for i in range(3)]
        rs = [A.alloc(512, F32, "rsk%d" % i) for i in range(3)]
        u1 = [A.alloc(512, F32, "u1%d" % i) for i in range(3)]
        u2 = [A.alloc(512, F32, "u2%d" % i) for i in range(3)]
        it = 0
        pend = []
        for ci in range(NQK):
            w_r = wr[ci % 2]
            self.dma("sp", w_r.ap, self.Wb["qk"][l % 2][ci], [self.Wb_buf["qk"][l % 2]], [w_r.buf], kwr[ci % 2])
            wrv = w_r.ap.rearrange("p (c f) -> p c f", c=8)
            st = stg[ci % 2]
            axial = ci < 5
            Ct, St = (tabs[2], tabs[3]) if axial else (tabs[0], tabs[1])
            dil = 1 if ci < 5 else B_DIL[(ci - 5) // 4]
            for j in range(8):
                k3 = it % 3
                k2_ = it % 2
                it += 1
                bR, bS, bE = self.bank[k3], self.bank[3 + k2_], self.bank[5 + k2_]
                for c in range(8):
                    self.mm(bR.ap, wrv[:, c, :], hv[:, c, j * 512:(j + 1) * 512], c == 0, c == 7,
                            [w_r.buf, hT.buf], [bR.buf])
                if pend:
                    pend.pop(0)()
                qb, sqk, rk, a1, a2 = qbf[k3], sqs[k3], rs[k3], u1[k3], u2[k3]
                tsl = slice(j * 512, (j + 1) * 512)
                self.act(qb.ap, bR.ap, AF.Copy, [bR.buf], [qb.buf])
                self.act(sqk.ap, bR.ap, AF.Square, [bR.buf], [sqk.buf])
                self.stt("dve", a1.ap, qb.ap, self.gq.ap[:, ci:ci + 1], Ct.ap[:, tsl], ALU.mult, ALU.mult,
                         [qb.buf, self.gq.buf, Ct.buf], [a1.buf])

                def tail(bS=bS, bE=bE, qb=qb, sqk=sqk, rk=rk, a1=a1, a2=a2, st=st, j=j, tsl=tsl, dil=dil, ci=ci, St=St):
                    self.mm(bS.ap, self.perm.ap, qb.ap, True, True, [qb.buf, self.cst.buf], [bS.buf])
                    self.mm(bE.ap, self.e64.ap, sqk.ap, True, True, [sqk.buf, self.cst.buf], [bE.buf])
                    self.act(rk.ap, bE.ap, AF.Ln, [bE.buf], [rk.buf], bias=float(64 * EPS))
                    self.act(rk.ap, rk.ap, AF.Exp, [rk.buf], [rk.buf], scale=-0.5)
                    self.stt("dve", a2.ap, bS.ap, self.gq.ap[:, NQK + ci:NQK + ci + 1], St.ap[:, tsl], ALU.mult, ALU.mult,
                             [bS.buf, self.gq.buf, St.buf], [a2.buf])
                    self.tt("dve", a1.ap, a1.ap, a2.ap, ALU.add, [a1.buf, a2.buf], [a1.buf])
                    if dil == 1:
                        dst = st.ap[:, tsl]
                        src1, src2 = a1.ap, rk.ap
                    else:
                        n = 512 // dil
                        dst = st.ap.rearrange("p (r m) -> p r m", r=dil)[:, :, j * n:(j + 1) * n]
                        src1 = a1.ap.rearrange("p (m r) -> p r m", r=dil)
                        src2 = rk.ap.rearrange("p (m r) -> p r m", r=dil)
                    self.tt("pool", dst, src1, src2, ALU.mult, [a1.buf, rk.buf], [st.buf])
                    if j == 7:
                        self.dma("sp", self.qkT[ci * 128:(ci + 1) * 128, :], st.ap, [st.buf], [self.qkT_buf[ci]],
                                 kst[ci % 2])

                pend.append(tail)
        while pend:
            pend.pop(0)()
        wv = A.alloc(8 * 896, BF16, "wv")
        self.dma("sp", wv.ap, self.Wb["v"][l % 2], [self.Wb_buf["v"][l % 2]], [wv.buf], self.key("k22"))
        wvv = wv.ap.rearrange("p (c n) -> p c n", c=8)
        vst = [A.alloc(896, BF16, "vst%d" % i) for i in range(2)]
        kv = [self.key("r10_%d" % _i) for _i in range(2)]
        for tb in range(32):
            vs = vst[tb % 2]
            for half in range(2):
                bk = self.bank[6 + half]
                for c in range(8):
                    self.mm(bk.ap[:, 0:448], hv[:, c, tb * 128:(tb + 1) * 128], wvv[:, c, half * 448:(half + 1) * 448],
                            c == 0, c == 7, [hT.buf, wv.buf], [bk.buf])
                self.copy("act" if half == 0 else "dve", vs.ap[:, half * 448:(half + 1) * 448], bk.ap[:, 0:448],
                          [bk.buf], [vs.buf])
            self.dma("sp", self.vtm[tb * 128:(tb + 1) * 128, :], vs.ap, [vs.buf], [self.vtm_buf], kv[tb % 2])
        S.barrier(self.cast_keys)
        A.release()

    def phase_gqa(self, l):
        A, S = self.A, self.S
        self.attn_mark = True
        A.mark()
        self.aT = A.alloc(4 * S_LEN, BF16, "attn_aT")
        self.bT = A.alloc(2 * S_LEN, BF16, "attn_bT")
        aTv = self.aT.ap.rearrange("p (c t) -> p c t", c=4)
        A.mark()
        kT = A.alloc(S_LEN, BF16, "kTa")
        self.dma("sp", kT.ap, self.qkT[4 * 128:5 * 128, :], [self.qkT_buf[4]], [kT.buf], self.key("k23"))
        vstage = A.alloc(32 * 128, BF16, "vstage")
        self.dma("sp", vstage.ap.rearrange("p (b n) -> p b n", b=32),
                 self.vtm.rearrange("(b p) n -> p b n", p=128)[:, :, 0:128], [self.vtm_buf], [vstage.buf], self.key("k24"))
        vaug = A.alloc(32 * 2 * 192, BF16, "vaug")
        vav = vaug.ap.rearrange("p (b g n) -> p b g n", b=32, g=2)
        self.memset("pool", vaug.ap, 1.0, [vaug.buf])
        vsv = vstage.ap.rearrange("p (b g n) -> p b g n", b=32, g=2)
        self.copy("dve", vav[:, :, :, 0:64], vsv, [vstage.buf], [vaug.buf])
        self.copy("pool", vav[:, :, :, 128:192], vsv, [vstage.buf], [vaug.buf])
        NQB = 6
        qp = [A.alloc(512, BF16, "qpad%d" % i) for i in range(NQB)]
        kq = [self.key("gq%d" % _i) for _i in range(NQB)]
        for q in qp:
            self.memset("pool", q.ap, 0.0, [q.buf])
        iters = [(j, h) for j in range(8) for h in range(8)]
        qsel = []
        cntg = [0, 0]
        for (j, h) in iters:
            g = h // 4
            qsel.append(g * 3 + cntg[g] % 3)
            cntg[g] += 1

        def qload(i):
            j, h = iters[i]
            g = h // 4
            q = qp[qsel[i]]
            r0 = (h // 2) * 128 + (h % 2) * 64
            self.dma("sp", q.ap[g * 64:(g + 1) * 64, :], self.qkT[r0:r0 + 64, j * 512:(j + 1) * 512],
                     [self.qkT_buf[h // 2]], [q.buf], kq[qsel[i]])

        NPT = 3
        pt = [A.alloc(1024, BF16, "pt%d" % i) for i in range(NPT)]
        rec = [A.alloc(512, F32, "rec%d" % i) for i in range(2)]
        stb = [T(self.ps[:, i * 1024:(i + 1) * 1024], Buf("st%d" % i)) for i in range(3)]
        otb = [self.bank[6], self.bank[7]]
        it = 0
        step = 0
        LOOK = 2
        for i0 in range(LOOK):
            qload(i0)
        for j in range(8):
            for h in range(8):
                g = h // 4
                ii = j * 8 + h
                if ii + LOOK < len(iters):
                    qload(ii + LOOK)
                q = qp[qsel[ii]]
                ot = otb[it % 2]
                it += 1
                odd = h % 2
                pend = []

                def do_mm2(s2, ptb):
                    for half in range(2):
                        kb = 2 * s2 + half
                        self.mm(ot.ap, vav[:, kb, g, odd * 64:odd * 64 + 128], ptb.ap[:, half * 512:(half + 1) * 512],
                                kb == 0, kb == 31, [vaug.buf, ptb.buf], [ot.buf])

                for s2 in range(16):
                    sb = stb[step % 3]
                    ptb = pt[step % NPT]
                    step += 1
                    for half in range(2):
                        kb = 2 * s2 + half
                        self.mm(sb.ap[:, half * 512:(half + 1) * 512], kT.ap[:, kb * 128:(kb + 1) * 128], q.ap,
                                True, True, [kT.buf, q.buf], [sb.buf])
                    self.act(ptb.ap, sb.ap, AF.Exp, [sb.buf], [ptb.buf], scale=8.0)
                    pend.append((s2, ptb))
                    if len(pend) > 1:
                        do_mm2(*pend.pop(0))
                while pend:
                    do_mm2(*pend.pop(0))
                r = rec[it % 2]
                if odd == 0:
                    self.recip(r.ap[0:64, :], ot.ap[64:128, :], [ot.buf], [r.buf])
                    self.tt("dve", aTv[0:64, h // 2, j * 512:(j + 1) * 512], ot.ap[0:64, :], r.ap[0:64, :], ALU.mult,
                            [ot.buf, r.buf], [self.aT.buf])
                else:
                    self.recip(r.ap[64:128, :], ot.ap[0:64, :], [ot.buf], [r.buf])
                    self.tt("dve", aTv[64:128, h // 2, j * 512:(j + 1) * 512], ot.ap[64:128, :], r.ap[64:128, :], ALU.mult,
                            [ot.buf, r.buf], [self.aT.buf])
                if ii % 3 == 1 and self.pending_cast:
                    self.cast_emit(self.pending_cast.pop(0), pace=[r.buf])
        S.barrier(self.cast_keys)
        A.release()

    def phase_dil(self, l):
        A, S = self.A, self.S
        A.mark()
        bTv = self.bT.ap.rearrange("p (c t) -> p c t", c=2)
        HS = S_LEN // 2
        U = A.alloc(4 * HS, F32, "U")
        Uv = U.ap.rearrange("p (h t) -> p h t", h=4)
        LM = HS
        NBM = LM // 128 + 1
        NSET = 2
        sets = []
        for i in range(NSET):
            st = dict(
                qe=A.alloc(2 * LM, BF16, "qe%d" % i), qo=A.alloc(2 * LM, BF16, "qo%d" % i),
                kr=A.alloc(2 * (LM + 128), BF16, "kr%d" % i),
                va=A.alloc(NBM * 512, BF16, "va%d" % i),
                vstg=A.alloc(NBM * 256, BF16, "vstg%d" % i),
                kqe=self.key("dqe%d" % i), kqo=self.key("dqo%d" % i), kkr=self.key("dkr%d" % i),
                kva=self.key("dva%d" % i))
            self.memset("pool", st["qe"].ap, 0.0, [st["qe"].buf])
            self.memset("pool", st["qo"].ap, 0.0, [st["qo"].buf])
            self.memset("pool", st["vstg"].ap, 0.0, [st["vstg"].buf])
            sets.append(st)
        NPT = 3
        pt = [A.alloc(1024, BF16, "ptd%d" % i) for i in range(NPT)]
        recs = [A.alloc(HS, F32, "recd%d" % i) for i in range(2)]
        stb = [T(self.ps[:, i * 1024:(i + 1) * 1024], Buf("std%d" % i)) for i in range(3)]
        otb = [self.bank[6], self.bank[7]]
        NOT = 2
        maskv = self.masks.ap.rearrange("p (k q) -> p k q", k=2)
        mask4 = maskv.unsqueeze(1).broadcast_to([128, 4, 2, 128])

        jobs = [(s, g, r) for s in range(2) for g in range(3) for r in range(B_DIL[g])]

        first_s1 = min(i for i, jb in enumerate(jobs) if jb[0] == 1)

        def init_ones(st_):
            v5 = st_["va"].ap.rearrange("p (b h two n) -> p b h two n", h=4, two=2, n=64)
            self.memset("pool", v5[:, :, :, 1, :], 1.0, [st_["va"].buf])

        def geom(s, g):
            d = B_DIL[g]
            L = S_LEN // d
            Lh = L // 2
            nb = Lh // 128
            m0 = s * Lh
            lo = max(0, m0 - 64)
            hi = min(L, m0 + Lh + 64)
            return d, L, Lh, nb, m0, lo, hi, lo - (m0 - 64), hi - (m0 - 64)

        def loads(ji):
            s, g, r = jobs[ji]
            st = sets[ji % NSET]
            d, L, Lh, nb, m0, lo, hi, ulo, uhi = geom(s, g)
            base = 5 + 4 * g
            qe, qo, kr, va, vstg = st["qe"], st["qo"], st["kr"], st["va"], st["vstg"]
            if s == 1 and ji - first_s1 < NSET:
                init_ones(st)
            krv = kr.ap.rearrange("p (c u) -> p c u", c=2)
            vav = va.ap.rearrange("p (b n) -> p b n", n=512)
            vsg = vstg.ap.rearrange("p (b n) -> p b n", n=256)
            qsrc = self.qkT.rearrange("(i p) (r m) -> p i r m", p=128, r=d)
            qev = qe.ap.rearrange("p (c m) -> p c m", c=2)
            qov = qo.ap.rearrange("p (c m) -> p c m", c=2)
            self.dma("sp", qev[0:64, :, 0:Lh], qsrc[0:64, base:base + 2, r, m0:m0 + Lh],
                     [self.qkT_buf[base], self.qkT_buf[base + 1]], [qe.buf], st["kqe"])
            self.dma("sp", qov[64:128, :, 0:Lh], qsrc[64:128, base:base + 2, r, m0:m0 + Lh],
                     [self.qkT_buf[base], self.qkT_buf[base + 1]], [qo.buf], st["kqo"])
            if ulo > 0:
                self.memset("pool", krv[:, :, 0:ulo], 0.0, [kr.buf])
            if uhi < Lh + 128:
                self.memset("pool", krv[:, :, uhi:Lh + 128], 0.0, [kr.buf])
            self.dma("sp", krv[:, :, ulo:uhi], qsrc[:, base + 2:base + 4, r, lo:hi],
                     [self.qkT_buf[base + 2], self.qkT_buf[base + 3]], [kr.buf], st["kkr"])
            vsrc = self.vtm.rearrange("(m r) n -> r m n", r=d)[r]
            c0 = 128 + 256 * g
            if ulo > 0:
                self.dma("sp", vsg[64:128, 0, :], vsrc[m0:m0 + 64, c0:c0 + 256], [self.vtm_buf], [vstg.buf], st["kva"])
                b_start = 1
            else:
                b_start = 0
            if uhi < Lh + 128:
                self.dma("sp", vsg[0:64, nb, :], vsrc[m0 + Lh - 64:m0 + Lh, c0:c0 + 256], [self.vtm_buf],
                         [vstg.buf], st["kva"])
                b_end = nb
            else:
                b_end = nb + 1
            mlo = m0 - 64 + 128 * b_start
            self.dma("sp", vsg[:, b_start:b_end, :],
                     vsrc[mlo:mlo + 128 * (b_end - b_start), c0:c0 + 256].rearrange("(b p) n -> p b n", p=128),
                     [self.vtm_buf], [vstg.buf], st["kva"])
            self.copy("pool", vav[:, 0:nb + 1, :].rearrange("p b (h two n) -> p b h two n", h=4, two=2)[:, :, :, 0, :],
                      vsg[:, 0:nb + 1, :].rearrange("p b (h n) -> p b h n", h=4), [vstg.buf], [va.buf])
            if ulo > 0:
                self.memset("pool", vav[0:64, 0, :], 0.0, [va.buf])
            if uhi < Lh + 128:
                self.memset("pool", vav[64:128, nb, :], 0.0, [va.buf])

        for st_ in sets:
            init_ones(st_)
        pend = []
        it = 0
        loads(0)
        for ji, (s, g, r) in enumerate(jobs):
            if g == 0 and r == 0:
                self.memset("pool", U.ap, 0.0, [U.buf])
            st = sets[ji % NSET]
            d, L, Lh, nb, m0, lo, hi, ulo, uhi = geom(s, g)
            qe, qo, kr, va = st["qe"], st["qo"], st["kr"], st["va"]
            krv = kr.ap.rearrange("p (c u) -> p c u", c=2)
            for mb in range(nb):
                sb = stb[it % 3]
                ptb = pt[it % NPT]
                ot = otb[it % 2]
                it += 1
                for h in range(4):
                    qsrc_t = qe if h % 2 == 0 else qo
                    qv = qsrc_t.ap.rearrange("p (c m) -> p c m", c=2)
                    for kk in range(2):
                        col = (h * 2 + kk) * 128
                        self.mm(sb.ap[:, col:col + 128], krv[:, h // 2, 128 * (mb + kk):128 * (mb + kk) + 128],
                                qv[:, h // 2, mb * 128:(mb + 1) * 128], True, True,
                                [kr.buf, qsrc_t.buf], [sb.buf])
                self.act(ptb.ap, sb.ap, AF.Exp, [sb.buf], [ptb.buf], scale=8.0)
                p4 = ptb.ap.rearrange("p (h k q) -> p h k q", h=4, k=2)
                self.tt("dve", p4, p4, mask4, ALU.mult, [ptb.buf, self.cst.buf], [ptb.buf])

                def stage2(ptb=ptb, ot=ot, va=va, mb=mb, d=d, r=r):
                    for h in range(4):
                        for kk in range(2):
                            col = (h * 2 + kk) * 128
                            o = (mb + kk) * 512 + (128 * h if h % 2 == 0 else 128 * h - 64)
                            self.mm(ot.ap[:, h * 128:(h + 1) * 128], va.ap[:, o:o + 128],
                                    ptb.ap[:, col:col + 128], kk == 0, kk == 1, [va.buf, ptb.buf], [ot.buf])
                    if d == 1:
                        uview = Uv[:, :, mb * 128:(mb + 1) * 128]
                    else:
                        uview = Uv.rearrange("p h (m r) -> p h r m", r=d)[:, :, r, mb * 128:(mb + 1) * 128]
                    self.tt("dve", uview, ot.ap.rearrange("p (h q) -> p h q", h=4), uview, ALU.add,
                            [ot.buf, U.buf], [U.buf])

                pend.append((ji, stage2))
                if len(pend) > 2:
                    pend.pop(0)[1]()
                if mb == 0 and ji + 1 < len(jobs):
                    while pend and pend[0][0] < ji:
                        pend.pop(0)[1]()
                    loads(ji + 1)
            if g == 2 and r == B_DIL[2] - 1:
                while pend:
                    pend.pop(0)[1]()
                for h in range(4):
                    rc = recs[h % 2]
                    if h % 2 == 0:
                        self.act(rc.ap[0:64, :], Uv[64:128, h, :], AF.Ln, [U.buf], [rc.buf])
                        self.act(rc.ap[0:64, :], rc.ap[0:64, :], AF.Exp, [rc.buf], [rc.buf], scale=-1.0)
                        self.tt("dve", bTv[0:64, h // 2, s * HS:(s + 1) * HS], Uv[0:64, h, :], rc.ap[0:64, :], ALU.mult,
                                [U.buf, rc.buf], [self.bT.buf])
                    else:
                        self.act(rc.ap[64:128, :], Uv[0:64, h, :], AF.Ln, [U.buf], [rc.buf])
                        self.act(rc.ap[64:128, :], rc.ap[64:128, :], AF.Exp, [rc.buf], [rc.buf], scale=-1.0)
                        self.tt("dve", bTv[64:128, h // 2, s * HS:(s + 1) * HS], Uv[64:128, h, :], rc.ap[64:128, :],
                                ALU.mult, [U.buf, rc.buf], [self.bT.buf])
        S.barrier(self.cast_keys)
        A.release()

    def va_lhs(self, vav, blk, h):
        o = blk * 512 + (128 * h if h % 2 == 0 else 128 * h - 64)
        return self.va_flat[:, o:o + 128]

    def phase_mix(self, l):
        A, S = self.A, self.S
        A.mark()
        aTv = self.aT.ap.rearrange("p (c t) -> p c t", c=4)
        bTv = self.bT.ap.rearrange("p (c t) -> p c t", c=2)
        wg = A.alloc(16 * 1024, BF16, "wg")
        self.dma("sp", wg.ap.rearrange("p (a k) -> p a k", a=16), self.Wb["g"][l % 2].rearrange("a p k -> p a k"),
                 [self.Wb_buf["g"][l % 2]], [wg.buf], self.key("k29"))
        wgv = wg.ap.rearrange("p (a c f) -> p a c f", a=16, c=8)
        wba = A.alloc(4096, BF16, "wba")
        self.dma("sp", wba.ap, self.Wb["ba"][l % 2], [self.Wb_buf["ba"][l % 2]], [wba.buf], self.key("k30"))
        wbav = wba.ap.rearrange("p (c n) -> p c n", c=4)
        wbb = A.alloc(2048, BF16, "wbb")
        self.dma("sp", wbb.ap, self.Wb["bb"][l % 2], [self.Wb_buf["bb"][l % 2]], [wbb.buf], self.key("k31"))
        wbbv = wbb.ap.rearrange("p (c n) -> p c n", c=2)
        wo = A.alloc(8192, BF16, "wo")
        self.dma("sp", wo.ap, self.Wb["o"][l % 2], [self.Wb_buf["o"][l % 2]], [wo.buf], self.key("k32"))
        wov = wo.ap.rearrange("p (c n) -> p c n", c=8)
        xts = [A.alloc(8 * 512, F32, "xtm%d" % i) for i in range(2)]
        kx = [self.key("r12_%d" % _i) for _i in range(2)]
        sq = A.alloc(8 * 512, BF16, "sqm")
        rstd = A.alloc(512, F32, "rstdm")
        tmp = [A.alloc(512, F32, "tmpm%d" % i) for i in range(2)]
        hts = [A.alloc(8 * 512, BF16, "htm%d" % i) for i in range(2)]
        gates = A.alloc(16 * 512, BF16, "gates")
        gv = gates.ap.rearrange("p (a t) -> p a t", a=16)
        mixed = A.alloc(8 * 512, BF16, "mixed")
        mv = mixed.ap.rearrange("p (c t) -> p c t", c=8)
        t1 = [A.alloc(512, F32, "t1%d" % i) for i in range(2)]
        t2 = [A.alloc(512, F32, "t2%d" % i) for i in range(2)]
        nb = 0

        def prep(j):
            xt = xts[j % 2]
            ht = hts[j % 2]
            self.dma("sp", xt.ap.rearrange("p (c t) -> p c t", c=8),
                     self.xT.rearrange("(c p) t -> p c t", p=128)[:, :, j * 512:(j + 1) * 512],
                     [self.xT_buf[j]], [xt.buf], kx[j % 2])
            self.make_h(xt, ht.ap.rearrange("p (c t) -> p c t", c=8), ht.buf, self.A1, 0, sq, rstd, tmp, self.bank[7])

        prep(0)
        for j in range(8):
            xt = xts[j % 2]
            ht = hts[j % 2]
            tsl = slice(j * 512, (j + 1) * 512)
            hv = ht.ap.rearrange("p (c t) -> p c t", c=8)
            for a in range(16):
                bk = self.bank[nb % 6]
                nb += 1
                for c in range(8):
                    self.mm(bk.ap, wgv[:, a, c, :], hv[:, c, :], c == 0, c == 7, [wg.buf, ht.buf], [bk.buf])
                self.act(gv[:, a, :], bk.ap, AF.Sigmoid, [bk.buf], [gates.buf])
            for oc in range(8):
                bA = self.bank[nb % 6]
                bB = self.bank[(nb + 1) % 6]
                nb += 2
                for c in range(4):
                    self.mm(bA.ap, wbav[:, c, oc * 128:(oc + 1) * 128], aTv[:, c, tsl], c == 0, c == 3,
                            [wba.buf, self.aT.buf], [bA.buf])
                for c in range(2):
                    self.mm(bB.ap, wbbv[:, c, oc * 128:(oc + 1) * 128], bTv[:, c, tsl], c == 0, c == 1,
                            [wbb.buf, self.bT.buf], [bB.buf])
                a1, a2 = t1[oc % 2], t2[oc % 2]
                self.tt("dve", a1.ap, bA.ap, gv[:, oc, :], ALU.mult, [bA.buf, gates.buf], [a1.buf])
                self.tt("dve", a2.ap, bB.ap, gv[:, 8 + oc, :], ALU.mult, [bB.buf, gates.buf], [a2.buf])
                self.tt("pool", mv[:, oc, :], a1.ap, a2.ap, ALU.add, [a1.buf, a2.buf], [mixed.buf])
            if j + 1 < 8:
                prep(j + 1)
            xv = xt.ap.rearrange("p (c t) -> p c t", c=8)
            for oc in range(8):
                bk = self.bank[nb % 6]
                nb += 1
                for c in range(8):
                    self.mm(bk.ap, wov[:, c, oc * 128:(oc + 1) * 128], mv[:, c, :], c == 0, c == 7,
                            [wo.buf, mixed.buf], [bk.buf])
                self.stt("dve", xv[:, oc, :], bk.ap, self.mod.ap[:, 16 + oc:17 + oc], xv[:, oc, :], ALU.mult, ALU.add,
                         [bk.buf, self.mod.buf, xt.buf], [xt.buf])
            self.dma("sp", self.xT.rearrange("(c p) t -> p c t", p=128)[:, :, tsl],
                     xt.ap.rearrange("p (c t) -> p c t", c=8), [xt.buf], [self.xT_buf[j]], kx[j % 2])
        S.barrier(self.cast_keys)
        A.release()
        A.release()

    def phase_ffn(self, l):
        A, S = self.A, self.S
        A.mark()
        TG = 1024
        NSUB = TG // 512
        NG = S_LEN // TG
        xts = [A.alloc(8 * 512, F32, "xtf%d" % i) for i in range(NSUB)]
        kx = [self.key("r13_%d" % _i) for _i in range(NSUB)]
        xpre = A.alloc(8 * 512, F32, "xpre")
        kxp = self.key("xpre")
        sq = A.alloc(8 * 512, BF16, "sqf")
        rstd = A.alloc(512, F32, "rstdf")
        tmp = [A.alloc(512, F32, "tmpf%d" % i) for i in range(2)]
        hts = [A.alloc(8 * TG, BF16, "htf%d" % i) for i in range(2)]
        uT = A.alloc(32 * TG, BF16, "uT")
        uv = uT.ap.rearrange("p (k t) -> p k t", k=32)
        w1 = [A.alloc(4 * 1024, BF16, "w1_%d" % i) for i in range(2)]
        k1 = [self.key("r14_%d" % _i) for _i in range(2)]
        w2 = [A.alloc(4096, BF16, "w2_%d" % i) for i in range(2)]
        k2 = [self.key("r15_%d" % _i) for _i in range(2)]
        rl = [A.alloc(512, F32, "rl%d" % i) for i in range(3)]
        nb = 0
        nr = 0
        xTv = self.xT.rearrange("(c p) t -> p c t", p=128)

        def prefetch_stages(tg, sub):
            ht = hts[tg % 2]
            hv_ = ht.ap.rearrange("p (c t) -> p c t", c=8)
            j = tg * NSUB + sub
            sA, sB, sC = self.make_h_stages(xpre, hv_[:, :, sub * 512:(sub + 1) * 512], ht.buf, self.A2, 24, sq, rstd,
                                            tmp, self.bank[7])

            def sA2():
                self.dma("sp", xpre.ap.rearrange("p (c t) -> p c t", c=8), xTv[:, :, j * 512:(j + 1) * 512],
                         [self.xT_buf[j]], [xpre.buf], kxp)
                sA()
            return sA2, sB, sC

        for sub in range(NSUB):
            for st in prefetch_stages(0, sub):
                st()
        for tg in range(NG):
            ht = hts[tg % 2]
            hv = ht.ap.rearrange("p (c t) -> p c t", c=8)
            for sub in range(NSUB):
                j = tg * NSUB + sub
                xt = xts[sub]
                self.dma("sp", xt.ap.rearrange("p (c t) -> p c t", c=8), xTv[:, :, j * 512:(j + 1) * 512],
                         [self.xT_buf[j]], [xt.buf], kx[sub])
            sched = {}
            if tg + 1 < NG:
                for sub in range(NSUB):
                    sA, sB, sC = prefetch_stages(tg + 1, sub)
                    q0 = sub * 4
                    sched.setdefault((q0, "pre"), []).append(sA)
                    sched.setdefault((q0 + 1, "post"), []).append(sB)
                    sched.setdefault((q0 + 2, "post"), []).append(sC)
            for q in range(8):
                for f_ in sched.get((q, "pre"), []):
                    f_()
                w = w1[q % 2]
                self.dma("sp", w.ap.rearrange("p (a k) -> p a k", a=4),
                         self.Wb["f1"][l % 2][q * 4:(q + 1) * 4].rearrange("a p k -> p a k"),
                         [self.Wb_buf["f1"][l % 2]], [w.buf], k1[q % 2])
                wv_ = w.ap.rearrange("p (a c f) -> p a c f", a=4, c=8)
                for a in range(4):
                    hc = q * 4 + a
                    for sub in range(NSUB):
                        bk = self.bank[nb % 7]
                        nb += 1
                        for c in range(8):
                            self.mm(bk.ap, wv_[:, a, c, :], hv[:, c, sub * 512:(sub + 1) * 512], c == 0, c == 7,
                                    [w.buf, ht.buf], [bk.buf])
                        r = rl[nr % 3]
                        nr += 1
                        self.act(r.ap, bk.ap, AF.Relu, [bk.buf], [r.buf])
                        self.tt("pool", uv[:, hc, sub * 512:(sub + 1) * 512], r.ap, r.ap, ALU.mult, [r.buf], [uT.buf])
                for f_ in sched.get((q, "post"), []):
                    f_()
            for oc in range(8):
                w = w2[oc % 2]
                self.dma("sp", w.ap, self.Wb["f2"][l % 2][oc], [self.Wb_buf["f2"][l % 2]], [w.buf], k2[oc % 2])
                wv_ = w.ap.rearrange("p (k f) -> p k f", k=32)
                for sub in range(NSUB):
                    xt = xts[sub]
                    xv = xt.ap.rearrange("p (c t) -> p c t", c=8)
                    bk = self.bank[nb % 7]
                    nb += 1
                    for kc in range(32):
                        self.mm(bk.ap, wv_[:, kc, :], uv[:, kc, sub * 512:(sub + 1) * 512], kc == 0, kc == 31,
                                [w.buf, uT.buf], [bk.buf])
                    self.stt("dve", xv[:, oc, :], bk.ap, self.mod.ap[:, 40 + oc:41 + oc], xv[:, oc, :], ALU.mult, ALU.add,
                             [bk.buf, self.mod.buf, xt.buf], [xt.buf])
            for sub in range(NSUB):
                j = tg * NSUB + sub
                xt = xts[sub]
                self.dma("sp", xTv[:, :, j * 512:(j + 1) * 512],
                         xt.ap.rearrange("p (c t) -> p c t", c=8), [xt.buf], [self.xT_buf[j]], kx[sub])
        S.barrier(self.cast_keys)
        A.release()


def _fm(v, n):
    return np.ascontiguousarray(v.reshape(v.shape[:-1] + (n, 128)).swapaxes(-1, -2))


def _lhs_chunks(W):
    K, N = W.shape
    return np.ascontiguousarray(W.reshape(K // 128, 128, N // 128, 128).transpose(2, 1, 0, 3)).reshape(N // 128, 128, K)


def _rhs_rows(W):
    K, N = W.shape
    return np.ascontiguousarray(W.reshape(K // 128, 128, N).transpose(1, 0, 2)).reshape(128, (K // 128) * N)


_PERM_AX = np.concatenate([np.arange(0, 16), np.arange(32, 48), np.arange(16, 32), np.arange(48, 64)])
_SWAP = np.concatenate([np.arange(32, 64), np.arange(0, 32)])


def _consts():
    f = np.float32
    ident = np.eye(128, dtype=f)
    ones = np.ones((128, 128), f)
    e64 = np.kron(np.eye(2, dtype=f), np.ones((64, 64), f))
    i = np.arange(128)[:, None]
    j = np.arange(128)[None, :]
    mask0 = (i >= j).astype(f)
    mask1 = (i <= j).astype(f)
    m = np.arange(128)
    partner = 64 * (m // 64) + ((m % 64) + 32) % 64
    perm = np.zeros((128, 128), f)
    perm[partner, m] = 1.0
    cst = np.ascontiguousarray(np.stack([ones, e64, mask0, mask1, perm], axis=1))
    theta = np.float32(10000.0)
    pos = np.arange(S_LEN)
    inv32 = (theta ** (-np.arange(0, 64, 2, dtype=f) / f(64))).astype(f)
    ang = pos.astype(f)[:, None] * inv32[None, :]
    cs, sn = np.cos(ang).astype(f), np.sin(ang).astype(f)
    c64 = np.concatenate([cs, cs], axis=1)
    s64 = np.concatenate([-sn, sn], axis=1)
    C_seq = np.ascontiguousarray(np.tile(c64, (1, 2)).T)
    S_seq = np.ascontiguousarray(np.tile(s64, (1, 2)).T)
    inv16 = (theta ** (-np.arange(0, 32, 2, dtype=f) / f(32))).astype(f)
    row = (pos // 64).astype(f)
    col = (pos % 64).astype(f)
    ar = row[:, None] * inv16[None, :]
    ac = col[:, None] * inv16[None, :]
    c32 = np.concatenate([np.cos(ar), np.cos(ac)], axis=1).astype(f)
    s32 = np.concatenate([np.sin(ar), np.sin(ac)], axis=1).astype(f)
    c64a = np.concatenate([c32, c32], axis=1)
    s64a = np.concatenate([-s32, s32], axis=1)
    C_ax = np.ascontiguousarray(np.tile(c64a, (1, 2)).T)
    S_ax = np.ascontiguousarray(np.tile(s64a, (1, 2)).T)
    tabs = np.ascontiguousarray(np.stack([C_seq, S_seq, C_ax, S_ax]))
    return ident, cst, tabs


def _prep_weights(inp, layers):
    f = np.float32
    out = {k: [] for k in ("b_ada", "g_mix", "g_mlp", "g_qk", "w_ada", "w_qk", "w_v", "w_g", "w_ba", "w_bb",
                           "w_o", "w_f1", "w_f2")}
    for l in layers:
        w_in = np.asarray(inp["w_in"][l], f)
        cols = {}
        off = 0
        names = ["qa", "ka", "va"] + [n + str(g) for g in range(3) for n in ("qb", "kb", "vb")] + ["ga", "gb"]
        sizes = [512, 128, 128] + [256] * 9 + [1024, 1024]
        for n, sz in zip(names, sizes):
            cols[n] = w_in[:, off:off + sz]
            off += sz

        def perm_heads(Wc, perm):
            K, N = Wc.shape
            return Wc.reshape(K, N // 64, 64)[:, :, perm].reshape(K, N)

        qn_a = np.asarray(inp["q_norm_a"][l], f)
        kn_a = np.asarray(inp["k_norm_a"][l], f)
        qn_b = np.asarray(inp["q_norm_b"][l], f)
        kn_b = np.asarray(inp["k_norm_b"][l], f)
        raw_cols, sw_cols, g_raw, g_sw = [], [], [], []

        def add(Wc, gain, axial):
            Wp = perm_heads(Wc, _PERM_AX) if axial else Wc
            gp = gain[_PERM_AX] if axial else gain
            raw_cols.append(Wp)
            sw_cols.append(perm_heads(Wp, _SWAP))
            nh = Wc.shape[1] // 64
            g_raw.append(np.tile(gp, nh))
            g_sw.append(np.tile(gp[_SWAP], nh))

        add(cols["qa"], qn_a, True)
        add(cols["ka"], kn_a, True)
        for g in range(3):
            add(cols["qb%d" % g], qn_b[g], False)
            add(cols["kb%d" % g], kn_b[g], False)
        Wqk = np.concatenate(raw_cols, axis=1)
        Wqks = np.concatenate(sw_cols, axis=1)
        gr = np.concatenate(g_raw)
        gs = np.concatenate(g_sw)
        out["g_qk"].append(np.concatenate([_fm(gr, NQK), _fm(gs, NQK)], axis=1))
        out["w_qk"].append(_lhs_chunks(Wqk))
        Wv = np.concatenate([cols["va"], cols["vb0"], cols["vb1"], cols["vb2"]], axis=1)
        out["w_v"].append(_rhs_rows(Wv))
        out["w_g"].append(_lhs_chunks(np.concatenate([cols["ga"], cols["gb"]], axis=1)))
        out["w_ba"].append(_rhs_rows(np.asarray(inp["w_branch_a"][l], f)))
        out["w_bb"].append(_rhs_rows(np.asarray(inp["w_branch_b"][l], f)))
        out["w_o"].append(_rhs_rows(np.asarray(inp["w_out"][l], f)))
        out["w_f1"].append(_lhs_chunks(np.asarray(inp["w_ff1"][l], f)))
        out["w_f2"].append(_lhs_chunks(np.asarray(inp["w_ff2"][l], f)))
        out["w_ada"].append(_lhs_chunks(np.asarray(inp["w_ada"][l], f)))
        out["b_ada"].append(_fm(np.asarray(inp["b_ada"][l], f), 48))
        out["g_mix"].append(_fm(np.asarray(inp["g_mix"][l], f), 8))
        out["g_mlp"].append(_fm(np.asarray(inp["g_mlp"][l], f), 8))
    return {k: np.ascontiguousarray(np.stack(v)) for k, v in out.items()}


_PROG_CACHE = {}


def _get_prog(NL, debug=False):
    key = (NL, debug)
    if key not in _PROG_CACHE:
        p = Prog(NL, debug)
        p.build()
        _PROG_CACHE[key] = p
    return _PROG_CACHE[key]


def run_layers(x, c, inp, layers, cores=NCORES, debug=False):
    prog = _get_prog(len(layers), debug)
    ident, cst, tabs = _consts()
    wts = _prep_weights(inp, layers)
    in_maps = []
    for b in range(cores):
        m = {"x": np.ascontiguousarray(x[b]), "c_fm": _fm(np.asarray(c[b], np.float32), 8), "ident": ident,
             "cst_bf": cst, "tabs": tabs}
        m.update(wts)
        in_maps.append(m)
    res = run_bass_kernel_spmd(prog.nc, in_maps, core_ids=list(range(cores)))
    if debug:
        return res
    return np.stack([np.asarray(r["out"]) for r in res.results])


FUSED = True


def kernel(x, c, w_ada, b_ada, g_mix, g_mlp, w_in, q_norm_a, k_norm_a, q_norm_b, k_norm_b,
           w_branch_a, w_branch_b, w_out, w_ff1, w_ff2):
    inp = dict(w_ada=w_ada, b_ada=b_ada, g_mix=g_mix, g_mlp=g_mlp, w_in=w_in, q_norm_a=q_norm_a, k_norm_a=k_norm_a,
               q_norm_b=q_norm_b, k_norm_b=k_norm_b, w_branch_a=w_branch_a, w_branch_b=w_branch_b, w_out=w_out,
               w_ff1=w_ff1, w_ff2=w_ff2)
    x = np.asarray(x, np.float32)
    c = np.asarray(c, np.float32)
    if FUSED:
        return run_layers(x, c, inp, list(range(DEPTH))).astype(np.float32)
    for l in range(DEPTH):
        x = run_layers(x, c, inp, [l])
    return x.astype(np.float32)
```
